# Optimizing a Trainium2 kernel written in Bass

```python
import math, functools
import jax, jax.numpy as jnp
from jax import lax
import numpy as np

D_MODEL = 2048
BATCH = 32
SEQ = 256
DEPTH = 4
DEC_BATCH = 2
DEC_SEQ = 1024
PAST_LEN = 512

GRID_W = 64
HEAD_DIM = 128
N_HEADS_A = 8
W_A = N_HEADS_A * HEAD_DIM
GROUP_DIM = 128
N_GROUPS_F = 4
W_F = N_GROUPS_F * GROUP_DIM
N_GROUPS_C = 4
W_C = N_GROUPS_C * GROUP_DIM
CHUNK = 128
MAX_WIN_R = 8
WIN_C = 16
QBLK = 128
D_FF = 4 * D_MODEL
N_MOD = 6
IN_WIDTH = 3 * W_A + W_F + 2 * W_C + 3 * D_MODEL
RMS_EPS = 1e-6

kernel_name = "hybrid_natten_fnet_gmlp_diffusion_step"


def _rmsnorm(x, g):
    x32 = x.astype(jnp.float32)
    y = x32 * lax.rsqrt(jnp.mean(x32 * x32, axis=-1, keepdims=True) + RMS_EPS)
    return (y * g.astype(jnp.float32)).astype(x.dtype)


def _context_attention(q, k, v):
    B, S, H, hd = q.shape
    scale = 1.0 / math.sqrt(hd)
    qb = q.reshape(B, S // QBLK, QBLK, H, hd).transpose(1, 0, 2, 3, 4)

    def one_block(qi):
        s = jnp.einsum('bqhd,bkhd->bhqk', qi, k).astype(jnp.float32) * scale
        p = jax.nn.softmax(s, axis=-1).astype(v.dtype)
        return jnp.einsum('bhqk,bkhd->bqhd', p, v)

    o = lax.map(one_block, qb)
    return o.transpose(1, 0, 2, 3, 4).reshape(B, S, H * hd)


def _neighbourhood_attention(q, k, v, ck, cv, rpb):
    B, N, H, hd = q.shape
    rows = N // GRID_W
    win_r = min(MAX_WIN_R, rows)
    scale = 1.0 / math.sqrt(hd)
    r = jnp.arange(rows)
    r_start = jnp.clip(r - win_r // 2, 0, rows - win_r)
    row_idx = r_start[:, None] + jnp.arange(win_r)[None, :]
    qc = jnp.arange(GRID_W)
    c_start = jnp.clip(qc - WIN_C // 2, 0, GRID_W - WIN_C)
    kc = jnp.arange(GRID_W)
    col_valid = (kc[None, :] >= c_start[:, None]) & (kc[None, :] < c_start[:, None] + WIN_C)
    dr = row_idx - r[:, None] + (MAX_WIN_R - 1)
    dc = jnp.clip(kc[None, :] - qc[:, None], -(WIN_C - 1), WIN_C - 1) + (WIN_C - 1)
    bias = rpb[:, dr[:, None, :, None], dc[None, :, None, :]]
    bias = bias.transpose(1, 2, 0, 3, 4).astype(jnp.float32)

    qg = q.reshape(B, rows, GRID_W, H, hd)
    kg = k.reshape(B, rows, GRID_W, H, hd)
    vg = v.reshape(B, rows, GRID_W, H, hd)
    k_rows = kg[:, row_idx]
    v_rows = vg[:, row_idx]

    s_lat = jnp.einsum('brqhd,brikhd->brqhik', qg, k_rows).astype(jnp.float32) * scale + bias[None]
    s_lat = jnp.where(col_valid[None, None, :, None, None, :], s_lat, -jnp.inf)
    s_ctx = jnp.einsum('brqhd,blhd->brqhl', qg, ck).astype(jnp.float32) * scale
    n_lat = win_r * GRID_W
    s = jnp.concatenate([s_lat.reshape(B, rows, GRID_W, H, n_lat), s_ctx], axis=-1)
    p = jax.nn.softmax(s, axis=-1).astype(v.dtype)
    p_lat = p[..., :n_lat].reshape(B, rows, GRID_W, H, win_r, GRID_W)
    p_ctx = p[..., n_lat:]
    o = (jnp.einsum('brqhik,brikhd->brqhd', p_lat, v_rows)
         + jnp.einsum('brqhl,blhd->brqhd', p_ctx, cv))
    return o.reshape(B, N, H * hd)


def _fourier_mix(f):
    B, N, _ = f.shape
    fg = f.reshape(B, N, N_GROUPS_F, GROUP_DIM).astype(jnp.float32)
    y = jnp.fft.fft2(fg, axes=(1, 3), norm='ortho').real
    return y.astype(f.dtype).reshape(B, N, W_F)


def _spatial_gating(uv, g_sgu, w_sp, b_sp):
    B, N, _ = uv.shape
    uv = jax.nn.gelu(uv, approximate=False)
    u, v = jnp.split(uv, 2, axis=-1)
    v = _rmsnorm(v.reshape(B, N, N_GROUPS_C, GROUP_DIM), g_sgu)
    vc = v.reshape(B, N // CHUNK, CHUNK, N_GROUPS_C, GROUP_DIM)
    s = jnp.einsum('gpq,bcqgd->bcpgd', w_sp, vc) + b_sp.T[:, :, None]
    return u * s.reshape(B, N, W_C)


def _layer(x, mod, attend, w_in, w_br_a, w_br_f, w_br_c, w_out, g_sgu, w_sp, b_sp,
           g_pre_mix, g_post_mix, g_pre_mlp, g_post_mlp, w1, w2):
    sh1, sc1, gt1, sh2, sc2, gt2 = jnp.split(mod, N_MOD, axis=-1)
    B, N, _ = x.shape
    h = _rmsnorm(x, g_pre_mix) * (1.0 + sc1) + sh1
    proj = h @ w_in
    splits = (W_A, 2 * W_A, 3 * W_A, 3 * W_A + W_F, 3 * W_A + W_F + 2 * W_C,
              3 * W_A + W_F + 2 * W_C + D_MODEL, 3 * W_A + W_F + 2 * W_C + 2 * D_MODEL)
    q, k, v, f, uv, ga, gf, gc = jnp.split(proj, splits, axis=-1)
    q = q.reshape(B, N, N_HEADS_A, HEAD_DIM)
    k = k.reshape(B, N, N_HEADS_A, HEAD_DIM)
    v = v.reshape(B, N, N_HEADS_A, HEAD_DIM)
    o_a = attend(q, k, v)
    o_f = _fourier_mix(f)
    o_c = _spatial_gating(uv, g_sgu, w_sp, b_sp)
    merged = (jax.nn.sigmoid(ga) * (o_a @ w_br_a)
              + jax.nn.sigmoid(gf) * (o_f @ w_br_f)
              + jax.nn.sigmoid(gc) * (o_c @ w_br_c))
    x = x + gt1 * _rmsnorm(merged @ w_out, g_post_mix)
    h = _rmsnorm(x, g_pre_mlp) * (1.0 + sc2) + sh2
    ff = jnp.square(jax.nn.relu(h @ w1)) @ w2
    x = x + gt2 * _rmsnorm(ff, g_post_mlp)
    return x, k, v


def setup_inputs(seed: int = 0) -> dict:
    key = jax.random.key(seed)
    ks = jax.random.split(key, 32)

    def nrm(k, shape, scale):
        return jax.random.normal(k, shape, jnp.float32) * scale

    def gain(k, shape):
        return 1.0 + 0.02 * jax.random.normal(k, shape, jnp.float32)

    return {
        "x_prompt": nrm(ks[0], (BATCH, SEQ, D_MODEL), 1.0),
        "x_sample": nrm(ks[1], (DEC_BATCH, DEC_SEQ, D_MODEL), 1.0),
        "cache_k": nrm(ks[2], (DEC_BATCH, DEPTH, PAST_LEN, N_HEADS_A, HEAD_DIM), 1.0),
        "cache_v": nrm(ks[3], (DEC_BATCH, DEPTH, PAST_LEN, N_HEADS_A, HEAD_DIM), 1.0),
        "c": nrm(ks[4], (DEC_BATCH, D_MODEL), 1.0),
        "c_ctx": nrm(ks[5], (D_MODEL,), 1.0),
        "w_ada": nrm(ks[6], (DEPTH, D_MODEL, N_MOD * D_MODEL), 0.5 * D_MODEL ** -0.5),
        "b_ada": nrm(ks[7], (DEPTH, N_MOD * D_MODEL), 0.02),
        "g_pre_mix": gain(ks[8], (DEPTH, D_MODEL)),
        "g_post_mix": gain(ks[9], (DEPTH, D_MODEL)),
        "g_pre_mlp": gain(ks[10], (DEPTH, D_MODEL)),
        "g_post_mlp": gain(ks[11], (DEPTH, D_MODEL)),
        "w_in": nrm(ks[12], (DEPTH, D_MODEL, IN_WIDTH), D_MODEL ** -0.5),
        "rpb": nrm(ks[13], (DEPTH, N_HEADS_A, 2 * MAX_WIN_R - 1, 2 * WIN_C - 1), 0.5),
        "g_sgu": gain(ks[14], (DEPTH, N_GROUPS_C, GROUP_DIM)),
        "w_spatial": nrm(ks[15], (DEPTH, N_GROUPS_C, CHUNK, CHUNK), CHUNK ** -0.5),
        "b_spatial": gain(ks[16], (DEPTH, N_GROUPS_C, CHUNK)),
        "w_br_a": nrm(ks[17], (DEPTH, W_A, D_MODEL), W_A ** -0.5),
        "w_br_f": nrm(ks[18], (DEPTH, W_F, D_MODEL), W_F ** -0.5),
        "w_br_c": nrm(ks[19], (DEPTH, W_C, D_MODEL), W_C ** -0.5),
        "w_out": nrm(ks[20], (DEPTH, D_MODEL, D_MODEL), D_MODEL ** -0.5),
        "w_mlp1": nrm(ks[21], (DEPTH, D_MODEL, D_FF), D_MODEL ** -0.5),
        "w_mlp2": nrm(ks[22], (DEPTH, D_FF, D_MODEL), D_FF ** -0.5),
    }


def reference(x_prompt, x_sample, cache_k, cache_v, c, c_ctx, w_ada, b_ada,
              g_pre_mix, g_post_mix, g_pre_mlp, g_post_mlp, w_in, rpb, g_sgu,
              w_spatial, b_spatial, w_br_a, w_br_f, w_br_c, w_out, w_mlp1, w_mlp2):
    y_p = x_prompt
    y_s = x_sample
    silu_ctx = jax.nn.silu(c_ctx)
    silu_c = jax.nn.silu(c)
    new_k, new_v = [], []
    for l in range(DEPTH):
        shared = (w_in[l], w_br_a[l], w_br_f[l], w_br_c[l], w_out[l], g_sgu[l],
                  w_spatial[l], b_spatial[l], g_pre_mix[l], g_post_mix[l],
                  g_pre_mlp[l], g_post_mlp[l], w_mlp1[l], w_mlp2[l])
        mod_ctx = silu_ctx @ w_ada[l] + b_ada[l]
        mod_lat = (silu_c @ w_ada[l] + b_ada[l])[:, None, :]
        y_p, k_l, v_l = _layer(y_p, mod_ctx, _context_attention, *shared)
        new_k.append(k_l)
        new_v.append(v_l)
        attend_lat = functools.partial(_neighbourhood_attention, ck=cache_k[:, l],
                                       cv=cache_v[:, l], rpb=rpb[l])
        y_s, _, _ = _layer(y_s, mod_lat, attend_lat, *shared)
    new_cache_k = jnp.stack(new_k, axis=1)
    new_cache_v = jnp.stack(new_v, axis=1)
    return (y_p, y_s, new_cache_k, new_cache_v)
```

```python
import math
import contextlib
import numpy as np
import ml_dtypes
import concourse.bass as bass
import concourse.mybir as mybir
from concourse.bass_utils import run_bass_kernel_spmd

F32 = mybir.dt.float32
BF16 = mybir.dt.bfloat16
AF = mybir.ActivationFunctionType
ALU = mybir.AluOpType
AX = mybir.AxisListType

NEG = -30000.0
EPS = 1e-6


class Cfg:
    def __init__(self, D=2048, H=8, GF=4, GC=4, DFF=8192, DEPTH=4, NCORES=8):
        self.D, self.H, self.GF, self.GC, self.DFF, self.DEPTH, self.NCORES = D, H, GF, GC, DFF, DEPTH, NCORES
        self.SEQ, self.PB, self.NS, self.PAST, self.GW = 256, 4, 1024, 512, 64
        self.KC = D // 128
        self.NO = H + GF + GC
        assert self.NO == self.KC
        self.WA, self.WF, self.WC = H * 128, GF * 128, GC * 128
        self.INW = 3 * self.WA + self.WF + 2 * self.WC + 3 * D
        self.NIN = self.INW // 128
        self.NHC = DFF // 1024
        self.TB = 1024
        self.oq, self.ok, self.ov = 0, H, 2 * H
        self.of = 3 * H
        self.ou = 3 * H + GF
        self.ovs = 3 * H + GF + GC
        self.oga = 3 * H + GF + 2 * GC
        self.ogf = self.oga + self.KC
        self.ogc = self.ogf + self.KC
        self.NMOD = DEPTH * 6 * self.KC


class Tok:
    __slots__ = ("sem", "key", "val")

    def __init__(self, sem, key, val):
        self.sem, self.key, self.val = sem, key, val


class Buf:
    __slots__ = ("name", "w", "r", "excl")

    def __init__(self, name, excl=False):
        self.name, self.w, self.r, self.excl = name, None, {}, excl


class V:
    __slots__ = ("ap", "bufs")

    def __init__(self, ap, bufs):
        self.ap, self.bufs = ap, list(bufs)


class Eng:
    def __init__(self, nc, es, name, handle):
        self.name, self.h = name, handle
        self.sem = es.enter_context(nc.semaphore("pg_" + name))
        self.key = "pg_" + name
        self.cnt = 0
        self.seen = {}

    def wait(self, tok):
        if self.seen.get(tok.key, 0) >= tok.val:
            return
        self.h.wait_ge(tok.sem, tok.val)
        self.seen[tok.key] = tok.val


class DSem:
    def __init__(self, nc, es, name):
        self.sem = es.enter_context(nc.semaphore("d_" + name))
        self.key = "d_" + name
        self.cnt = 0


def _bufs(vs):
    out = []
    for v in vs:
        if isinstance(v, V):
            out.extend(v.bufs)
        elif isinstance(v, Buf):
            out.append(v)
        else:
            out.extend(_bufs(v))
    return out


class K:
    def __init__(self, nc, es):
        self.nc, self.es = nc, es
        self.pe = Eng(nc, es, "pe", nc.tensor)
        self.act = Eng(nc, es, "act", nc.scalar)
        self.dve = Eng(nc, es, "dve", nc.vector)
        self.pool = Eng(nc, es, "pool", nc.gpsimd)
        self.sp = Eng(nc, es, "sp", nc.sync)
        self.nds = 0

    def dsem(self, name):
        self.nds += 1
        return DSem(self.nc, self.es, "%s%d" % (name, self.nds))

    def _deps(self, eng, rb, wb):
        deps = []
        for b in rb:
            if b.w is not None:
                deps.append(b.w)
        for b in wb:
            deps.extend(b.r.values())
            if b.w is not None:
                deps.append(b.w)
        for t in deps:
            if eng is self.pe and t.key == self.pe.key:
                continue
            eng.wait(t)

    def _reg(self, tok, rb, wb):
        for b in rb:
            o = b.r.get(tok.key)
            if o is None or o.val < tok.val:
                b.r[tok.key] = tok
        for b in wb:
            b.w = tok
            b.r = {}

    @staticmethod
    def _split(reads, writes):
        rb, wb = _bufs(reads), _bufs(writes)
        ex = [b for b in rb if b.excl]
        if ex:
            rb = [b for b in rb if not b.excl]
            wb = wb + [b for b in ex if b not in wb]
        return rb, wb

    def op(self, eng, fn, reads=(), writes=()):
        rb, wb = self._split(reads, writes)
        self._deps(eng, rb, wb)
        ins = fn()
        eng.cnt += 1
        ins.then_inc(eng.sem, 1)
        self._reg(Tok(eng.sem, eng.key, eng.cnt), rb, wb)

    def group(self, fns, reads=(), writes=()):
        eng = self.pe
        rb, wb = self._split(reads, writes)
        self._deps(eng, rb, wb)
        ins = None
        for f in fns:
            ins = f()
        eng.cnt += 1
        ins.then_inc(eng.sem, 1)
        self._reg(Tok(eng.sem, eng.key, eng.cnt), rb, wb)

    def dma(self, q, out, in_, ds, reads=(), writes=()):
        rb, wb = _bufs(reads), _bufs(writes)
        self._deps(q, rb, wb)
        ins = q.h.dma_start(out=out, in_=in_)
        ins.then_inc(ds.sem, 16)
        ds.cnt += 16
        self._reg(Tok(ds.sem, ds.key, ds.cnt), rb, wb)


def build(cfg, plan_tags=None, rec_tags=None):
    c = cfg
    D, H, GF, GC, KC, DEPTH, TB = c.D, c.H, c.GF, c.GC, c.KC, c.DEPTH, c.TB
    nc = bass.Bass("TRN2", target_bir_lowering=False)

    def din(name, shape, dt=F32):
        return nc.dram_tensor(name, list(shape), dt, kind="ExternalInput").ap()

    def dout(name, shape, dt=F32):
        return nc.dram_tensor(name, list(shape), dt, kind="ExternalOutput").ap()

    xp_d = din("xp", [TB, D])
    xs_d = din("xs", [TB, D])
    ckT_d = din("ckT", [DEPTH * H * 128, c.PAST])
    cv_d = din("cv", [DEPTH * c.PAST, H * 128])
    cvec_d = din("cvec", [128, KC * 2])
    wada_d = din("wada", [c.NMOD * 128, KC * 128])
    bada_d = din("bada", [128, c.NMOD])
    gains_d = din("gains", [128, 4 * DEPTH * KC])
    win_d = din("win", [DEPTH * c.NIN * 128, KC * 128])
    wbr_d = din("wbr", [DEPTH * KC * 128, KC * 128])
    wout_d = din("wout", [DEPTH * KC * 128, KC * 128])
    w1_d = din("w1", [DEPTH * (c.DFF // 128) * 128, KC * 128])
    w2_d = din("w2", [DEPTH * c.NHC * (KC // 2) * 128, 8 * 256])
    wspT_d = din("wspT", [DEPTH * GC * 128, 128])
    bsp_d = din("bsp", [DEPTH * GC * 128, 512])
    gsgu_d = din("gsgu", [DEPTH * GC * 128, 128])
    bias_d = din("biast", [DEPTH * H * 8 * 128, 640])
    csc_d = din("csc", [128, 256], BF16)
    dftp_d = din("dftp", [2 * 128, 2 * 256], BF16)
    dfts_d = din("dfts", [4 * 2 * 128, 8 * 256], BF16)

    yp_d = dout("yp", [TB, D])
    ys_d = dout("ys", [TB, D])
    nk_d = dout("nk", [c.PB * DEPTH * c.SEQ, H * 128])
    nv_d = dout("nv", [c.PB * DEPTH * c.SEQ, H * 128])

    with contextlib.ExitStack() as es:
        k = K(nc, es)
        pe, act, dve, pool, sp = k.pe, k.act, k.dve, k.pool, k.sp

        def sb(name, shape, dt):
            return es.enter_context(nc.sbuf_tensor("s_" + name, list(shape), dt))

        RB = [sb("rb%d" % i, [128, KC * 1024], BF16) for i in range(3)]
        RBf = [r[:, :].bitcast(F32) for r in RB]
        RU = [[Buf("rb%d_%d" % (i, u)) for u in range(2 * KC)] for i in range(3)]
        xres = sb("xres", [128, KC * 1024], F32)
        XB = [[Buf("x%d_%d" % (j, t)) for t in range(2)] for j in range(KC)]
        NSLOT = 4
        SLOT = max(KC * 128, 8 * 256)
        wb_t = [sb("wb%d" % i, [128, SLOT], BF16) for i in range(NSLOT)]
        wb_b = [Buf("wb%d" % i) for i in range(NSLOT)]
        wb_s = [k.dsem("wb") for _ in range(NSLOT)]
        wb_s2 = [k.dsem("wbh") for _ in range(NSLOT)]
        SCRB = 23040
        scr_t = sb("scr", [128, SCRB // 2], BF16)
        scr_pages = [Buf("scr%d" % i) for i in range(SCRB // 512)]
        modsb = sb("modsb", [128, c.NMOD * 2], F32)
        modb = Buf("modsb")
        gains = sb("gains", [128, 4 * DEPTH * KC], F32)
        gainsb = Buf("gains")
        lvec = sb("lvec", [128, 6 * KC], F32)
        lvb = Buf("lvec")
        identb = sb("identb", [128, 128], BF16)
        identf = sb("identf", [128, 128], F32)
        onesb = sb("onesb", [128, 128], BF16)
        cbuf = Buf("consts")
        csc = sb("csc", [128, 256], BF16)
        bada_p = sb("badap", [128, c.NMOD], F32)
        sT_p = sb("sTp", [128, KC * 2], BF16)
        badab = Buf("badab")
        sTb = Buf("sTb")
        final_sems = []
        stage_s = [k.dsem("stg"), k.dsem("stg")]
        kout_s, vout_s, bias_s, ck_s, cvh_s = k.dsem("ko"), k.dsem("vo"), k.dsem("bi"), k.dsem("ck"), k.dsem("cvh")
        wsp_s, bsp_s, gsg_s = k.dsem("wsp"), k.dsem("bsp"), k.dsem("gsg")
        final_sems += stage_s + [kout_s, vout_s]

        def scr(off, nbytes, dt, shape=None):
            assert off % 4 == 0 and off + nbytes <= SCRB, (off, nbytes)
            ap = scr_t[:, off // 2:(off + nbytes) // 2]
            if dt == F32:
                ap = ap.bitcast(F32)
            if shape is not None:
                ap = ap.rearrange("p (a b) -> p a b", a=shape[0])
            return V(ap, scr_pages[off // 512:(off + nbytes + 511) // 512])

        def sub(v, ap):
            return V(ap, v.bufs)

        psS = es.enter_context(nc.psum_tensor("psS", [128, 1536], F32))
        singles = [es.enter_context(nc.psum_tensor("ps%d" % i, [128, 512], F32)) for i in range(5)]
        bank_ap = [psS[:, i * 512:(i + 1) * 512] for i in range(3)] + [t[:, :] for t in singles]
        bank_b = [Buf("ps%d" % i, excl=True) for i in range(8)]
        rot = [0, 0]

        def bank():
            i = rot[0] % 5
            rot[0] += 1
            return V(bank_ap[i], [bank_b[i]])

        def bank2():
            i = 3 + rot[1] % 2
            rot[1] += 1
            return V(bank_ap[i], [bank_b[i]])

        def hbank():
            return V(bank_ap[5], [bank_b[5]])

        def sbank(t):
            return V(bank_ap[6 + t], [bank_b[6 + t]])

        def hch(i, j, t0=0, t1=1024):
            return V(RB[i][:, j * 1024 + t0:j * 1024 + t1], RU[i][2 * j + t0 // 512:2 * j + (t1 - 1) // 512 + 1])

        def fch(pair, jj, tb):
            half = KC // 2
            b = pair[jj // half]
            loc = jj % half
            return V(RBf[b][:, loc * 1024 + tb * 512:loc * 1024 + tb * 512 + 512], RU[b][4 * loc + 2 * tb:4 * loc + 2 * tb + 2])

        def xch(j, tb):
            return V(xres[:, j * 1024 + tb * 512:j * 1024 + tb * 512 + 512], [XB[j][tb]])

        wq = []
        wstate = {"n": 0, "plan": iter(plan_tags) if plan_tags is not None else None}

        def src_of(tag):
            kind = tag[0]
            if kind == "ada":
                idx = tag[1]
                return wada_d[idx * 128:(idx + 1) * 128, :], KC * 128, True
            sbk, l = tag[1], tag[2]

            def win(ch):
                r0 = (l * c.NIN + ch) * 128
                return win_d[r0:r0 + 128, :], KC * 128, True
            if kind == "q":
                return win(c.oq + tag[3])
            if kind == "k":
                return win(c.ok + tag[3])
            if kind == "v":
                return win(c.ov + tag[3])
            if kind == "f":
                return win(c.of + tag[3])
            if kind == "u":
                return win(c.ou + tag[3])
            if kind == "vs":
                return win(c.ovs + tag[3])
            if kind == "ga":
                return win(c.oga + tag[3])
            if kind == "gf":
                return win(c.ogf + tag[3])
            if kind == "gc":
                return win(c.ogc + tag[3])
            if kind in ("dftc", "dfts"):
                cs_ = 0 if kind == "dftc" else 1
                if sbk == "P":
                    return dftp_d[cs_ * 128:(cs_ + 1) * 128, :], 512, False
                nb = tag[4]
                return dfts_d[(nb * 2 + cs_) * 128:(nb * 2 + cs_ + 1) * 128, :], 2048, False
            if kind == "br":
                r0 = (l * KC + tag[3]) * 128
                return wbr_d[r0:r0 + 128, :], KC * 128, True
            if kind == "wo":
                r0 = (l * KC + tag[3]) * 128
                return wout_d[r0:r0 + 128, :], KC * 128, True
            if kind == "w1":
                r0 = (l * (c.DFF // 128) + tag[3] * 8 + tag[4]) * 128
                return w1_d[r0:r0 + 128, :], KC * 128, True
            if kind == "w2":
                r0 = ((l * c.NHC + tag[3]) * (KC // 2) + tag[4]) * 128
                return w2_d[r0:r0 + 128, :], 2048, True
            raise KeyError(tag)

        def wissue():
            try:
                tag = next(wstate["plan"])
            except StopIteration:
                return False
            src, n, cast = src_of(tag)
            i = wstate["n"] % NSLOT
            wstate["n"] += 1
            q = pool if cast else sp
            k.dma(q, wb_t[i][:, 0:n], src, (wb_s if cast else wb_s2)[i], writes=[wb_b[i]])
            wq.append((tag, V(wb_t[i][:, 0:n], [wb_b[i]])))
            return True

        def wget(tag):
            if plan_tags is None:
                rec_tags.append(tag)
                n = src_of(tag)[1]
                return V(wb_t[0][:, 0:n], [wb_b[0]])
            while len(wq) < NSLOT - 1:
                if not wissue():
                    break
            t, v = wq.pop(0)
            assert t == tag, (t, tag)
            return v

        def wrefill():
            if plan_tags is None:
                return
            while len(wq) < NSLOT - 1:
                if not wissue():
                    break

        k.op(pool, lambda: nc.gpsimd.memset(identb[:, :], 1.0), writes=[cbuf])
        k.op(pool, lambda: nc.gpsimd.affine_select(out=identb[:, :], in_=identb[:, :], pattern=[[-1, 128]],
                                                   compare_op=ALU.is_equal, fill=0.0, base=0, channel_multiplier=1),
             reads=[cbuf], writes=[cbuf])
        k.op(pool, lambda: nc.gpsimd.memset(identf[:, :], 1.0), writes=[cbuf])
        k.op(pool, lambda: nc.gpsimd.affine_select(out=identf[:, :], in_=identf[:, :], pattern=[[-1, 128]],
                                                   compare_op=ALU.is_equal, fill=0.0, base=0, channel_multiplier=1),
             reads=[cbuf], writes=[cbuf])
        k.op(pool, lambda: nc.gpsimd.memset(onesb[:, :], 1.0), writes=[cbuf])
        k.dma(sp, csc[:, :], csc_d[:, :], k.dsem("m"), writes=[cbuf])
        k.dma(sp, gains[:, :], gains_d[:, :], k.dsem("m"), writes=[gainsb])
        CB = [cbuf]

        cv_f = scr(0, KC * 2 * 4, F32)
        k.dma(sp, cv_f.ap, cvec_d[:, :], k.dsem("m"), writes=[cv_f])
        k.dma(sp, bada_p[:, :], bada_d[:, :], k.dsem("m"), writes=[badab])
        k.op(act, lambda: nc.scalar.activation(out=sT_p[:, :], in_=cv_f.ap, func=AF.Silu), reads=[cv_f], writes=[sTb])
        mod3 = modsb[:, :].rearrange("p (i v) -> p i v", v=2)

        def ada_chunk(idx):
            w = wget(("ada", idx))
            bk = bank()
            fns = [lambda kc=kc: nc.tensor.matmul(bk.ap[:, 0:2], lhsT=w.ap[:, kc * 128:(kc + 1) * 128],
                                                  rhs=sT_p[:, kc * 2:kc * 2 + 2], start=(kc == 0), stop=(kc == KC - 1))
                   for kc in range(KC)]
            k.group(fns, reads=[w, sTb], writes=[bk])
            k.op(dve, lambda: nc.vector.tensor_scalar(out=mod3[:, idx, :], in0=bk.ap[:, 0:2], scalar1=bada_p[:, idx:idx + 1],
                                                      scalar2=None, op0=ALU.add), reads=[bk, badab], writes=[modb])
            wrefill()

        N0 = 6 * KC
        for idx in range(N0):
            ada_chunk(idx)
        pending_ada = list(range(N0, c.NMOD))
        ada_on = [False]

        def ada_tick():
            if not ada_on[0] or not pending_ada:
                return
            n = 2 if len(pending_ada) > 200 else 1
            for _ in range(n):
                if pending_ada:
                    ada_chunk(pending_ada.pop(0))

        def ada_drain():
            while pending_ada:
                ada_chunk(pending_ada.pop(0))

        def modv(l, kind, v):
            i0 = (l * 6 + kind) * KC
            return modsb[:, :].rearrange("p (i v) -> p i v", v=2)[:, i0:i0 + KC, v]

        def gainv(kind, l):
            i0 = (kind * DEPTH + l) * KC
            return gains[:, i0:i0 + KC]

        def lv(i):
            return lvec[:, i * KC:(i + 1) * KC]

        def layer_vectors(l, v):
            rd, wr = [modb, gainsb], [lvb]
            k.op(dve, lambda: nc.vector.scalar_tensor_tensor(out=lv(0), in0=modv(l, 1, v), scalar=1.0, in1=gainv(0, l),
                                                             op0=ALU.add, op1=ALU.mult), reads=rd, writes=wr)
            k.op(dve, lambda: nc.vector.tensor_copy(out=lv(1), in_=modv(l, 0, v)), reads=rd, writes=wr)
            k.op(dve, lambda: nc.vector.tensor_tensor(out=lv(2), in0=modv(l, 2, v), in1=gainv(1, l), op=ALU.mult),
                 reads=rd, writes=wr)
            k.op(dve, lambda: nc.vector.scalar_tensor_tensor(out=lv(3), in0=modv(l, 4, v), scalar=1.0, in1=gainv(2, l),
                                                             op0=ALU.add, op1=ALU.mult), reads=rd, writes=wr)
            k.op(dve, lambda: nc.vector.tensor_copy(out=lv(4), in_=modv(l, 3, v)), reads=rd, writes=wr)
            k.op(dve, lambda: nc.vector.tensor_tensor(out=lv(5), in0=modv(l, 5, v), in1=gainv(3, l), op=ALU.mult),
                 reads=rd, writes=wr)

        def rstd_from(ssb, dst):
            k.op(dve, lambda: nc.vector.tensor_scalar(out=dst.ap, in0=ssb.ap, scalar1=1.0 / D, scalar2=EPS,
                                                      op0=ALU.mult, op1=ALU.add), reads=[ssb], writes=[dst])
            k.op(act, lambda: nc.scalar.activation(out=dst.ap, in_=dst.ap, func=AF.Sqrt), reads=[dst], writes=[dst])
            k.op(dve, lambda: nc.vector.reciprocal(out=dst.ap, in_=dst.ap), reads=[dst], writes=[dst])

        def stat_acc(src_v, tb, first, last, sqi):
            sq = scr(16384 + (sqi % 2) * 1024, 1024, BF16)
            k.op(act, lambda: nc.scalar.activation(out=sq.ap, in_=src_v.ap, func=AF.Square), reads=[src_v], writes=[sq])
            ssb = sbank(tb)
            k.group([lambda: nc.tensor.matmul(ssb.ap, lhsT=onesb[:, :], rhs=sq.ap, start=first, stop=last)],
                    reads=[sq] + CB, writes=[ssb])

        def norm_to_h(ih, gi, si, tb):
            rs = scr(tb * 2048, 2048, F32)
            rstd_from(sbank(tb), rs)
            for j in range(KC):
                tmp = scr(4096 + (j % 2) * 2048, 2048, F32)
                xv = xch(j, tb)
                k.op(dve, lambda: nc.vector.tensor_tensor(out=tmp.ap, in0=xv.ap, in1=rs.ap, op=ALU.mult),
                     reads=[xv, rs], writes=[tmp])
                hv = hch(ih, j, tb * 512, tb * 512 + 512)
                k.op(act, lambda: nc.scalar.activation(out=hv.ap, in_=tmp.ap, func=AF.Identity,
                                                       scale=lv(gi)[:, j:j + 1], bias=lv(si)[:, j:j + 1]),
                     reads=[tmp, lvb], writes=[hv])

        def sublayer_finish(pair, gti, nxt):
            for tb in range(2):
                rs = scr(tb * 2048, 2048, F32)
                rstd_from(sbank(tb), rs)
                for j in range(KC):
                    tmp = scr(4096 + (j % 2) * 2048, 2048, F32)
                    rv = fch(pair, j, tb)
                    xv = xch(j, tb)
                    k.op(dve, lambda: nc.vector.tensor_tensor(out=tmp.ap, in0=rv.ap, in1=rs.ap, op=ALU.mult),
                         reads=[rv, rs], writes=[tmp])
                    k.op(dve, lambda: nc.vector.scalar_tensor_tensor(out=xv.ap, in0=tmp.ap, scalar=lv(gti)[:, j:j + 1],
                                                                     in1=xv.ap, op0=ALU.mult, op1=ALU.add),
                         reads=[tmp, xv, lvb], writes=[xv])
                    if nxt is not None:
                        stat_acc(xv, tb, j == 0, j == KC - 1, j)
                if nxt is not None:
                    norm_to_h(nxt[0], nxt[1], nxt[2], tb)

        def proj_fm(w, ih, evac):
            for tb in range(2):
                bk = bank()
                hs = [hch(ih, kc, tb * 512, tb * 512 + 512) for kc in range(KC)]
                fns = [lambda kc=kc, bk=bk, hs=hs: nc.tensor.matmul(
                    bk.ap, lhsT=w.ap[:, kc * 128:(kc + 1) * 128], rhs=hs[kc].ap,
                    start=(kc == 0), stop=(kc == KC - 1)) for kc in range(KC)]
                k.group(fns, reads=[w] + hs, writes=[bk])
                evac(tb, bk)
            wrefill()
            ada_tick()

        small = scr(22016, 512, F32)

        def superblock(sbk):
            isP = sbk == "P"
            vsel = 0 if isP else 1
            x_d = xp_d if isP else xs_d
            y_d = yp_d if isP else ys_d
            for t in range(8):
                for fb in range(D // 512):
                    si = (t * (D // 512) + fb) % 2
                    st = scr(si * 2048, 2048, F32)
                    k.dma(sp, st.ap, x_d[t * 128:(t + 1) * 128, fb * 512:(fb + 1) * 512], stage_s[si], writes=[st])
                    bk = bank()
                    k.group([lambda i=i, bk=bk, st=st: nc.tensor.transpose(
                        bk.ap[:, i * 128:(i + 1) * 128], st.ap[:, i * 128:(i + 1) * 128], identf[:, :])
                        for i in range(4)], reads=[st] + CB, writes=[bk])
                    for i in range(4):
                        j = fb * 4 + i
                        tb, tt = t // 4, t % 4
                        xv = xch(j, tb)
                        k.op(dve if i % 2 == 0 else act,
                             (lambda i=i, bk=bk, xv=xv, tt=tt: nc.vector.tensor_copy(
                                 out=xv.ap[:, tt * 128:(tt + 1) * 128], in_=bk.ap[:, i * 128:(i + 1) * 128]))
                             if i % 2 == 0 else
                             (lambda i=i, bk=bk, xv=xv, tt=tt: nc.scalar.copy(
                                 out=xv.ap[:, tt * 128:(tt + 1) * 128], in_=bk.ap[:, i * 128:(i + 1) * 128])),
                             reads=[bk], writes=[xv])
            ih, io, im = 0, 1, 2
            layer_vectors(0, vsel)
            ada_on[0] = True
            for tb in range(2):
                for j in range(KC):
                    stat_acc(xch(j, tb), tb, j == 0, j == KC - 1, j)
                norm_to_h(ih, 0, 1, tb)

            for l in range(DEPTH):
                qT = scr(0, 2048, BF16)
                kT = scr(2048, 2048, BF16)
                for h in range(H):
                    oh = h
                    if isP:
                        prompt_head(l, h, ih, io, oh, qT, kT)
                    else:
                        sample_head(l, h, ih, io, oh, qT, kT)
                for g in range(GF):
                    fourier_group(sbk, l, g, ih, io, H + g)
                for g in range(GC):
                    sgu_group(sbk, l, g, ih, io, H + GF + g)
                for j in range(KC):
                    sg = [scr(i * 2048, 2048, BF16) for i in range(3)]

                    def ev_gate(gi):
                        def ev(tb, bk, gi=gi):
                            sv = sub(sg[gi], sg[gi].ap[:, tb * 512:(tb + 1) * 512])
                            k.op(act, lambda: nc.scalar.activation(out=sv.ap, in_=bk.ap, func=AF.Sigmoid),
                                 reads=[bk], writes=[sv])
                        return ev
                    proj_fm(wget(("ga", sbk, l, j)), ih, ev_gate(0))
                    proj_fm(wget(("gf", sbk, l, j)), ih, ev_gate(1))
                    proj_fm(wget(("gc", sbk, l, j)), ih, ev_gate(2))
                    wbr = wget(("br", sbk, l, j))
                    for tb in range(2):
                        acc = scr(6144, 2048, F32)
                        ranges = [(0, H), (H, H + GF), (H + GF, KC)]
                        for gi, (a0, a1) in enumerate(ranges):
                            bk = bank()
                            os_ = [hch(io, kc, tb * 512, tb * 512 + 512) for kc in range(a0, a1)]
                            fns = [lambda n=n, kc=kc, bk=bk, os_=os_, a0=a0, a1=a1: nc.tensor.matmul(
                                bk.ap, lhsT=wbr.ap[:, kc * 128:(kc + 1) * 128], rhs=os_[n].ap,
                                start=(kc == a0), stop=(kc == a1 - 1)) for n, kc in enumerate(range(a0, a1))]
                            k.group(fns, reads=[wbr] + os_, writes=[bk])
                            mv = hch(im, j, tb * 512, tb * 512 + 512)
                            sv = sub(sg[gi], sg[gi].ap[:, tb * 512:(tb + 1) * 512])
                            if gi == 0:
                                k.op(dve, lambda: nc.vector.tensor_tensor(
                                    out=acc.ap, in0=bk.ap, in1=sv.ap, op=ALU.mult), reads=[bk, sv], writes=[acc])
                            else:
                                t2 = scr(8192, 2048, F32)
                                k.op(dve, lambda: nc.vector.tensor_tensor(
                                    out=t2.ap, in0=bk.ap, in1=sv.ap, op=ALU.mult), reads=[bk, sv], writes=[t2])
                                dst = acc if gi == 1 else mv
                                k.op(dve, lambda: nc.vector.tensor_tensor(
                                    out=dst.ap, in0=t2.ap, in1=acc.ap, op=ALU.add), reads=[t2, acc], writes=[dst])
                    wrefill()
                pair = (ih, io)
                for j in range(KC):
                    w = wget(("wo", sbk, l, j))
                    for tb in range(2):
                        bk = bank()
                        ms = [hch(im, kc, tb * 512, tb * 512 + 512) for kc in range(KC)]
                        fns = [lambda kc=kc, bk=bk, ms=ms, w=w: nc.tensor.matmul(
                            bk.ap, lhsT=w.ap[:, kc * 128:(kc + 1) * 128], rhs=ms[kc].ap,
                            start=(kc == 0), stop=(kc == KC - 1)) for kc in range(KC)]
                        k.group(fns, reads=[w] + ms, writes=[bk])
                        rv = fch(pair, j, tb)
                        k.op(dve, lambda bk=bk, rv=rv: nc.vector.tensor_copy(out=rv.ap, in_=bk.ap), reads=[bk], writes=[rv])
                        stat_acc(bk, tb, j == 0, j == KC - 1, j)
                    wrefill()
                sublayer_finish(pair, 2, (im, 3, 4))
                for hc in range(c.NHC):
                    for fc in range(8):
                        w = wget(("w1", sbk, l, hc, fc))

                        def ev_a(tb, bk, fc=fc):
                            av = scr(fc * 2048 + tb * 1024, 1024, BF16)
                            k.op(act, lambda: nc.scalar.activation(out=av.ap, in_=bk.ap, func=AF.Relu), reads=[bk], writes=[av])
                            k.op(dve, lambda: nc.vector.tensor_tensor(out=av.ap, in0=av.ap, in1=av.ap, op=ALU.mult),
                                 reads=[av], writes=[av])
                        proj_fm(w, im, ev_a)
                    for jb in range(KC // 2):
                        w = wget(("w2", sbk, l, hc, jb))
                        for jj in range(2):
                            j = jb * 2 + jj
                            for tb in range(2):
                                bk = bank()
                                as_ = [scr(fc * 2048 + tb * 1024, 1024, BF16) for fc in range(8)]
                                fns = [lambda fc=fc, bk=bk, as_=as_, w=w, jj=jj: nc.tensor.matmul(
                                    bk.ap, lhsT=w.ap[:, fc * 256 + jj * 128:fc * 256 + jj * 128 + 128], rhs=as_[fc].ap,
                                    start=(fc == 0), stop=(fc == 7)) for fc in range(8)]
                                k.group(fns, reads=[w] + as_, writes=[bk])
                                fv = fch(pair, j, tb)
                                if hc == 0:
                                    k.op(dve, lambda bk=bk, fv=fv: nc.vector.tensor_copy(out=fv.ap, in_=bk.ap),
                                         reads=[bk], writes=[fv])
                                else:
                                    k.op(dve, lambda bk=bk, fv=fv: nc.vector.tensor_tensor(
                                        out=fv.ap, in0=bk.ap, in1=fv.ap, op=ALU.add), reads=[bk, fv], writes=[fv])
                                if hc == c.NHC - 1:
                                    stat_acc(fv, tb, j == 0, j == KC - 1, j)
                        wrefill()
                sublayer_finish(pair, 5, None)
                if l + 1 < DEPTH:
                    ada_drain()
                    layer_vectors(l + 1, vsel)
                    for tb in range(2):
                        for j in range(KC):
                            stat_acc(xch(j, tb), tb, j == 0, j == KC - 1, j)
                        norm_to_h(im, 0, 1, tb)
                    ih, io, im = im, ih, io
            outs = []
            for t in range(8):
                tb, tt = t // 4, t % 4
                for fb in range(D // 512):
                    bk = bank()
                    xs_ = [xch(fb * 4 + i, tb) for i in range(4)]
                    k.group([lambda i=i, bk=bk, xs_=xs_, tt=tt: nc.tensor.transpose(
                        bk.ap[:, i * 128:(i + 1) * 128], xs_[i].ap[:, tt * 128:(tt + 1) * 128], identf[:, :])
                        for i in range(4)], reads=xs_ + CB, writes=[bk])
                    si = (t * (D // 512) + fb) % 2
                    st = scr(si * 2048, 2048, F32)
                    k.op(act, lambda bk=bk, st=st: nc.scalar.copy(out=st.ap, in_=bk.ap), reads=[bk], writes=[st])
                    k.dma(sp, y_d[t * 128:(t + 1) * 128, fb * 512:(fb + 1) * 512], st.ap, stage_s[si], reads=[st])

        def prompt_head(l, h, ih, io, oh, qT, kT):
            kf = scr(4096, 2048, F32)
            vf = scr(6144, 2048, F32)
            vtm = scr(8192, 2048, BF16, (8, 128))
            kout = scr(10240, 2048, F32, (4, 128))
            vout = scr(12288, 2048, F32, (4, 128))
            Pm = scr(14336, 1024, BF16, (2, 256))
            PT = scr(15360, 1024, BF16, (4, 128))
            scale = 1.0 / math.sqrt(128.0)

            def ev_q(tb, bk):
                qv = sub(qT, qT.ap[:, tb * 512:(tb + 1) * 512])
                k.op(act, lambda: nc.scalar.activation(out=qv.ap, in_=bk.ap, func=AF.Copy, scale=scale),
                     reads=[bk], writes=[qv])
            proj_fm(wget(("q", "P", l, h)), ih, ev_q)

            def kv_out(tb, srcf, outst, dst_d, with_tm, kv_s):
                bk = bank()
                k.group([lambda i=i, bk=bk: nc.tensor.transpose(
                    bk.ap[:, i * 128:(i + 1) * 128], srcf.ap[:, i * 128:(i + 1) * 128], identf[:, :])
                    for i in range(4)], reads=[srcf] + CB, writes=[bk])
                k.op(act, lambda: nc.scalar.copy(out=outst.ap, in_=bk.ap.rearrange("p (a b) -> p a b", a=4)),
                     reads=[bk], writes=[outst])
                if with_tm:
                    tv = sub(vtm, vtm.ap[:, tb * 4:(tb + 1) * 4, :])
                    k.op(dve, lambda: nc.vector.tensor_copy(out=tv.ap, in_=bk.ap.rearrange("p (a b) -> p a b", a=4)),
                         reads=[bk], writes=[tv])
                for bb in range(2):
                    b = tb * 2 + bb
                    r0 = (b * DEPTH + l) * c.SEQ
                    dst = dst_d[r0:r0 + 256, h * 128:(h + 1) * 128].rearrange("(a p) d -> p a d", p=128)
                    k.dma(sp, dst, outst.ap[:, bb * 2:bb * 2 + 2, :], kv_s, reads=[outst])

            def ev_k(tb, bk):
                kv = sub(kT, kT.ap[:, tb * 512:(tb + 1) * 512])
                k.op(dve, lambda: nc.vector.tensor_copy(out=kv.ap, in_=bk.ap), reads=[bk], writes=[kv])
                k.op(act, lambda: nc.scalar.copy(out=kf.ap, in_=bk.ap), reads=[bk], writes=[kf])
                kv_out(tb, kf, kout, nk_d, False, kout_s)
            proj_fm(wget(("k", "P", l, h)), ih, ev_k)

            def ev_v(tb, bk):
                k.op(act, lambda: nc.scalar.copy(out=vf.ap, in_=bk.ap), reads=[bk], writes=[vf])
                kv_out(tb, vf, vout, nv_d, True, vout_s)
            proj_fm(wget(("v", "P", l, h)), ih, ev_v)

            Sb = {}

            def pbufs(b):
                Pm = scr(14336 + (b % 2) * 2048, 1024, BF16, (2, 256))
                PT = scr(15360 + (b % 2) * 2048, 1024, BF16, (4, 128))
                sm = scr(22016 + (b % 2) * 512, 512, F32)
                return Pm, PT, sm

            def p_S(b):
                t0 = b * 256
                sbk_ = bank()
                Sb[b] = sbk_
                for qt in range(2):
                    k.group([lambda qt=qt: nc.tensor.matmul(
                        sbk_.ap[:, qt * 256:(qt + 1) * 256], lhsT=qT.ap[:, t0 + qt * 128:t0 + (qt + 1) * 128],
                        rhs=kT.ap[:, t0:t0 + 256], start=True, stop=True)], reads=[qT, kT], writes=[sbk_])

            def p_a(b):
                Pm, PT, sm = pbufs(b)
                sbk_ = Sb[b]
                mx, nmx, rsum = sm.ap[:, 0:2], sm.ap[:, 2:4], sm.ap[:, 4:6]
                k.op(dve, lambda: nc.vector.reduce_max(out=mx, in_=sbk_.ap.rearrange("p (a b) -> p a b", a=2), axis=AX.X),
                     reads=[sbk_], writes=[sm])
                k.op(dve, lambda: nc.vector.tensor_scalar(out=nmx, in0=mx, scalar1=-1.0, scalar2=None, op0=ALU.mult),
                     reads=[sm], writes=[sm])
                k.op(dve, lambda: nc.vector.memset(rsum, 0.0), writes=[sm])
                for qt in range(2):
                    k.op(act, lambda qt=qt: nc.scalar.activation(
                        out=Pm.ap[:, qt, :], in_=sbk_.ap[:, qt * 256:(qt + 1) * 256], func=AF.Exp,
                        bias=nmx[:, qt:qt + 1], scale=1.0, accum_out=rsum[:, qt:qt + 1]),
                        reads=[sbk_, sm], writes=[Pm, sm])

            def p_b(b):
                Pm, PT, sm = pbufs(b)
                rsum, rrec = sm.ap[:, 4:6], sm.ap[:, 6:8]
                k.op(dve, lambda: nc.vector.reciprocal(out=rrec, in_=rsum), reads=[sm], writes=[sm])
                for qt in range(2):
                    k.op(dve, lambda qt=qt: nc.vector.tensor_scalar(
                        out=Pm.ap[:, qt, :], in0=Pm.ap[:, qt, :], scalar1=rrec[:, qt:qt + 1], scalar2=None,
                        op0=ALU.mult), reads=[Pm, sm], writes=[Pm])

            def p_T(b):
                Pm, PT, sm = pbufs(b)
                tbk = bank()
                tb16 = tbk.ap.bitcast(BF16)
                fns = []
                for kt in range(2):
                    for qt in range(2):
                        fns.append(lambda kt=kt, qt=qt: nc.tensor.transpose(
                            tb16[:, (kt * 2 + qt) * 128:(kt * 2 + qt + 1) * 128],
                            Pm.ap[:, qt, kt * 128:(kt + 1) * 128], identb[:, :]))
                k.group(fns, reads=[Pm] + CB, writes=[tbk])
                k.op(act, lambda: nc.scalar.copy(out=PT.ap, in_=tb16[:, 0:512].rearrange("p (a b) -> p a b", a=4)),
                     reads=[tbk], writes=[PT])

            def p_O(b):
                Pm, PT, sm = pbufs(b)
                t0 = b * 256
                obk = bank()
                k.group([lambda kt=kt: nc.tensor.matmul(
                    obk.ap[:, 0:256], lhsT=vtm.ap[:, b * 2 + kt, :],
                    rhs=PT.ap[:, kt * 2:kt * 2 + 2, :].rearrange("p a b -> p (a b)"),
                    start=(kt == 0), stop=(kt == 1)) for kt in range(2)], reads=[vtm, PT], writes=[obk])
                ov = hch(io, oh, t0, t0 + 256)
                k.op(act, lambda: nc.scalar.copy(out=ov.ap, in_=obk.ap[:, 0:256]), reads=[obk], writes=[ov])

            p_S(0)
            p_S(1)
            p_a(0)
            for i in range(c.PB + 2):
                if 0 <= i - 1 < c.PB:
                    p_T(i - 1)
                if i + 2 < c.PB:
                    p_S(i + 2)
                if i + 1 < c.PB:
                    p_a(i + 1)
                if i < c.PB:
                    p_b(i)
                if 0 <= i - 2 < c.PB:
                    p_O(i - 2)

        def sample_head(l, h, ih, io, oh, qT, kT):
            vT = scr(4096, 2048, BF16)
            vtm = scr(6144, 2048, BF16, (8, 128))
            ckT = scr(8192, 1024, BF16)
            cvh = scr(9216, 1024, BF16, (4, 128))
            bias = scr(10240, 2560, F32)
            Pm = scr(12800, 2304, BF16)
            PT = scr(15104, 2304, BF16, (9, 128))
            scale = 1.0 / math.sqrt(128.0)
            r0 = (l * H + h) * 128
            k.dma(pool, ckT.ap, ckT_d[r0:r0 + 128, :], ck_s, writes=[ckT])
            k.dma(pool, cvh.ap, cv_d[l * c.PAST:(l + 1) * c.PAST, h * 128:(h + 1) * 128].rearrange(
                "(a p) d -> p a d", p=128), cvh_s, writes=[cvh])

            def ev_q(tb, bk):
                qv = sub(qT, qT.ap[:, tb * 512:(tb + 1) * 512])
                k.op(act, lambda: nc.scalar.activation(out=qv.ap, in_=bk.ap, func=AF.Copy, scale=scale),
                     reads=[bk], writes=[qv])
            proj_fm(wget(("q", "S", l, h)), ih, ev_q)

            def ev_k(tb, bk):
                kv = sub(kT, kT.ap[:, tb * 512:(tb + 1) * 512])
                k.op(dve, lambda: nc.vector.tensor_copy(out=kv.ap, in_=bk.ap), reads=[bk], writes=[kv])
            proj_fm(wget(("k", "S", l, h)), ih, ev_k)

            def ev_v(tb, bk):
                vv = sub(vT, vT.ap[:, tb * 512:(tb + 1) * 512])
                k.op(act, lambda: nc.scalar.copy(out=vv.ap, in_=bk.ap), reads=[bk], writes=[vv])
            proj_fm(wget(("v", "S", l, h)), ih, ev_v)
            tbk = bank()
            tb16 = tbk.ap.bitcast(BF16)
            k.group([lambda i=i: nc.tensor.transpose(tb16[:, i * 128:(i + 1) * 128], vT.ap[:, i * 128:(i + 1) * 128],
                                                     identb[:, :]) for i in range(8)], reads=[vT] + CB, writes=[tbk])
            k.op(dve, lambda: nc.vector.tensor_copy(out=vtm.ap, in_=tb16.rearrange("p (a b) -> p a b", a=8)),
                 reads=[tbk], writes=[vtm])
            def geom(qt):
                rs0 = min(max(2 * qt - 4, 0), 8)
                rs1 = min(max(2 * qt - 3, 0), 8)
                kt0, kt1 = rs0 // 2, (rs1 + 7) // 2
                nkt = kt1 - kt0 + 1
                return kt0, nkt

            def bufs(qt):
                par = qt % 2
                RBK = [3 * par, 3 * par + 1, 3 * par + 2]

                def sap(c0, n):
                    bi = c0 // 512
                    return bank_ap[RBK[bi]][:, c0 - 512 * bi:c0 - 512 * bi + n]
                Pm = scr(12800 + par * 2304, 2304, BF16)
                PT = scr(17408 + par * 2304, 2304, BF16, (9, 128))
                smv = scr(22016 + par * 512, 512, F32)
                return RBK, sap, Pm, PT, smv

            def bias_load(qt):
                br0 = ((l * H + h) * 8 + qt) * 128
                k.dma(sp, bias.ap, bias_d[br0:br0 + 128, :], bias_s, writes=[bias])

            def stage_S(qt):
                kt0, nkt = geom(qt)
                RBK, sap, Pm, PT, smv = bufs(qt)
                Sreg = [bank_b[i] for i in RBK]
                qv = qT.ap[:, qt * 128:(qt + 1) * 128]
                segs = [(0, 512, kT.ap[:, kt0 * 128:kt0 * 128 + 512])]
                if nkt == 5:
                    segs.append((512, 128, kT.ap[:, kt0 * 128 + 512:kt0 * 128 + 640]))
                    segs.append((640, 384, ckT.ap[:, 0:384]))
                    segs.append((1024, 128, ckT.ap[:, 384:512]))
                else:
                    segs.append((512, 512, ckT.ap[:, 0:512]))
                k.group([lambda c0=c0, n=n, rhs=rhs: nc.tensor.matmul(sap(c0, n), lhsT=qv, rhs=rhs, start=True, stop=True)
                         for (c0, n, rhs) in segs], reads=[qT, kT, ckT], writes=Sreg)

            def stage_a(qt):
                kt0, nkt = geom(qt)
                RBK, sap, Pm, PT, smv = bufs(qt)
                m = smv.ap
                bsegs = [(0, 0, 512), (1, 512, 512)] + ([(2, 1024, 128)] if nkt == 5 else [])
                k.op(dve, lambda: nc.vector.tensor_tensor(out=sap(0, 512), in0=sap(0, 512), in1=bias.ap[:, 0:512], op=ALU.add),
                     reads=[bank_b[RBK[0]], bias], writes=[bank_b[RBK[0]]])
                if nkt == 5:
                    k.op(dve, lambda: nc.vector.tensor_tensor(out=sap(512, 128), in0=sap(512, 128), in1=bias.ap[:, 512:640],
                                                              op=ALU.add), reads=[bank_b[RBK[1]], bias], writes=[bank_b[RBK[1]]])
                if qt + 1 < 8:
                    bias_load(qt + 1)
                for i, (bi, c0, n) in enumerate(bsegs):
                    k.op(dve, lambda i=i, c0=c0, n=n: nc.vector.reduce_max(out=m[:, 8 + i:9 + i], in_=sap(c0, n), axis=AX.X),
                         reads=[bank_b[RBK[bi]]], writes=[smv])
                k.op(dve, lambda: nc.vector.reduce_max(out=m[:, 0:1], in_=m[:, 8:8 + len(bsegs)], axis=AX.X),
                     reads=[smv], writes=[smv])
                k.op(dve, lambda: nc.vector.tensor_scalar(out=m[:, 1:2], in0=m[:, 0:1], scalar1=-1.0, scalar2=None, op0=ALU.mult),
                     reads=[smv], writes=[smv])
                k.op(dve, lambda: nc.vector.memset(m[:, 12:16], 0.0), writes=[smv])
                for i, (bi, c0, n) in enumerate(bsegs):
                    k.op(act, lambda i=i, c0=c0, n=n: nc.scalar.activation(
                        out=Pm.ap[:, c0:c0 + n], in_=sap(c0, n), func=AF.Exp, bias=m[:, 1:2], scale=1.0,
                        accum_out=m[:, 12 + i:13 + i]), reads=[bank_b[RBK[bi]], smv], writes=[Pm, smv])

            def stage_b(qt):
                kt0, nkt = geom(qt)
                RBK, sap, Pm, PT, smv = bufs(qt)
                m = smv.ap
                ntot = nkt * 128 + 512
                nb_ = 3 if nkt == 5 else 2
                k.op(dve, lambda: nc.vector.reduce_sum(out=m[:, 2:3], in_=m[:, 12:12 + nb_], axis=AX.X),
                     reads=[smv], writes=[smv])
                k.op(dve, lambda: nc.vector.reciprocal(out=m[:, 3:4], in_=m[:, 2:3]), reads=[smv], writes=[smv])
                k.op(act, lambda: nc.scalar.activation(out=Pm.ap[:, 0:ntot], in_=Pm.ap[:, 0:ntot], func=AF.Copy, scale=m[:, 3:4]),
                     reads=[Pm, smv], writes=[Pm])

            def stage_T(qt):
                kt0, nkt = geom(qt)
                RBK, sap, Pm, PT, smv = bufs(qt)
                nt = nkt + 4
                t1 = V(bank_ap[7], [bank_b[7]])
                t1b = t1.ap.bitcast(BF16)
                k.group([lambda i=i: nc.tensor.transpose(t1b[:, i * 128:(i + 1) * 128], Pm.ap[:, i * 128:(i + 1) * 128],
                                                         identb[:, :]) for i in range(8)], reads=[Pm] + CB, writes=[t1])
                k.op(act, lambda: nc.scalar.copy(out=PT.ap[:, 0:4, :], in_=t1b[:, 0:512].rearrange("p (a b) -> p a b", a=4)),
                     reads=[t1], writes=[PT])
                k.op(dve, lambda: nc.vector.tensor_copy(out=PT.ap[:, 4:8, :], in_=t1b[:, 512:1024].rearrange("p (a b) -> p a b", a=4)),
                     reads=[t1], writes=[PT])
                if nt == 9:
                    k.group([lambda: nc.tensor.transpose(t1b[:, 0:128], Pm.ap[:, 1024:1152], identb[:, :])],
                            reads=[Pm] + CB, writes=[t1])
                    k.op(dve, lambda: nc.vector.tensor_copy(out=PT.ap[:, 8, :], in_=t1b[:, 0:128]), reads=[t1], writes=[PT])

            def stage_O(qt):
                kt0, nkt = geom(qt)
                RBK, sap, Pm, PT, smv = bufs(qt)
                nt = nkt + 4
                obk = V(bank_ap[6], [bank_b[6]])
                fns = []
                for i in range(nt):
                    lhs = vtm.ap[:, kt0 + i, :] if i < nkt else cvh.ap[:, i - nkt, :]
                    fns.append(lambda i=i, lhs=lhs: nc.tensor.matmul(
                        obk.ap[:, (qt % 4) * 128:(qt % 4 + 1) * 128], lhsT=lhs, rhs=PT.ap[:, i, :],
                        start=(i == 0), stop=(i == nt - 1)))
                k.group(fns, reads=[vtm, cvh, PT], writes=[obk])
                if qt % 4 == 3:
                    tb = qt // 4
                    ov = hch(io, oh, tb * 512, tb * 512 + 512)
                    k.op(act, lambda: nc.scalar.copy(out=ov.ap, in_=obk.ap), reads=[obk], writes=[ov])

            bias_load(0)
            stage_S(0)
            stage_S(1)
            stage_a(0)
            for i in range(8 + 2):
                if 0 <= i - 1 < 8:
                    stage_T(i - 1)
                if i + 2 < 8:
                    stage_S(i + 2)
                if i + 1 < 8:
                    stage_a(i + 1)
                if i < 8:
                    stage_b(i)
                if 0 <= i - 2 < 8:
                    stage_O(i - 2)

        def fourier_group(sbk, l, g, ih, io, oc):
            isP = sbk == "P"
            fT = scr(0, 2048, BF16)
            AB = scr(2048, 4096, BF16, (8, 256))

            def ev_f(tb, bk):
                fv = sub(fT, fT.ap[:, tb * 512:(tb + 1) * 512])
                k.op(act, lambda: nc.scalar.copy(out=fv.ap, in_=bk.ap), reads=[bk], writes=[fv])
            proj_fm(wget(("f", sbk, l, g)), ih, ev_f)
            for t2 in range(4):
                bk = bank()
                for i in range(2):
                    t = t2 * 2 + i
                    k.group([lambda t=t, i=i, bk=bk: nc.tensor.matmul(
                        bk.ap[:, i * 256:(i + 1) * 256], lhsT=fT.ap[:, t * 128:(t + 1) * 128], rhs=csc[:, :],
                        start=True, stop=True)], reads=[fT] + CB, writes=[bk])
                av = sub(AB, AB.ap[:, t2 * 2:t2 * 2 + 2, :])
                k.op(dve if t2 % 2 == 0 else act,
                     (lambda bk=bk, av=av: nc.vector.tensor_copy(out=av.ap, in_=bk.ap.rearrange("p (a b) -> p a b", a=2)))
                     if t2 % 2 == 0 else
                     (lambda bk=bk, av=av: nc.scalar.copy(out=av.ap, in_=bk.ap.rearrange("p (a b) -> p a b", a=2))),
                     reads=[bk], writes=[av])
            if isP:
                wc = wget(("dftc", sbk, l, g, 0))
                ws_ = wget(("dfts", sbk, l, g, 0))
                for b in range(c.PB):
                    bk = bank()
                    fns = []
                    for m in range(2):
                        t = b * 2 + m
                        fns.append(lambda t=t, m=m, bk=bk: nc.tensor.matmul(
                            bk.ap[:, 0:256], lhsT=AB.ap[:, t, 0:128], rhs=wc.ap[:, m * 256:(m + 1) * 256],
                            start=(m == 0), stop=False))
                        fns.append(lambda t=t, m=m, bk=bk: nc.tensor.matmul(
                            bk.ap[:, 0:256], lhsT=AB.ap[:, t, 128:256], rhs=ws_.ap[:, m * 256:(m + 1) * 256],
                            start=False, stop=(m == 1)))
                    k.group(fns, reads=[AB, wc, ws_], writes=[bk])
                    ov = hch(io, oc, b * 256, b * 256 + 256)
                    k.op(act, lambda bk=bk, ov=ov: nc.scalar.copy(out=ov.ap, in_=bk.ap[:, 0:256]), reads=[bk], writes=[ov])
                wrefill()
            else:
                for nb in range(4):
                    wc = wget(("dftc", sbk, l, g, nb))
                    ws_ = wget(("dfts", sbk, l, g, nb))
                    bk = bank()
                    fns = []
                    for m in range(8):
                        fns.append(lambda m=m, bk=bk, wc=wc: nc.tensor.matmul(
                            bk.ap[:, 0:256], lhsT=AB.ap[:, m, 0:128], rhs=wc.ap[:, m * 256:(m + 1) * 256],
                            start=(m == 0), stop=False))
                        fns.append(lambda m=m, bk=bk, ws_=ws_: nc.tensor.matmul(
                            bk.ap[:, 0:256], lhsT=AB.ap[:, m, 128:256], rhs=ws_.ap[:, m * 256:(m + 1) * 256],
                            start=False, stop=(m == 7)))
                    k.group(fns, reads=[AB, wc, ws_], writes=[bk])
                    ov = hch(io, oc, nb * 256, nb * 256 + 256)
                    k.op(act, lambda bk=bk, ov=ov: nc.scalar.copy(out=ov.ap, in_=bk.ap[:, 0:256]), reads=[bk], writes=[ov])
                    wrefill()

        def sgu_group(sbk, l, g, ih, io, oc):
            uT = scr(0, 2048, BF16)
            vsT = scr(2048, 2048, BF16)
            vn = scr(4096, 2048, BF16, (8, 128))
            wsp = scr(6144, 256, BF16)
            gsg = scr(6656, 512, F32)
            bsp = scr(8192, 2048, F32)
            tmp = scr(10240, 2048, F32)
            r0 = (l * GC + g) * 128
            k.dma(pool, wsp.ap, wspT_d[r0:r0 + 128, :], wsp_s, writes=[wsp])
            k.dma(sp, bsp.ap, bsp_d[r0:r0 + 128, :], bsp_s, writes=[bsp])
            k.dma(sp, gsg.ap, gsgu_d[r0:r0 + 128, :], gsg_s, writes=[gsg])

            def ev_u(tb, bk):
                uv = sub(uT, uT.ap[:, tb * 512:(tb + 1) * 512])
                k.op(act, lambda: nc.scalar.activation(out=uv.ap, in_=bk.ap, func=AF.Gelu), reads=[bk], writes=[uv])
            proj_fm(wget(("u", sbk, l, g)), ih, ev_u)

            def ev_v(tb, bk):
                vv = sub(vsT, vsT.ap[:, tb * 512:(tb + 1) * 512])
                k.op(act, lambda: nc.scalar.activation(out=vv.ap, in_=bk.ap, func=AF.Gelu), reads=[bk], writes=[vv])
            proj_fm(wget(("vs", sbk, l, g)), ih, ev_v)
            m = small.ap
            tbk = bank()
            tb16 = tbk.ap.bitcast(BF16)
            k.group([lambda i=i: nc.tensor.transpose(tb16[:, i * 128:(i + 1) * 128], vsT.ap[:, i * 128:(i + 1) * 128],
                                                     identb[:, :]) for i in range(8)], reads=[vsT] + CB, writes=[tbk])
            k.op(dve, lambda: nc.vector.memset(m[:, 16:24], 0.0), writes=[small])
            for i in range(8):
                k.op(act, lambda i=i: nc.scalar.activation(out=tmp.ap.bitcast(BF16)[:, 0:128], in_=tb16[:, i * 128:(i + 1) * 128],
                                                           func=AF.Square, accum_out=m[:, 16 + i:17 + i]),
                     reads=[tbk, small], writes=[tmp, small])
            k.op(dve, lambda: nc.vector.tensor_scalar(out=m[:, 24:32], in0=m[:, 16:24], scalar1=1.0 / 128, scalar2=EPS,
                                                      op0=ALU.mult, op1=ALU.add), reads=[small], writes=[small])
            k.op(act, lambda: nc.scalar.activation(out=m[:, 24:32], in_=m[:, 24:32], func=AF.Sqrt), reads=[small], writes=[small])
            k.op(dve, lambda: nc.vector.reciprocal(out=m[:, 32:40], in_=m[:, 24:32]), reads=[small], writes=[small])
            for i in range(8):
                k.op(dve, lambda i=i: nc.vector.scalar_tensor_tensor(
                    out=vn.ap[:, i, :], in0=tb16[:, i * 128:(i + 1) * 128], scalar=m[:, 32 + i:33 + i], in1=gsg.ap,
                    op0=ALU.mult, op1=ALU.mult), reads=[tbk, small, gsg], writes=[vn])
            for tb in range(2):
                bk = bank()
                for i in range(4):
                    ch = tb * 4 + i
                    k.group([lambda i=i, ch=ch, bk=bk: nc.tensor.matmul(
                        bk.ap[:, i * 128:(i + 1) * 128], lhsT=vn.ap[:, ch, :], rhs=wsp.ap, start=True, stop=True)],
                        reads=[vn, wsp], writes=[bk])
                k.op(dve, lambda bk=bk: nc.vector.tensor_tensor(out=tmp.ap, in0=bk.ap, in1=bsp.ap, op=ALU.add),
                     reads=[bk, bsp], writes=[tmp])
                ov = hch(io, oc, tb * 512, tb * 512 + 512)
                k.op(dve, lambda ov=ov, tb=tb: nc.vector.tensor_tensor(
                    out=ov.ap, in0=tmp.ap, in1=uT.ap[:, tb * 512:(tb + 1) * 512], op=ALU.mult),
                    reads=[tmp, uT], writes=[ov])

        superblock("P")
        superblock("S")
        for ds in final_sems:
            sp.wait(Tok(ds.sem, ds.key, ds.cnt))
        for e in (pe, act, dve):
            sp.wait(Tok(e.sem, e.key, e.cnt))
    return nc


def build_two_pass(cfg):
    rec = []
    build(cfg, plan_tags=None, rec_tags=rec)
    return build(cfg, plan_tags=rec)


def _chunk_layout(w, KC):
    Kd, N = w.shape
    a = w.reshape(KC, 128, N // 128, 128).transpose(2, 1, 0, 3)
    return np.ascontiguousarray(a).reshape((N // 128) * 128, KC * 128)


def _bias_index(cfg):
    GW, rows, WR, WC_ = cfg.GW, cfg.NS // cfg.GW, 8, 16
    idx = np.full((8, 128, 640), 15 * 31, dtype=np.int64)
    for qt in range(8):
        rs0 = min(max(2 * qt - 4, 0), 8)
        rs1 = min(max(2 * qt - 3, 0), 8)
        kt0, kt1 = rs0 // 2, (rs1 + 7) // 2
        for q in range(128):
            r, cq = 2 * qt + q // GW, q % GW
            rstart = min(max(r - WR // 2, 0), rows - WR)
            cstart = min(max(cq - WC_ // 2, 0), GW - WC_)
            for kl in range((kt1 - kt0 + 1) * 128):
                key = kt0 * 128 + kl
                kr, kc_ = key // GW, key % GW
                if rstart <= kr < rstart + WR and cstart <= kc_ < cstart + WC_:
                    dr = kr - r + 7
                    dc = min(max(kc_ - cq, -15), 15) + 15
                    idx[qt, q, kl] = dr * 31 + dc
    return idx


_CACHE = {}


def _consts(cfg):
    if "c" in _CACHE:
        return _CACHE["c"]
    bf = ml_dtypes.bfloat16
    n = np.arange(128)
    ang = 2 * np.pi * np.outer(n, n) / 128.0
    csc = np.concatenate([np.cos(ang), np.sin(ang)], axis=1) / math.sqrt(128.0)

    def dft(N):
        m = np.arange(N)
        a = 2 * np.pi * (np.outer(m, m) % N) / N
        return np.cos(a) / math.sqrt(N), -np.sin(a) / math.sqrt(N)
    c256, s256 = dft(256)
    dftp = np.stack([c256, s256]).reshape(2, 2, 128, 256).transpose(0, 2, 1, 3).reshape(256, 512)
    c1k, s1k = dft(1024)
    d = np.stack([c1k, s1k]).reshape(2, 8, 128, 4, 256).transpose(3, 0, 2, 1, 4).reshape(4 * 2 * 128, 8 * 256)
    out = dict(csc=np.ascontiguousarray(csc).astype(bf), dftp=np.ascontiguousarray(dftp).astype(bf),
               dfts=np.ascontiguousarray(d).astype(bf), bidx=_bias_index(cfg))
    _CACHE["c"] = out
    return out


def prepare(cfg, inp):
    c = cfg
    KC, DEPTH, H, GC = c.KC, c.DEPTH, c.H, c.GC
    f = lambda a: np.ascontiguousarray(np.asarray(a, dtype=np.float32))
    cs = _consts(c)
    shared = {}
    shared["wada"] = np.concatenate([_chunk_layout(f(inp["w_ada"][l]), KC) for l in range(DEPTH)], axis=0)
    shared["bada"] = np.ascontiguousarray(f(inp["b_ada"]).reshape(DEPTH * 6 * KC, 128).T)
    g = np.stack([f(inp["g_pre_mix"]), f(inp["g_post_mix"]), f(inp["g_pre_mlp"]), f(inp["g_post_mlp"])])
    shared["gains"] = np.ascontiguousarray(g.reshape(4 * DEPTH * KC, 128).T)
    shared["win"] = np.concatenate([_chunk_layout(f(inp["w_in"][l]), KC) for l in range(DEPTH)], axis=0)
    wbr = [np.concatenate([f(inp["w_br_a"][l]), f(inp["w_br_f"][l]), f(inp["w_br_c"][l])], axis=0) for l in range(DEPTH)]
    shared["wbr"] = np.concatenate([_chunk_layout(w, KC) for w in wbr], axis=0)
    shared["wout"] = np.concatenate([_chunk_layout(f(inp["w_out"][l]), KC) for l in range(DEPTH)], axis=0)
    shared["w1"] = np.concatenate([_chunk_layout(f(inp["w_mlp1"][l]), KC) for l in range(DEPTH)], axis=0)
    w2 = f(inp["w_mlp2"]).reshape(DEPTH, c.NHC, 8, 128, KC // 2, 256).transpose(0, 1, 4, 3, 2, 5)
    shared["w2"] = np.ascontiguousarray(w2).reshape(DEPTH * c.NHC * (KC // 2) * 128, 8 * 256)
    shared["wspT"] = np.ascontiguousarray(f(inp["w_spatial"]).transpose(0, 1, 3, 2)).reshape(DEPTH * GC * 128, 128)
    shared["bsp"] = np.ascontiguousarray(np.broadcast_to(f(inp["b_spatial"])[:, :, None, None, :], (DEPTH, GC, 128, 4, 128))).reshape(-1, 512)
    shared["gsgu"] = np.ascontiguousarray(np.broadcast_to(f(inp["g_sgu"])[:, :, None, :], (DEPTH, GC, 128, 128))).reshape(-1, 128)
    rpb = f(inp["rpb"]).reshape(DEPTH, H, 15 * 31)
    rpb_ext = np.concatenate([rpb, np.full((DEPTH, H, 1), NEG, np.float32)], axis=2)
    shared["biast"] = np.ascontiguousarray(rpb_ext[:, :, cs["bidx"]]).reshape(DEPTH * H * 8 * 128, 640)
    shared["csc"], shared["dftp"], shared["dfts"] = cs["csc"], cs["dftp"], cs["dfts"]
    xp, xs = f(inp["x_prompt"]), f(inp["x_sample"])
    ck, cv, cc, cctx = f(inp["cache_k"]), f(inp["cache_v"]), f(inp["c"]), f(inp["c_ctx"])
    in_maps = []
    for core in range(c.NCORES):
        b = core // 4
        m = dict(shared)
        m["xp"] = xp[core * c.PB:(core + 1) * c.PB].reshape(c.TB, c.D)
        m["xs"] = xs[b]
        m["ckT"] = np.ascontiguousarray(ck[b].transpose(0, 2, 3, 1)).reshape(DEPTH * H * 128, c.PAST)
        m["cv"] = cv[b].reshape(DEPTH * c.PAST, H * 128)
        m["cvec"] = np.ascontiguousarray(np.stack([cctx.reshape(KC, 128), cc[b].reshape(KC, 128)], axis=2).transpose(1, 0, 2)).reshape(128, KC * 2)
        in_maps.append(m)
    return in_maps


def run(cfg, inp, trace=False):
    in_maps = prepare(cfg, inp)
    nc = build_two_pass(cfg)
    res = run_bass_kernel_spmd(nc, in_maps, core_ids=list(range(cfg.NCORES)), **({"trace": True} if trace else {}))
    c = cfg
    R = res.results
    yp = np.concatenate([R[i]["yp"].reshape(c.PB, c.SEQ, c.D) for i in range(c.NCORES)], axis=0)
    nb = (c.NCORES + 3) // 4
    ys = np.zeros((nb, c.NS, c.D), np.float32)
    for i in range(c.NCORES):
        q = i % 4
        ys[i // 4, q * 256:(q + 1) * 256] = R[i]["ys"][q * 256:(q + 1) * 256]
    nk = np.concatenate([R[i]["nk"].reshape(c.PB, c.DEPTH, c.SEQ, c.H, 128) for i in range(c.NCORES)], axis=0)
    nv = np.concatenate([R[i]["nv"].reshape(c.PB, c.DEPTH, c.SEQ, c.H, 128) for i in range(c.NCORES)], axis=0)
    return (yp.astype(np.float32), ys, nk.astype(np.float32), nv.astype(np.float32)), res


def kernel(**inputs):
    cfg = Cfg()
    outs, _ = run(cfg, inputs)
    return outs
```

```python
import math
import contextlib
import numpy as np
import ml_dtypes
import concourse.bass as bass
import concourse.mybir as mybir
from concourse.bass_utils import run_bass_kernel_spmd

F32 = mybir.dt.float32
BF16 = mybir.dt.bfloat16
AF = mybir.ActivationFunctionType
ALU = mybir.AluOpType
AX = mybir.AxisListType

NEG = -30000.0
EPS = 1e-6


class Cfg:
    def __init__(self, D=2048, H=8, GF=4, GC=4, DFF=8192, DEPTH=4, NCORES=8):
        self.D, self.H, self.GF, self.GC, self.DFF, self.DEPTH, self.NCORES = D, H, GF, GC, DFF, DEPTH, NCORES
        self.SEQ, self.PB, self.NS, self.PAST, self.GW = 256, 4, 1024, 512, 64
        self.KC = D // 128
        self.NO = H + GF + GC
        assert self.NO == self.KC
        self.WA, self.WF, self.WC = H * 128, GF * 128, GC * 128
        self.INW = 3 * self.WA + self.WF + 2 * self.WC + 3 * D
        self.NIN = self.INW // 128
        self.NHC = DFF // 1024
        self.TB = 1024
        self.oq, self.ok, self.ov = 0, H, 2 * H
        self.of = 3 * H
        self.ou = 3 * H + GF
        self.ovs = 3 * H + GF + GC
        self.oga = 3 * H + GF + 2 * GC
        self.ogf = self.oga + self.KC
        self.ogc = self.ogf + self.KC
        self.NMOD = DEPTH * 6 * self.KC


class Tok:
    __slots__ = ("sem", "key", "val")

    def __init__(self, sem, key, val):
        self.sem, self.key, self.val = sem, key, val


class Buf:
    __slots__ = ("name", "w", "r", "excl")

    def __init__(self, name, excl=False):
        self.name, self.w, self.r, self.excl = name, None, {}, excl


class V:
    __slots__ = ("ap", "bufs")

    def __init__(self, ap, bufs):
        self.ap, self.bufs = ap, list(bufs)


class Eng:
    def __init__(self, nc, es, name, handle):
        self.name, self.h = name, handle
        self.sem = es.enter_context(nc.semaphore("pg_" + name))
        self.key = "pg_" + name
        self.cnt = 0
        self.seen = {}

    def wait(self, tok):
        if self.seen.get(tok.key, 0) >= tok.val:
            return
        self.h.wait_ge(tok.sem, tok.val)
        self.seen[tok.key] = tok.val


class DSem:
    def __init__(self, nc, es, name):
        self.sem = es.enter_context(nc.semaphore("d_" + name))
        self.key = "d_" + name
        self.cnt = 0


def _bufs(vs):
    out = []
    for v in vs:
        if isinstance(v, V):
            out.extend(v.bufs)
        elif isinstance(v, Buf):
            out.append(v)
        else:
            out.extend(_bufs(v))
    return out


class K:
    def __init__(self, nc, es):
        self.nc, self.es = nc, es
        self.pe = Eng(nc, es, "pe", nc.tensor)
        self.act = Eng(nc, es, "act", nc.scalar)
        self.dve = Eng(nc, es, "dve", nc.vector)
        self.pool = Eng(nc, es, "pool", nc.gpsimd)
        self.sp = Eng(nc, es, "sp", nc.sync)
        self.nds = 0

    def dsem(self, name):
        self.nds += 1
        return DSem(self.nc, self.es, "%s%d" % (name, self.nds))

    def _deps(self, eng, rb, wb):
        deps = []
        for b in rb:
            if b.w is not None:
                deps.append(b.w)
        for b in wb:
            deps.extend(b.r.values())
            if b.w is not None:
                deps.append(b.w)
        for t in deps:
            if eng is self.pe and t.key == self.pe.key:
                continue
            eng.wait(t)

    def _reg(self, tok, rb, wb):
        for b in rb:
            o = b.r.get(tok.key)
            if o is None or o.val < tok.val:
                b.r[tok.key] = tok
        for b in wb:
            b.w = tok
            b.r = {}

    @staticmethod
    def _split(reads, writes):
        rb, wb = _bufs(reads), _bufs(writes)
        ex = [b for b in rb if b.excl]
        if ex:
            rb = [b for b in rb if not b.excl]
            wb = wb + [b for b in ex if b not in wb]
        return rb, wb

    def op(self, eng, fn, reads=(), writes=()):
        rb, wb = self._split(reads, writes)
        self._deps(eng, rb, wb)
        ins = fn()
        eng.cnt += 1
        ins.then_inc(eng.sem, 1)
        self._reg(Tok(eng.sem, eng.key, eng.cnt), rb, wb)

    def group(self, fns, reads=(), writes=()):
        eng = self.pe
        rb, wb = self._split(reads, writes)
        self._deps(eng, rb, wb)
        ins = None
        for f in fns:
            ins = f()
        eng.cnt += 1
        ins.then_inc(eng.sem, 1)
        self._reg(Tok(eng.sem, eng.key, eng.cnt), rb, wb)

    def dma(self, q, out, in_, ds, reads=(), writes=()):
        rb, wb = _bufs(reads), _bufs(writes)
        self._deps(q, rb, wb)
        ins = q.h.dma_start(out=out, in_=in_)
        ins.then_inc(ds.sem, 16)
        ds.cnt += 16
        self._reg(Tok(ds.sem, ds.key, ds.cnt), rb, wb)


def build(cfg, plan_tags=None, rec_tags=None):
    c = cfg
    D, H, GF, GC, KC, DEPTH, TB = c.D, c.H, c.GF, c.GC, c.KC, c.DEPTH, c.TB
    nc = bass.Bass("TRN2", target_bir_lowering=False)

    def din(name, shape, dt=F32):
        return nc.dram_tensor(name, list(shape), dt, kind="ExternalInput").ap()

    def dout(name, shape, dt=F32):
        return nc.dram_tensor(name, list(shape), dt, kind="ExternalOutput").ap()

    xp_d = din("xp", [TB, D])
    xs_d = din("xs", [TB, D])
    ckT_d = din("ckT", [DEPTH * H * 128, c.PAST])
    cv_d = din("cv", [DEPTH * c.PAST, H * 128])
    cvec_d = din("cvec", [128, KC * 2])
    wada_d = din("wada", [c.NMOD * 128, KC * 128])
    bada_d = din("bada", [128, c.NMOD])
    gains_d = din("gains", [128, 4 * DEPTH * KC])
    win_d = din("win", [DEPTH * c.NIN * 128, KC * 128])
    wbr_d = din("wbr", [DEPTH * KC * 128, KC * 128])
    wout_d = din("wout", [DEPTH * KC * 128, KC * 128])
    w1_d = din("w1", [DEPTH * (c.DFF // 128) * 128, KC * 128])
    w2_d = din("w2", [DEPTH * c.NHC * (KC // 2) * 128, 8 * 256])
    wspT_d = din("wspT", [DEPTH * GC * 128, 128])
    bsp_d = din("bsp", [DEPTH * GC * 128, 512])
    gsgu_d = din("gsgu", [DEPTH * GC * 128, 128])
    bias_d = din("biast", [DEPTH * H * 8 * 128, 640])
    csc_d = din("csc", [128, 256], BF16)
    dftp_d = din("dftp", [2 * 128, 2 * 256], BF16)
    dfts_d = din("dfts", [4 * 2 * 128, 8 * 256], BF16)

    yp_d = dout("yp", [TB, D])
    ys_d = dout("ys", [TB, D])
    nk_d = dout("nk", [c.PB * DEPTH * c.SEQ, H * 128])
    nv_d = dout("nv", [c.PB * DEPTH * c.SEQ, H * 128])

    with contextlib.ExitStack() as es:
        k = K(nc, es)
        pe, act, dve, pool, sp = k.pe, k.act, k.dve, k.pool, k.sp

        def sb(name, shape, dt):
            return es.enter_context(nc.sbuf_tensor("s_" + name, list(shape), dt))

        RB = [sb("rb%d" % i, [128, KC * 1024], BF16) for i in range(3)]
        RBf = [r[:, :].bitcast(F32) for r in RB]
        RU = [[Buf("rb%d_%d" % (i, u)) for u in range(2 * KC)] for i in range(3)]
        xres = sb("xres", [128, KC * 1024], F32)
        XB = [[Buf("x%d_%d" % (j, t)) for t in range(2)] for j in range(KC)]
        NSLOT = 4
        SLOT = max(KC * 128, 8 * 256)
        wb_t = [sb("wb%d" % i, [128, SLOT], BF16) for i in range(NSLOT)]
        wb_b = [Buf("wb%d" % i) for i in range(NSLOT)]
        wb_s = [k.dsem("wb") for _ in range(NSLOT)]
        wb_s2 = [k.dsem("wbh") for _ in range(NSLOT)]
        SCRB = 24576
        scr_t = sb("scr", [128, SCRB // 2], BF16)
        scr_pages = [Buf("scr%d" % i) for i in range(SCRB // 512)]
        modsb = sb("modsb", [128, c.NMOD * 2], F32)
        modb = Buf("modsb")
        gains = sb("gains", [128, 4 * DEPTH * KC], F32)
        gainsb = Buf("gains")
        lvec = sb("lvec", [128, 6 * KC], F32)
        lvb = Buf("lvec")
        identb = sb("identb", [128, 128], BF16)
        identf = sb("identf", [128, 128], F32)
        onesb = sb("onesb", [128, 128], BF16)
        cbuf = Buf("consts")
        csc = sb("csc", [128, 256], BF16)
        bada_p = sb("badap", [128, c.NMOD], F32)
        sT_p = sb("sTp", [128, KC * 2], BF16)
        badab = Buf("badab")
        sTb = Buf("sTb")
        final_sems = []
        stage_s = [k.dsem("stg"), k.dsem("stg")]
        kout_s, vout_s, bias_s, ck_s, cvh_s = k.dsem("ko"), k.dsem("vo"), k.dsem("bi"), k.dsem("ck"), k.dsem("cvh")
        wsp_s, bsp_s, gsg_s = k.dsem("wsp"), k.dsem("bsp"), k.dsem("gsg")
        final_sems += stage_s + [kout_s, vout_s]

        def scr(off, nbytes, dt, shape=None):
            assert off % 4 == 0 and off + nbytes <= SCRB, (off, nbytes)
            ap = scr_t[:, off // 2:(off + nbytes) // 2]
            if dt == F32:
                ap = ap.bitcast(F32)
            if shape is not None:
                ap = ap.rearrange("p (a b) -> p a b", a=shape[0])
            return V(ap, scr_pages[off // 512:(off + nbytes + 511) // 512])

        def sub(v, ap):
            return V(ap, v.bufs)

        psS = es.enter_context(nc.psum_tensor("psS", [128, 1536], F32))
        singles = [es.enter_context(nc.psum_tensor("ps%d" % i, [128, 512], F32)) for i in range(5)]
        bank_ap = [psS[:, i * 512:(i + 1) * 512] for i in range(3)] + [t[:, :] for t in singles]
        bank_b = [Buf("ps%d" % i, excl=True) for i in range(8)]
        rot = [0, 0]

        def bank():
            i = rot[0] % 5
            rot[0] += 1
            return V(bank_ap[i], [bank_b[i]])

        def bank2():
            i = 3 + rot[1] % 2
            rot[1] += 1
            return V(bank_ap[i], [bank_b[i]])

        def hbank():
            return V(bank_ap[5], [bank_b[5]])

        def sbank(t):
            return V(bank_ap[6 + t], [bank_b[6 + t]])

        def hch(i, j, t0=0, t1=1024):
            return V(RB[i][:, j * 1024 + t0:j * 1024 + t1], RU[i][2 * j + t0 // 512:2 * j + (t1 - 1) // 512 + 1])

        def fch(pair, jj, tb):
            half = KC // 2
            b = pair[jj // half]
            loc = jj % half
            return V(RBf[b][:, loc * 1024 + tb * 512:loc * 1024 + tb * 512 + 512], RU[b][4 * loc + 2 * tb:4 * loc + 2 * tb + 2])

        def xch(j, tb):
            return V(xres[:, j * 1024 + tb * 512:j * 1024 + tb * 512 + 512], [XB[j][tb]])

        wq = []
        wstate = {"n": 0, "plan": iter(plan_tags) if plan_tags is not None else None}

        def src_of(tag):
            kind = tag[0]
            if kind == "ada":
                idx = tag[1]
                return wada_d[idx * 128:(idx + 1) * 128, :], KC * 128, True
            sbk, l = tag[1], tag[2]

            def win(ch):
                r0 = (l * c.NIN + ch) * 128
                return win_d[r0:r0 + 128, :], KC * 128, True
            if kind == "q":
                return win(c.oq + tag[3])
            if kind == "k":
                return win(c.ok + tag[3])
            if kind == "v":
                return win(c.ov + tag[3])
            if kind == "f":
                return win(c.of + tag[3])
            if kind == "u":
                return win(c.ou + tag[3])
            if kind == "vs":
                return win(c.ovs + tag[3])
            if kind == "ga":
                return win(c.oga + tag[3])
            if kind == "gf":
                return win(c.ogf + tag[3])
            if kind == "gc":
                return win(c.ogc + tag[3])
            if kind in ("dftc", "dfts"):
                cs_ = 0 if kind == "dftc" else 1
                if sbk == "P":
                    return dftp_d[cs_ * 128:(cs_ + 1) * 128, :], 512, False
                nb = tag[4]
                return dfts_d[(nb * 2 + cs_) * 128:(nb * 2 + cs_ + 1) * 128, :], 2048, False
            if kind == "br":
                r0 = (l * KC + tag[3]) * 128
                return wbr_d[r0:r0 + 128, :], KC * 128, True
            if kind == "wo":
                r0 = (l * KC + tag[3]) * 128
                return wout_d[r0:r0 + 128, :], KC * 128, True
            if kind == "w1":
                r0 = (l * (c.DFF // 128) + tag[3] * 8 + tag[4]) * 128
                return w1_d[r0:r0 + 128, :], KC * 128, True
            if kind == "w2":
                r0 = ((l * c.NHC + tag[3]) * (KC // 2) + tag[4]) * 128
                return w2_d[r0:r0 + 128, :], 2048, True
            raise KeyError(tag)

        def wissue():
            try:
                tag = next(wstate["plan"])
            except StopIteration:
                return False
            src, n, cast = src_of(tag)
            i = wstate["n"] % NSLOT
            wstate["n"] += 1
            q = pool if cast else sp
            k.dma(q, wb_t[i][:, 0:n], src, (wb_s if cast else wb_s2)[i], writes=[wb_b[i]])
            wq.append((tag, V(wb_t[i][:, 0:n], [wb_b[i]])))
            return True

        def wget(tag):
            if plan_tags is None:
                rec_tags.append(tag)
                n = src_of(tag)[1]
                return V(wb_t[0][:, 0:n], [wb_b[0]])
            while len(wq) < NSLOT - 1:
                if not wissue():
                    break
            t, v = wq.pop(0)
            assert t == tag, (t, tag)
            return v

        def wrefill():
            if plan_tags is None:
                return
            while len(wq) < NSLOT - 1:
                if not wissue():
                    break

        k.op(pool, lambda: nc.gpsimd.memset(identb[:, :], 1.0), writes=[cbuf])
        k.op(pool, lambda: nc.gpsimd.affine_select(out=identb[:, :], in_=identb[:, :], pattern=[[-1, 128]],
                                                   compare_op=ALU.is_equal, fill=0.0, base=0, channel_multiplier=1),
             reads=[cbuf], writes=[cbuf])
        k.op(pool, lambda: nc.gpsimd.memset(identf[:, :], 1.0), writes=[cbuf])
        k.op(pool, lambda: nc.gpsimd.affine_select(out=identf[:, :], in_=identf[:, :], pattern=[[-1, 128]],
                                                   compare_op=ALU.is_equal, fill=0.0, base=0, channel_multiplier=1),
             reads=[cbuf], writes=[cbuf])
        k.op(pool, lambda: nc.gpsimd.memset(onesb[:, :], 1.0), writes=[cbuf])
        k.dma(sp, csc[:, :], csc_d[:, :], k.dsem("m"), writes=[cbuf])
        k.dma(sp, gains[:, :], gains_d[:, :], k.dsem("m"), writes=[gainsb])
        CB = [cbuf]

        cv_f = scr(0, KC * 2 * 4, F32)
        k.dma(sp, cv_f.ap, cvec_d[:, :], k.dsem("m"), writes=[cv_f])
        k.dma(sp, bada_p[:, :], bada_d[:, :], k.dsem("m"), writes=[badab])
        k.op(act, lambda: nc.scalar.activation(out=sT_p[:, :], in_=cv_f.ap, func=AF.Silu), reads=[cv_f], writes=[sTb])
        mod3 = modsb[:, :].rearrange("p (i v) -> p i v", v=2)

        def ada_chunk(idx):
            w = wget(("ada", idx))
            bk = bank()
            fns = [lambda kc=kc: nc.tensor.matmul(bk.ap[:, 0:2], lhsT=w.ap[:, kc * 128:(kc + 1) * 128],
                                                  rhs=sT_p[:, kc * 2:kc * 2 + 2], start=(kc == 0), stop=(kc == KC - 1))
                   for kc in range(KC)]
            k.group(fns, reads=[w, sTb], writes=[bk])
            k.op(dve, lambda: nc.vector.tensor_scalar(out=mod3[:, idx, :], in0=bk.ap[:, 0:2], scalar1=bada_p[:, idx:idx + 1],
                                                      scalar2=None, op0=ALU.add), reads=[bk, badab], writes=[modb])
            wrefill()

        N0 = 6 * KC
        for idx in range(N0):
            ada_chunk(idx)
        pending_ada = list(range(N0, c.NMOD))
        ada_on = [False]

        def ada_tick():
            if not ada_on[0] or not pending_ada:
                return
            n = 2 if len(pending_ada) > 200 else 1
            for _ in range(n):
                if pending_ada:
                    ada_chunk(pending_ada.pop(0))

        def ada_drain():
            while pending_ada:
                ada_chunk(pending_ada.pop(0))

        def modv(l, kind, v):
            i0 = (l * 6 + kind) * KC
            return modsb[:, :].rearrange("p (i v) -> p i v", v=2)[:, i0:i0 + KC, v]

        def gainv(kind, l):
            i0 = (kind * DEPTH + l) * KC
            return gains[:, i0:i0 + KC]

        def lv(i):
            return lvec[:, i * KC:(i + 1) * KC]

        def layer_vectors(l, v):
            rd, wr = [modb, gainsb], [lvb]
            k.op(dve, lambda: nc.vector.scalar_tensor_tensor(out=lv(0), in0=modv(l, 1, v), scalar=1.0, in1=gainv(0, l),
                                                             op0=ALU.add, op1=ALU.mult), reads=rd, writes=wr)
            k.op(dve, lambda: nc.vector.tensor_copy(out=lv(1), in_=modv(l, 0, v)), reads=rd, writes=wr)
            k.op(dve, lambda: nc.vector.tensor_tensor(out=lv(2), in0=modv(l, 2, v), in1=gainv(1, l), op=ALU.mult),
                 reads=rd, writes=wr)
            k.op(dve, lambda: nc.vector.scalar_tensor_tensor(out=lv(3), in0=modv(l, 4, v), scalar=1.0, in1=gainv(2, l),
                                                             op0=ALU.add, op1=ALU.mult), reads=rd, writes=wr)
            k.op(dve, lambda: nc.vector.tensor_copy(out=lv(4), in_=modv(l, 3, v)), reads=rd, writes=wr)
            k.op(dve, lambda: nc.vector.tensor_tensor(out=lv(5), in0=modv(l, 5, v), in1=gainv(3, l), op=ALU.mult),
                 reads=rd, writes=wr)

        def rstd_from(ssb, dst):
            k.op(dve, lambda: nc.vector.tensor_scalar(out=dst.ap, in0=ssb.ap, scalar1=1.0 / D, scalar2=EPS,
                                                      op0=ALU.mult, op1=ALU.add), reads=[ssb], writes=[dst])
            k.op(act, lambda: nc.scalar.activation(out=dst.ap, in_=dst.ap, func=AF.Sqrt), reads=[dst], writes=[dst])
            k.op(dve, lambda: nc.vector.reciprocal(out=dst.ap, in_=dst.ap), reads=[dst], writes=[dst])

        def stat_acc(src_v, tb, first, last, sqi):
            sq = scr(16384 + (sqi % 2) * 1024, 1024, BF16)
            k.op(act, lambda: nc.scalar.activation(out=sq.ap, in_=src_v.ap, func=AF.Square), reads=[src_v], writes=[sq])
            ssb = sbank(tb)
            k.group([lambda: nc.tensor.matmul(ssb.ap, lhsT=onesb[:, :], rhs=sq.ap, start=first, stop=last)],
                    reads=[sq] + CB, writes=[ssb])

        def norm_to_h(ih, gi, si, tb):
            rs = scr(tb * 2048, 2048, F32)
            rstd_from(sbank(tb), rs)
            for j in range(KC):
                tmp = scr(4096 + (j % 2) * 2048, 2048, F32)
                xv = xch(j, tb)
                k.op(dve, lambda: nc.vector.tensor_tensor(out=tmp.ap, in0=xv.ap, in1=rs.ap, op=ALU.mult),
                     reads=[xv, rs], writes=[tmp])
                hv = hch(ih, j, tb * 512, tb * 512 + 512)
                k.op(act, lambda: nc.scalar.activation(out=hv.ap, in_=tmp.ap, func=AF.Identity,
                                                       scale=lv(gi)[:, j:j + 1], bias=lv(si)[:, j:j + 1]),
                     reads=[tmp, lvb], writes=[hv])

        def sublayer_finish(pair, gti, nxt):
            for tb in range(2):
                rs = scr(tb * 2048, 2048, F32)
                rstd_from(sbank(tb), rs)
                for j in range(KC):
                    tmp = scr(4096 + (j % 2) * 2048, 2048, F32)
                    rv = fch(pair, j, tb)
                    xv = xch(j, tb)
                    k.op(dve, lambda: nc.vector.tensor_tensor(out=tmp.ap, in0=rv.ap, in1=rs.ap, op=ALU.mult),
                         reads=[rv, rs], writes=[tmp])
                    k.op(dve, lambda: nc.vector.scalar_tensor_tensor(out=xv.ap, in0=tmp.ap, scalar=lv(gti)[:, j:j + 1],
                                                                     in1=xv.ap, op0=ALU.mult, op1=ALU.add),
                         reads=[tmp, xv, lvb], writes=[xv])
                    if nxt is not None:
                        stat_acc(xv, tb, j == 0, j == KC - 1, j)
                if nxt is not None:
                    norm_to_h(nxt[0], nxt[1], nxt[2], tb)

        def proj_groups(w, ih):
            res = []
            for tb in range(2):
                bk = bank()
                hs = [hch(ih, kc, tb * 512, tb * 512 + 512) for kc in range(KC)]
                fns = [lambda kc=kc, bk=bk, hs=hs: nc.tensor.matmul(
                    bk.ap, lhsT=w.ap[:, kc * 128:(kc + 1) * 128], rhs=hs[kc].ap,
                    start=(kc == 0), stop=(kc == KC - 1)) for kc in range(KC)]
                k.group(fns, reads=[w] + hs, writes=[bk])
                res.append((tb, bk))
            wrefill()
            return res

        def proj_fm(w, ih, evac):
            for tb in range(2):
                bk = bank()
                hs = [hch(ih, kc, tb * 512, tb * 512 + 512) for kc in range(KC)]
                fns = [lambda kc=kc, bk=bk, hs=hs: nc.tensor.matmul(
                    bk.ap, lhsT=w.ap[:, kc * 128:(kc + 1) * 128], rhs=hs[kc].ap,
                    start=(kc == 0), stop=(kc == KC - 1)) for kc in range(KC)]
                k.group(fns, reads=[w] + hs, writes=[bk])
                evac(tb, bk)
            wrefill()
            ada_tick()

        small = scr(22016, 512, F32)

        def superblock(sbk):
            isP = sbk == "P"
            vsel = 0 if isP else 1
            x_d = xp_d if isP else xs_d
            y_d = yp_d if isP else ys_d
            for t in range(8):
                for fb in range(D // 512):
                    si = (t * (D // 512) + fb) % 2
                    st = scr(si * 2048, 2048, F32)
                    k.dma(sp, st.ap, x_d[t * 128:(t + 1) * 128, fb * 512:(fb + 1) * 512], stage_s[si], writes=[st])
                    bk = bank()
                    k.group([lambda i=i, bk=bk, st=st: nc.tensor.transpose(
                        bk.ap[:, i * 128:(i + 1) * 128], st.ap[:, i * 128:(i + 1) * 128], identf[:, :])
                        for i in range(4)], reads=[st] + CB, writes=[bk])
                    for i in range(4):
                        j = fb * 4 + i
                        tb, tt = t // 4, t % 4
                        xv = xch(j, tb)
                        k.op(dve if i % 2 == 0 else act,
                             (lambda i=i, bk=bk, xv=xv, tt=tt: nc.vector.tensor_copy(
                                 out=xv.ap[:, tt * 128:(tt + 1) * 128], in_=bk.ap[:, i * 128:(i + 1) * 128]))
                             if i % 2 == 0 else
                             (lambda i=i, bk=bk, xv=xv, tt=tt: nc.scalar.copy(
                                 out=xv.ap[:, tt * 128:(tt + 1) * 128], in_=bk.ap[:, i * 128:(i + 1) * 128])),
                             reads=[bk], writes=[xv])
            ih, io, im = 0, 1, 2
            layer_vectors(0, vsel)
            ada_on[0] = True
            for tb in range(2):
                for j in range(KC):
                    stat_acc(xch(j, tb), tb, j == 0, j == KC - 1, j)
                norm_to_h(ih, 0, 1, tb)

            for l in range(DEPTH):
                qT = scr(0, 2048, BF16)
                kT = scr(2048, 2048, BF16)
                carry = None
                for h in range(H):
                    oh = h
                    if isP:
                        carry = prompt_head(l, h, ih, io, oh, qT, kT, carry)
                    else:
                        carry = sample_head(l, h, ih, io, oh, qT, kT, carry)
                for g in range(GF):
                    fourier_group(sbk, l, g, ih, io, H + g)
                for g in range(GC):
                    sgu_group(sbk, l, g, ih, io, H + GF + g)
                for j in range(KC):
                    sg = [scr(i * 2048, 2048, BF16) for i in range(3)]

                    def ev_gate(gi):
                        def ev(tb, bk, gi=gi):
                            sv = sub(sg[gi], sg[gi].ap[:, tb * 512:(tb + 1) * 512])
                            k.op(act, lambda: nc.scalar.activation(out=sv.ap, in_=bk.ap, func=AF.Sigmoid),
                                 reads=[bk], writes=[sv])
                        return ev
                    proj_fm(wget(("ga", sbk, l, j)), ih, ev_gate(0))
                    proj_fm(wget(("gf", sbk, l, j)), ih, ev_gate(1))
                    proj_fm(wget(("gc", sbk, l, j)), ih, ev_gate(2))
                    wbr = wget(("br", sbk, l, j))
                    for tb in range(2):
                        acc = scr(6144, 2048, F32)
                        ranges = [(0, H), (H, H + GF), (H + GF, KC)]
                        for gi, (a0, a1) in enumerate(ranges):
                            bk = bank()
                            os_ = [hch(io, kc, tb * 512, tb * 512 + 512) for kc in range(a0, a1)]
                            fns = [lambda n=n, kc=kc, bk=bk, os_=os_, a0=a0, a1=a1: nc.tensor.matmul(
                                bk.ap, lhsT=wbr.ap[:, kc * 128:(kc + 1) * 128], rhs=os_[n].ap,
                                start=(kc == a0), stop=(kc == a1 - 1)) for n, kc in enumerate(range(a0, a1))]
                            k.group(fns, reads=[wbr] + os_, writes=[bk])
                            mv = hch(im, j, tb * 512, tb * 512 + 512)
                            sv = sub(sg[gi], sg[gi].ap[:, tb * 512:(tb + 1) * 512])
                            if gi == 0:
                                k.op(dve, lambda: nc.vector.tensor_tensor(
                                    out=acc.ap, in0=bk.ap, in1=sv.ap, op=ALU.mult), reads=[bk, sv], writes=[acc])
                            else:
                                t2 = scr(8192, 2048, F32)
                                k.op(dve, lambda: nc.vector.tensor_tensor(
                                    out=t2.ap, in0=bk.ap, in1=sv.ap, op=ALU.mult), reads=[bk, sv], writes=[t2])
                                dst = acc if gi == 1 else mv
                                k.op(dve, lambda: nc.vector.tensor_tensor(
                                    out=dst.ap, in0=t2.ap, in1=acc.ap, op=ALU.add), reads=[t2, acc], writes=[dst])
                    wrefill()
                pair = (ih, io)
                for j in range(KC):
                    w = wget(("wo", sbk, l, j))
                    for tb in range(2):
                        bk = bank()
                        ms = [hch(im, kc, tb * 512, tb * 512 + 512) for kc in range(KC)]
                        fns = [lambda kc=kc, bk=bk, ms=ms, w=w: nc.tensor.matmul(
                            bk.ap, lhsT=w.ap[:, kc * 128:(kc + 1) * 128], rhs=ms[kc].ap,
                            start=(kc == 0), stop=(kc == KC - 1)) for kc in range(KC)]
                        k.group(fns, reads=[w] + ms, writes=[bk])
                        rv = fch(pair, j, tb)
                        k.op(dve, lambda bk=bk, rv=rv: nc.vector.tensor_copy(out=rv.ap, in_=bk.ap), reads=[bk], writes=[rv])
                        stat_acc(bk, tb, j == 0, j == KC - 1, j)
                    wrefill()
                sublayer_finish(pair, 2, (im, 3, 4))
                for hc in range(c.NHC):
                    for fc in range(8):
                        w = wget(("w1", sbk, l, hc, fc))

                        def ev_a(tb, bk, fc=fc):
                            av = scr(fc * 2048 + tb * 1024, 1024, BF16)
                            k.op(act, lambda: nc.scalar.activation(out=av.ap, in_=bk.ap, func=AF.Relu), reads=[bk], writes=[av])
                            k.op(dve, lambda: nc.vector.tensor_tensor(out=av.ap, in0=av.ap, in1=av.ap, op=ALU.mult),
                                 reads=[av], writes=[av])
                        proj_fm(w, im, ev_a)
                    for jb in range(KC // 2):
                        w = wget(("w2", sbk, l, hc, jb))
                        for jj in range(2):
                            j = jb * 2 + jj
                            for tb in range(2):
                                bk = bank()
                                as_ = [scr(fc * 2048 + tb * 1024, 1024, BF16) for fc in range(8)]
                                fns = [lambda fc=fc, bk=bk, as_=as_, w=w, jj=jj: nc.tensor.matmul(
                                    bk.ap, lhsT=w.ap[:, fc * 256 + jj * 128:fc * 256 + jj * 128 + 128], rhs=as_[fc].ap,
                                    start=(fc == 0), stop=(fc == 7)) for fc in range(8)]
                                k.group(fns, reads=[w] + as_, writes=[bk])
                                fv = fch(pair, j, tb)
                                if hc == 0:
                                    k.op(dve, lambda bk=bk, fv=fv: nc.vector.tensor_copy(out=fv.ap, in_=bk.ap),
                                         reads=[bk], writes=[fv])
                                else:
                                    k.op(dve, lambda bk=bk, fv=fv: nc.vector.tensor_tensor(
                                        out=fv.ap, in0=bk.ap, in1=fv.ap, op=ALU.add), reads=[bk, fv], writes=[fv])
                                if hc == c.NHC - 1:
                                    stat_acc(fv, tb, j == 0, j == KC - 1, j)
                        wrefill()
                sublayer_finish(pair, 5, None)
                if l + 1 < DEPTH:
                    ada_drain()
                    layer_vectors(l + 1, vsel)
                    for tb in range(2):
                        for j in range(KC):
                            stat_acc(xch(j, tb), tb, j == 0, j == KC - 1, j)
                        norm_to_h(im, 0, 1, tb)
                    ih, io, im = im, ih, io
            outs = []
            for t in range(8):
                tb, tt = t // 4, t % 4
                for fb in range(D // 512):
                    bk = bank()
                    xs_ = [xch(fb * 4 + i, tb) for i in range(4)]
                    k.group([lambda i=i, bk=bk, xs_=xs_, tt=tt: nc.tensor.transpose(
                        bk.ap[:, i * 128:(i + 1) * 128], xs_[i].ap[:, tt * 128:(tt + 1) * 128], identf[:, :])
                        for i in range(4)], reads=xs_ + CB, writes=[bk])
                    si = (t * (D // 512) + fb) % 2
                    st = scr(si * 2048, 2048, F32)
                    k.op(act, lambda bk=bk, st=st: nc.scalar.copy(out=st.ap, in_=bk.ap), reads=[bk], writes=[st])
                    k.dma(sp, y_d[t * 128:(t + 1) * 128, fb * 512:(fb + 1) * 512], st.ap, stage_s[si], reads=[st])

        def prompt_head(l, h, ih, io, oh, qT, kT, carry):
            kf = scr(4096, 2048, F32)
            vf = scr(6144, 2048, F32)
            vtm = scr(8192, 2048, BF16, (8, 128))
            kout = scr(10240, 2048, F32, (4, 128))
            vout = scr(12288, 2048, F32, (4, 128))
            Pm = scr(14336, 1024, BF16, (2, 256))
            PT = scr(15360, 1024, BF16, (4, 128))
            scale = 1.0 / math.sqrt(128.0)

            def ev_q(tb, bk):
                qv = sub(qT, qT.ap[:, tb * 512:(tb + 1) * 512])
                k.op(act, lambda: nc.scalar.activation(out=qv.ap, in_=bk.ap, func=AF.Copy, scale=scale),
                     reads=[bk], writes=[qv])
            if carry is None:
                gq = proj_groups(wget(("q", "P", l, h)), ih)
                gk = proj_groups(wget(("k", "P", l, h)), ih)
            else:
                gq, gk = carry
            for tb, bk in gq:
                ev_q(tb, bk)

            def kv_out(tb, srcf, outst, dst_d, with_tm, kv_s):
                bk = bank()
                k.group([lambda i=i, bk=bk: nc.tensor.transpose(
                    bk.ap[:, i * 128:(i + 1) * 128], srcf.ap[:, i * 128:(i + 1) * 128], identf[:, :])
                    for i in range(4)], reads=[srcf] + CB, writes=[bk])
                k.op(act, lambda: nc.scalar.copy(out=outst.ap, in_=bk.ap.rearrange("p (a b) -> p a b", a=4)),
                     reads=[bk], writes=[outst])
                if with_tm:
                    tv = sub(vtm, vtm.ap[:, tb * 4:(tb + 1) * 4, :])
                    k.op(dve, lambda: nc.vector.tensor_copy(out=tv.ap, in_=bk.ap.rearrange("p (a b) -> p a b", a=4)),
                         reads=[bk], writes=[tv])
                for bb in range(2):
                    b = tb * 2 + bb
                    r0 = (b * DEPTH + l) * c.SEQ
                    dst = dst_d[r0:r0 + 256, h * 128:(h + 1) * 128].rearrange("(a p) d -> p a d", p=128)
                    k.dma(sp, dst, outst.ap[:, bb * 2:bb * 2 + 2, :], kv_s, reads=[outst])

            kst = [kf, vf]
            for tb, bk in gk:
                kv = sub(kT, kT.ap[:, tb * 512:(tb + 1) * 512])
                k.op(dve, lambda: nc.vector.tensor_copy(out=kv.ap, in_=bk.ap), reads=[bk], writes=[kv])
                k.op(act, lambda: nc.scalar.copy(out=kst[tb].ap, in_=bk.ap), reads=[bk], writes=[kst[tb]])
            for tb, bk in gk:
                kv_out(tb, kst[tb], kout, nk_d, False, kout_s)

            def ev_v(tb, bk):
                k.op(act, lambda: nc.scalar.copy(out=vf.ap, in_=bk.ap), reads=[bk], writes=[vf])
                kv_out(tb, vf, vout, nv_d, True, vout_s)
            proj_fm(wget(("v", "P", l, h)), ih, ev_v)

            Sb = {}

            def pbufs(b):
                Pm = scr([14336, 16384, 18432][b % 3], 1024, BF16, (2, 256))
                PT = scr(15360 + (b % 2) * 2048, 1024, BF16, (4, 128))
                sm = scr(22016 + (b % 2) * 512, 512, F32)
                return Pm, PT, sm

            def p_S(b):
                t0 = b * 256
                sbk_ = bank()
                Sb[b] = sbk_
                for qt in range(2):
                    k.group([lambda qt=qt: nc.tensor.matmul(
                        sbk_.ap[:, qt * 256:(qt + 1) * 256], lhsT=qT.ap[:, t0 + qt * 128:t0 + (qt + 1) * 128],
                        rhs=kT.ap[:, t0:t0 + 256], start=True, stop=True)], reads=[qT, kT], writes=[sbk_])

            def p_a(b):
                Pm, PT, sm = pbufs(b)
                sbk_ = Sb[b]
                mx, nmx, rsum = sm.ap[:, 0:2], sm.ap[:, 2:4], sm.ap[:, 4:6]
                k.op(dve, lambda: nc.vector.reduce_max(out=mx, in_=sbk_.ap.rearrange("p (a b) -> p a b", a=2), axis=AX.X),
                     reads=[sbk_], writes=[sm])
                k.op(dve, lambda: nc.vector.tensor_scalar(out=nmx, in0=mx, scalar1=-1.0, scalar2=None, op0=ALU.mult),
                     reads=[sm], writes=[sm])
                k.op(dve, lambda: nc.vector.memset(rsum, 0.0), writes=[sm])
                for qt in range(2):
                    k.op(act, lambda qt=qt: nc.scalar.activation(
                        out=Pm.ap[:, qt, :], in_=sbk_.ap[:, qt * 256:(qt + 1) * 256], func=AF.Exp,
                        bias=nmx[:, qt:qt + 1], scale=1.0, accum_out=rsum[:, qt:qt + 1]),
                        reads=[sbk_, sm], writes=[Pm, sm])

            def p_b(b):
                Pm, PT, sm = pbufs(b)
                rsum, rrec = sm.ap[:, 4:6], sm.ap[:, 6:8]
                k.op(dve, lambda: nc.vector.reciprocal(out=rrec, in_=rsum), reads=[sm], writes=[sm])
                for qt in range(2):
                    k.op(dve, lambda qt=qt: nc.vector.tensor_scalar(
                        out=Pm.ap[:, qt, :], in0=Pm.ap[:, qt, :], scalar1=rrec[:, qt:qt + 1], scalar2=None,
                        op0=ALU.mult), reads=[Pm, sm], writes=[Pm])

            def p_T(b):
                Pm, PT, sm = pbufs(b)
                tbk = V(bank_ap[7], [bank_b[7]])
                tb16 = tbk.ap.bitcast(BF16)
                fns = []
                for kt in range(2):
                    for qt in range(2):
                        fns.append(lambda kt=kt, qt=qt: nc.tensor.transpose(
                            tb16[:, (kt * 2 + qt) * 128:(kt * 2 + qt + 1) * 128],
                            Pm.ap[:, qt, kt * 128:(kt + 1) * 128], identb[:, :]))
                k.group(fns, reads=[Pm] + CB, writes=[tbk])
                k.op(act, lambda: nc.scalar.copy(out=PT.ap, in_=tb16[:, 0:512].rearrange("p (a b) -> p a b", a=4)),
                     reads=[tbk], writes=[PT])

            def p_O(b):
                Pm, PT, sm = pbufs(b)
                t0 = b * 256
                obk = V(bank_ap[6], [bank_b[6]])
                k.group([lambda kt=kt: nc.tensor.matmul(
                    obk.ap[:, 0:256], lhsT=vtm.ap[:, b * 2 + kt, :],
                    rhs=PT.ap[:, kt * 2:kt * 2 + 2, :].rearrange("p a b -> p (a b)"),
                    start=(kt == 0), stop=(kt == 1)) for kt in range(2)], reads=[vtm, PT], writes=[obk])
                ov = hch(io, oh, t0, t0 + 256)
                k.op(act, lambda: nc.scalar.copy(out=ov.ap, in_=obk.ap[:, 0:256]), reads=[obk], writes=[ov])

            nxt = None
            p_S(0)
            p_S(1)
            p_a(0)
            for i in range(c.PB + 3):
                if 0 <= i - 2 < c.PB:
                    p_T(i - 2)
                if i + 2 < c.PB:
                    p_S(i + 2)
                if i + 1 < c.PB:
                    p_a(i + 1)
                    if i + 1 == c.PB - 1 and h + 1 < H:
                        nxt = (proj_groups(wget(("q", "P", l, h + 1)), ih), proj_groups(wget(("k", "P", l, h + 1)), ih))
                if i < c.PB:
                    p_b(i)
                if 0 <= i - 3 < c.PB:
                    p_O(i - 3)
            return nxt

        def sample_head(l, h, ih, io, oh, qT, kT, carry):
            vT = scr(4096, 2048, BF16)
            vtm = scr(6656, 2048, BF16, (8, 128))
            ckT = scr(8704, 1024, BF16)
            cvh = scr(9728, 1024, BF16, (4, 128))
            bias = scr(10752, 2560, F32)
            scale = 1.0 / math.sqrt(128.0)
            def ck_load(hh):
                r0 = (l * H + hh) * 128
                k.dma(pool, ckT.ap, ckT_d[r0:r0 + 128, :], ck_s, writes=[ckT])
            if carry is None:
                ck_load(h)
                gq = proj_groups(wget(("q", "S", l, h)), ih)
                gk = proj_groups(wget(("k", "S", l, h)), ih)
            else:
                gq, gk = carry
            k.dma(pool, cvh.ap, cv_d[l * c.PAST:(l + 1) * c.PAST, h * 128:(h + 1) * 128].rearrange(
                "(a p) d -> p a d", p=128), cvh_s, writes=[cvh])

            def ev_q(tb, bk):
                qv = sub(qT, qT.ap[:, tb * 512:(tb + 1) * 512])
                k.op(act, lambda: nc.scalar.activation(out=qv.ap, in_=bk.ap, func=AF.Copy, scale=scale),
                     reads=[bk], writes=[qv])
            for tb, bk in gq:
                ev_q(tb, bk)

            def ev_k(tb, bk):
                kv = sub(kT, kT.ap[:, tb * 512:(tb + 1) * 512])
                k.op(dve, lambda: nc.vector.tensor_copy(out=kv.ap, in_=bk.ap), reads=[bk], writes=[kv])
            for tb, bk in gk:
                ev_k(tb, bk)

            def ev_v(tb, bk):
                vv = sub(vT, vT.ap[:, tb * 512:(tb + 1) * 512])
                k.op(act, lambda: nc.scalar.copy(out=vv.ap, in_=bk.ap), reads=[bk], writes=[vv])
            proj_fm(wget(("v", "S", l, h)), ih, ev_v)
            tbk = bank()
            tb16 = tbk.ap.bitcast(BF16)
            k.group([lambda i=i: nc.tensor.transpose(tb16[:, i * 128:(i + 1) * 128], vT.ap[:, i * 128:(i + 1) * 128],
                                                     identb[:, :]) for i in range(8)], reads=[vT] + CB, writes=[tbk])
            k.op(dve, lambda: nc.vector.tensor_copy(out=vtm.ap, in_=tb16.rearrange("p (a b) -> p a b", a=8)),
                 reads=[tbk], writes=[vtm])
            def geom(qt):
                rs0 = min(max(2 * qt - 4, 0), 8)
                rs1 = min(max(2 * qt - 3, 0), 8)
                kt0, kt1 = rs0 // 2, (rs1 + 7) // 2
                nkt = kt1 - kt0 + 1
                return kt0, nkt

            def bufs(qt):
                par = qt % 2
                RBK = [3 * par, 3 * par + 1, 3 * par + 2]

                def sap(c0, n):
                    bi = c0 // 512
                    return bank_ap[RBK[bi]][:, c0 - 512 * bi:c0 - 512 * bi + n]
                Pm = scr([13312, 15872, 4096][qt % 3], 2304, BF16)
                PT = scr(18432 + par * 2560, 2304, BF16, (9, 128))
                smv = scr(23552 + par * 512, 512, F32)
                return RBK, sap, Pm, PT, smv

            def bias_load(qt):
                br0 = ((l * H + h) * 8 + qt) * 128
                k.dma(sp, bias.ap, bias_d[br0:br0 + 128, :], bias_s, writes=[bias])

            def stage_S(qt):
                kt0, nkt = geom(qt)
                RBK, sap, Pm, PT, smv = bufs(qt)
                Sreg = [bank_b[i] for i in RBK]
                qv = qT.ap[:, qt * 128:(qt + 1) * 128]
                segs = [(0, 512, kT.ap[:, kt0 * 128:kt0 * 128 + 512])]
                if nkt == 5:
                    segs.append((512, 128, kT.ap[:, kt0 * 128 + 512:kt0 * 128 + 640]))
                    segs.append((640, 384, ckT.ap[:, 0:384]))
                    segs.append((1024, 128, ckT.ap[:, 384:512]))
                else:
                    segs.append((512, 512, ckT.ap[:, 0:512]))
                k.group([lambda c0=c0, n=n, rhs=rhs: nc.tensor.matmul(sap(c0, n), lhsT=qv, rhs=rhs, start=True, stop=True)
                         for (c0, n, rhs) in segs], reads=[qT, kT, ckT], writes=Sreg)

            def stage_a(qt):
                kt0, nkt = geom(qt)
                RBK, sap, Pm, PT, smv = bufs(qt)
                m = smv.ap
                bsegs = [(0, 0, 512), (1, 512, 512)] + ([(2, 1024, 128)] if nkt == 5 else [])
                k.op(dve, lambda: nc.vector.tensor_tensor(out=sap(0, 512), in0=sap(0, 512), in1=bias.ap[:, 0:512], op=ALU.add),
                     reads=[bank_b[RBK[0]], bias], writes=[bank_b[RBK[0]]])
                if nkt == 5:
                    k.op(dve, lambda: nc.vector.tensor_tensor(out=sap(512, 128), in0=sap(512, 128), in1=bias.ap[:, 512:640],
                                                              op=ALU.add), reads=[bank_b[RBK[1]], bias], writes=[bank_b[RBK[1]]])
                if qt + 1 < 8:
                    bias_load(qt + 1)
                for i, (bi, c0, n) in enumerate(bsegs):
                    k.op(dve, lambda i=i, c0=c0, n=n: nc.vector.reduce_max(out=m[:, 8 + i:9 + i], in_=sap(c0, n), axis=AX.X),
                         reads=[bank_b[RBK[bi]]], writes=[smv])
                k.op(dve, lambda: nc.vector.reduce_max(out=m[:, 0:1], in_=m[:, 8:8 + len(bsegs)], axis=AX.X),
                     reads=[smv], writes=[smv])
                k.op(dve, lambda: nc.vector.tensor_scalar(out=m[:, 1:2], in0=m[:, 0:1], scalar1=-1.0, scalar2=None, op0=ALU.mult),
                     reads=[smv], writes=[smv])
                k.op(dve, lambda: nc.vector.memset(m[:, 12:16], 0.0), writes=[smv])
                for i, (bi, c0, n) in enumerate(bsegs):
                    k.op(act, lambda i=i, c0=c0, n=n: nc.scalar.activation(
                        out=Pm.ap[:, c0:c0 + n], in_=sap(c0, n), func=AF.Exp, bias=m[:, 1:2], scale=1.0,
                        accum_out=m[:, 12 + i:13 + i]), reads=[bank_b[RBK[bi]], smv], writes=[Pm, smv])

            def stage_b(qt):
                kt0, nkt = geom(qt)
                RBK, sap, Pm, PT, smv = bufs(qt)
                m = smv.ap
                ntot = nkt * 128 + 512
                nb_ = 3 if nkt == 5 else 2
                k.op(dve, lambda: nc.vector.reduce_sum(out=m[:, 2:3], in_=m[:, 12:12 + nb_], axis=AX.X),
                     reads=[smv], writes=[smv])
                k.op(dve, lambda: nc.vector.reciprocal(out=m[:, 3:4], in_=m[:, 2:3]), reads=[smv], writes=[smv])
                k.op(act, lambda: nc.scalar.activation(out=Pm.ap[:, 0:ntot], in_=Pm.ap[:, 0:ntot], func=AF.Copy, scale=m[:, 3:4]),
                     reads=[Pm, smv], writes=[Pm])

            def stage_T(qt):
                kt0, nkt = geom(qt)
                RBK, sap, Pm, PT, smv = bufs(qt)
                nt = nkt + 4
                t1 = V(bank_ap[7], [bank_b[7]])
                t1b = t1.ap.bitcast(BF16)
                k.group([lambda i=i: nc.tensor.transpose(t1b[:, i * 128:(i + 1) * 128], Pm.ap[:, i * 128:(i + 1) * 128],
                                                         identb[:, :]) for i in range(8)], reads=[Pm] + CB, writes=[t1])
                k.op(act, lambda: nc.scalar.copy(out=PT.ap[:, 0:4, :], in_=t1b[:, 0:512].rearrange("p (a b) -> p a b", a=4)),
                     reads=[t1], writes=[PT])
                k.op(dve, lambda: nc.vector.tensor_copy(out=PT.ap[:, 4:8, :], in_=t1b[:, 512:1024].rearrange("p (a b) -> p a b", a=4)),
                     reads=[t1], writes=[PT])
                if nt == 9:
                    k.group([lambda: nc.tensor.transpose(t1b[:, 0:128], Pm.ap[:, 1024:1152], identb[:, :])],
                            reads=[Pm] + CB, writes=[t1])
                    k.op(dve, lambda: nc.vector.tensor_copy(out=PT.ap[:, 8, :], in_=t1b[:, 0:128]), reads=[t1], writes=[PT])

            def stage_O(qt):
                kt0, nkt = geom(qt)
                RBK, sap, Pm, PT, smv = bufs(qt)
                nt = nkt + 4
                obk = V(bank_ap[6], [bank_b[6]])
                fns = []
                for i in range(nt):
                    lhs = vtm.ap[:, kt0 + i, :] if i < nkt else cvh.ap[:, i - nkt, :]
                    fns.append(lambda i=i, lhs=lhs: nc.tensor.matmul(
                        obk.ap[:, (qt % 4) * 128:(qt % 4 + 1) * 128], lhsT=lhs, rhs=PT.ap[:, i, :],
                        start=(i == 0), stop=(i == nt - 1)))
                k.group(fns, reads=[vtm, cvh, PT], writes=[obk])
                if qt % 4 == 3:
                    tb = qt // 4
                    ov = hch(io, oh, tb * 512, tb * 512 + 512)
                    k.op(act, lambda: nc.scalar.copy(out=ov.ap, in_=obk.ap), reads=[obk], writes=[ov])

            bias_load(0)
            stage_S(0)
            stage_S(1)
            stage_a(0)
            nxt = None
            for i in range(8 + 3):
                if 0 <= i - 2 < 8:
                    stage_T(i - 2)
                if i + 2 < 8:
                    stage_S(i + 2)
                if i + 1 < 8:
                    stage_a(i + 1)
                    if i + 1 == 7 and h + 1 < H:
                        ck_load(h + 1)
                        nxt = (proj_groups(wget(("q", "S", l, h + 1)), ih), proj_groups(wget(("k", "S", l, h + 1)), ih))
                if i < 8:
                    stage_b(i)
                if 0 <= i - 3 < 8:
                    stage_O(i - 3)
            return nxt

        def fourier_group(sbk, l, g, ih, io, oc):
            isP = sbk == "P"
            fT = scr(0, 2048, BF16)
            AB = scr(2048, 4096, BF16, (8, 256))

            def ev_f(tb, bk):
                fv = sub(fT, fT.ap[:, tb * 512:(tb + 1) * 512])
                k.op(act, lambda: nc.scalar.copy(out=fv.ap, in_=bk.ap), reads=[bk], writes=[fv])
            proj_fm(wget(("f", sbk, l, g)), ih, ev_f)
            for t2 in range(4):
                bk = bank()
                for i in range(2):
                    t = t2 * 2 + i
                    k.group([lambda t=t, i=i, bk=bk: nc.tensor.matmul(
                        bk.ap[:, i * 256:(i + 1) * 256], lhsT=fT.ap[:, t * 128:(t + 1) * 128], rhs=csc[:, :],
                        start=True, stop=True)], reads=[fT] + CB, writes=[bk])
                av = sub(AB, AB.ap[:, t2 * 2:t2 * 2 + 2, :])
                k.op(dve if t2 % 2 == 0 else act,
                     (lambda bk=bk, av=av: nc.vector.tensor_copy(out=av.ap, in_=bk.ap.rearrange("p (a b) -> p a b", a=2)))
                     if t2 % 2 == 0 else
                     (lambda bk=bk, av=av: nc.scalar.copy(out=av.ap, in_=bk.ap.rearrange("p (a b) -> p a b", a=2))),
                     reads=[bk], writes=[av])
            if isP:
                wc = wget(("dftc", sbk, l, g, 0))
                ws_ = wget(("dfts", sbk, l, g, 0))
                for b in range(c.PB):
                    bk = bank()
                    fns = []
                    for m in range(2):
                        t = b * 2 + m
                        fns.append(lambda t=t, m=m, bk=bk: nc.tensor.matmul(
                            bk.ap[:, 0:256], lhsT=AB.ap[:, t, 0:128], rhs=wc.ap[:, m * 256:(m + 1) * 256],
                            start=(m == 0), stop=False))
                        fns.append(lambda t=t, m=m, bk=bk: nc.tensor.matmul(
                            bk.ap[:, 0:256], lhsT=AB.ap[:, t, 128:256], rhs=ws_.ap[:, m * 256:(m + 1) * 256],
                            start=False, stop=(m == 1)))
                    k.group(fns, reads=[AB, wc, ws_], writes=[bk])
                    ov = hch(io, oc, b * 256, b * 256 + 256)
                    k.op(act, lambda bk=bk, ov=ov: nc.scalar.copy(out=ov.ap, in_=bk.ap[:, 0:256]), reads=[bk], writes=[ov])
                wrefill()
            else:
                for nb in range(4):
                    wc = wget(("dftc", sbk, l, g, nb))
                    ws_ = wget(("dfts", sbk, l, g, nb))
                    bk = bank()
                    fns = []
                    for m in range(8):
                        fns.append(lambda m=m, bk=bk, wc=wc: nc.tensor.matmul(
                            bk.ap[:, 0:256], lhsT=AB.ap[:, m, 0:128], rhs=wc.ap[:, m * 256:(m + 1) * 256],
                            start=(m == 0), stop=False))
                        fns.append(lambda m=m, bk=bk, ws_=ws_: nc.tensor.matmul(
                            bk.ap[:, 0:256], lhsT=AB.ap[:, m, 128:256], rhs=ws_.ap[:, m * 256:(m + 1) * 256],
                            start=False, stop=(m == 7)))
                    k.group(fns, reads=[AB, wc, ws_], writes=[bk])
                    ov = hch(io, oc, nb * 256, nb * 256 + 256)
                    k.op(act, lambda bk=bk, ov=ov: nc.scalar.copy(out=ov.ap, in_=bk.ap[:, 0:256]), reads=[bk], writes=[ov])
                    wrefill()

        def sgu_group(sbk, l, g, ih, io, oc):
            uT = scr(0, 2048, BF16)
            vsT = scr(2048, 2048, BF16)
            vn = scr(4096, 2048, BF16, (8, 128))
            wsp = scr(6144, 256, BF16)
            gsg = scr(6656, 512, F32)
            bsp = scr(8192, 2048, F32)
            tmp = scr(10240, 2048, F32)
            r0 = (l * GC + g) * 128
            k.dma(pool, wsp.ap, wspT_d[r0:r0 + 128, :], wsp_s, writes=[wsp])
            k.dma(sp, bsp.ap, bsp_d[r0:r0 + 128, :], bsp_s, writes=[bsp])
            k.dma(sp, gsg.ap, gsgu_d[r0:r0 + 128, :], gsg_s, writes=[gsg])

            def ev_u(tb, bk):
                uv = sub(uT, uT.ap[:, tb * 512:(tb + 1) * 512])
                k.op(act, lambda: nc.scalar.activation(out=uv.ap, in_=bk.ap, func=AF.Gelu), reads=[bk], writes=[uv])
            proj_fm(wget(("u", sbk, l, g)), ih, ev_u)

            def ev_v(tb, bk):
                vv = sub(vsT, vsT.ap[:, tb * 512:(tb + 1) * 512])
                k.op(act, lambda: nc.scalar.activation(out=vv.ap, in_=bk.ap, func=AF.Gelu), reads=[bk], writes=[vv])
            proj_fm(wget(("vs", sbk, l, g)), ih, ev_v)
            m = small.ap
            tbk = bank()
            tb16 = tbk.ap.bitcast(BF16)
            k.group([lambda i=i: nc.tensor.transpose(tb16[:, i * 128:(i + 1) * 128], vsT.ap[:, i * 128:(i + 1) * 128],
                                                     identb[:, :]) for i in range(8)], reads=[vsT] + CB, writes=[tbk])
            k.op(dve, lambda: nc.vector.memset(m[:, 16:24], 0.0), writes=[small])
            for i in range(8):
                k.op(act, lambda i=i: nc.scalar.activation(out=tmp.ap.bitcast(BF16)[:, 0:128], in_=tb16[:, i * 128:(i + 1) * 128],
                                                           func=AF.Square, accum_out=m[:, 16 + i:17 + i]),
                     reads=[tbk, small], writes=[tmp, small])
            k.op(dve, lambda: nc.vector.tensor_scalar(out=m[:, 24:32], in0=m[:, 16:24], scalar1=1.0 / 128, scalar2=EPS,
                                                      op0=ALU.mult, op1=ALU.add), reads=[small], writes=[small])
            k.op(act, lambda: nc.scalar.activation(out=m[:, 24:32], in_=m[:, 24:32], func=AF.Sqrt), reads=[small], writes=[small])
            k.op(dve, lambda: nc.vector.reciprocal(out=m[:, 32:40], in_=m[:, 24:32]), reads=[small], writes=[small])
            for i in range(8):
                k.op(dve, lambda i=i: nc.vector.scalar_tensor_tensor(
                    out=vn.ap[:, i, :], in0=tb16[:, i * 128:(i + 1) * 128], scalar=m[:, 32 + i:33 + i], in1=gsg.ap,
                    op0=ALU.mult, op1=ALU.mult), reads=[tbk, small, gsg], writes=[vn])
            for tb in range(2):
                bk = bank()
                for i in range(4):
                    ch = tb * 4 + i
                    k.group([lambda i=i, ch=ch, bk=bk: nc.tensor.matmul(
                        bk.ap[:, i * 128:(i + 1) * 128], lhsT=vn.ap[:, ch, :], rhs=wsp.ap, start=True, stop=True)],
                        reads=[vn, wsp], writes=[bk])
                k.op(dve, lambda bk=bk: nc.vector.tensor_tensor(out=tmp.ap, in0=bk.ap, in1=bsp.ap, op=ALU.add),
                     reads=[bk, bsp], writes=[tmp])
                ov = hch(io, oc, tb * 512, tb * 512 + 512)
                k.op(dve, lambda ov=ov, tb=tb: nc.vector.tensor_tensor(
                    out=ov.ap, in0=tmp.ap, in1=uT.ap[:, tb * 512:(tb + 1) * 512], op=ALU.mult),
                    reads=[tmp, uT], writes=[ov])

        superblock("P")
        superblock("S")
        for ds in final_sems:
            sp.wait(Tok(ds.sem, ds.key, ds.cnt))
        for e in (pe, act, dve):
            sp.wait(Tok(e.sem, e.key, e.cnt))
    return nc


def build_two_pass(cfg):
    rec = []
    build(cfg, plan_tags=None, rec_tags=rec)
    return build(cfg, plan_tags=rec)


def _chunk_layout(w, KC):
    Kd, N = w.shape
    a = w.reshape(KC, 128, N // 128, 128).transpose(2, 1, 0, 3)
    return np.ascontiguousarray(a).reshape((N // 128) * 128, KC * 128)


def _bias_index(cfg):
    GW, rows, WR, WC_ = cfg.GW, cfg.NS // cfg.GW, 8, 16
    idx = np.full((8, 128, 640), 15 * 31, dtype=np.int64)
    for qt in range(8):
        rs0 = min(max(2 * qt - 4, 0), 8)
        rs1 = min(max(2 * qt - 3, 0), 8)
        kt0, kt1 = rs0 // 2, (rs1 + 7) // 2
        for q in range(128):
            r, cq = 2 * qt + q // GW, q % GW
            rstart = min(max(r - WR // 2, 0), rows - WR)
            cstart = min(max(cq - WC_ // 2, 0), GW - WC_)
            for kl in range((kt1 - kt0 + 1) * 128):
                key = kt0 * 128 + kl
                kr, kc_ = key // GW, key % GW
                if rstart <= kr < rstart + WR and cstart <= kc_ < cstart + WC_:
                    dr = kr - r + 7
                    dc = min(max(kc_ - cq, -15), 15) + 15
                    idx[qt, q, kl] = dr * 31 + dc
    return idx


_CACHE = {}


def _consts(cfg):
    if "c" in _CACHE:
        return _CACHE["c"]
    bf = ml_dtypes.bfloat16
    n = np.arange(128)
    ang = 2 * np.pi * np.outer(n, n) / 128.0
    csc = np.concatenate([np.cos(ang), np.sin(ang)], axis=1) / math.sqrt(128.0)

    def dft(N):
        m = np.arange(N)
        a = 2 * np.pi * (np.outer(m, m) % N) / N
        return np.cos(a) / math.sqrt(N), -np.sin(a) / math.sqrt(N)
    c256, s256 = dft(256)
    dftp = np.stack([c256, s256]).reshape(2, 2, 128, 256).transpose(0, 2, 1, 3).reshape(256, 512)
    c1k, s1k = dft(1024)
    d = np.stack([c1k, s1k]).reshape(2, 8, 128, 4, 256).transpose(3, 0, 2, 1, 4).reshape(4 * 2 * 128, 8 * 256)
    out = dict(csc=np.ascontiguousarray(csc).astype(bf), dftp=np.ascontiguousarray(dftp).astype(bf),
               dfts=np.ascontiguousarray(d).astype(bf), bidx=_bias_index(cfg))
    _CACHE["c"] = out
    return out


def prepare(cfg, inp):
    c = cfg
    KC, DEPTH, H, GC = c.KC, c.DEPTH, c.H, c.GC
    f = lambda a: np.ascontiguousarray(np.asarray(a, dtype=np.float32))
    cs = _consts(c)
    shared = {}
    shared["wada"] = np.concatenate([_chunk_layout(f(inp["w_ada"][l]), KC) for l in range(DEPTH)], axis=0)
    shared["bada"] = np.ascontiguousarray(f(inp["b_ada"]).reshape(DEPTH * 6 * KC, 128).T)
    g = np.stack([f(inp["g_pre_mix"]), f(inp["g_post_mix"]), f(inp["g_pre_mlp"]), f(inp["g_post_mlp"])])
    shared["gains"] = np.ascontiguousarray(g.reshape(4 * DEPTH * KC, 128).T)
    shared["win"] = np.concatenate([_chunk_layout(f(inp["w_in"][l]), KC) for l in range(DEPTH)], axis=0)
    wbr = [np.concatenate([f(inp["w_br_a"][l]), f(inp["w_br_f"][l]), f(inp["w_br_c"][l])], axis=0) for l in range(DEPTH)]
    shared["wbr"] = np.concatenate([_chunk_layout(w, KC) for w in wbr], axis=0)
    shared["wout"] = np.concatenate([_chunk_layout(f(inp["w_out"][l]), KC) for l in range(DEPTH)], axis=0)
    shared["w1"] = np.concatenate([_chunk_layout(f(inp["w_mlp1"][l]), KC) for l in range(DEPTH)], axis=0)
    w2 = f(inp["w_mlp2"]).reshape(DEPTH, c.NHC, 8, 128, KC // 2, 256).transpose(0, 1, 4, 3, 2, 5)
    shared["w2"] = np.ascontiguousarray(w2).reshape(DEPTH * c.NHC * (KC // 2) * 128, 8 * 256)
    shared["wspT"] = np.ascontiguousarray(f(inp["w_spatial"]).transpose(0, 1, 3, 2)).reshape(DEPTH * GC * 128, 128)
    shared["bsp"] = np.ascontiguousarray(np.broadcast_to(f(inp["b_spatial"])[:, :, None, None, :], (DEPTH, GC, 128, 4, 128))).reshape(-1, 512)
    shared["gsgu"] = np.ascontiguousarray(np.broadcast_to(f(inp["g_sgu"])[:, :, None, :], (DEPTH, GC, 128, 128))).reshape(-1, 128)
    rpb = f(inp["rpb"]).reshape(DEPTH, H, 15 * 31)
    rpb_ext = np.concatenate([rpb, np.full((DEPTH, H, 1), NEG, np.float32)], axis=2)
    shared["biast"] = np.ascontiguousarray(rpb_ext[:, :, cs["bidx"]]).reshape(DEPTH * H * 8 * 128, 640)
    shared["csc"], shared["dftp"], shared["dfts"] = cs["csc"], cs["dftp"], cs["dfts"]
    xp, xs = f(inp["x_prompt"]), f(inp["x_sample"])
    ck, cv, cc, cctx = f(inp["cache_k"]), f(inp["cache_v"]), f(inp["c"]), f(inp["c_ctx"])
    in_maps = []
    for core in range(c.NCORES):
        b = core // 4
        m = dict(shared)
        m["xp"] = xp[core * c.PB:(core + 1) * c.PB].reshape(c.TB, c.D)
        m["xs"] = xs[b]
        m["ckT"] = np.ascontiguousarray(ck[b].transpose(0, 2, 3, 1)).reshape(DEPTH * H * 128, c.PAST)
        m["cv"] = cv[b].reshape(DEPTH * c.PAST, H * 128)
        m["cvec"] = np.ascontiguousarray(np.stack([cctx.reshape(KC, 128), cc[b].reshape(KC, 128)], axis=2).transpose(1, 0, 2)).reshape(128, KC * 2)
        in_maps.append(m)
    return in_maps


def run(cfg, inp, trace=False):
    in_maps = prepare(cfg, inp)
    nc = build_two_pass(cfg)
    res = run_bass_kernel_spmd(nc, in_maps, core_ids=list(range(cfg.NCORES)), **({"trace": True} if trace else {}))
    c = cfg
    R = res.results
    yp = np.concatenate([R[i]["yp"].reshape(c.PB, c.SEQ, c.D) for i in range(c.NCORES)], axis=0)
    nb = (c.NCORES + 3) // 4
    ys = np.zeros((nb, c.NS, c.D), np.float32)
    for i in range(c.NCORES):
        q = i % 4
        ys[i // 4, q * 256:(q + 1) * 256] = R[i]["ys"][q * 256:(q + 1) * 256]
    nk = np.concatenate([R[i]["nk"].reshape(c.PB, c.DEPTH, c.SEQ, c.H, 128) for i in range(c.NCORES)], axis=0)
    nv = np.concatenate([R[i]["nv"].reshape(c.PB, c.DEPTH, c.SEQ, c.H, 128) for i in range(c.NCORES)], axis=0)
    return (yp.astype(np.float32), ys, nk.astype(np.float32), nv.astype(np.float32)), res


def kernel(**inputs):
    cfg = Cfg()
    outs, _ = run(cfg, inputs)
    return outs
```

```python
import math
import contextlib
import numpy as np
import ml_dtypes
import concourse.bass as bass
import concourse.mybir as mybir
from concourse.bass_utils import run_bass_kernel_spmd

F32 = mybir.dt.float32
BF16 = mybir.dt.bfloat16
AF = mybir.ActivationFunctionType
ALU = mybir.AluOpType
AX = mybir.AxisListType

NEG = -30000.0
EPS = 1e-6


class Cfg:
    def __init__(self, D=2048, H=8, GF=4, GC=4, DFF=8192, DEPTH=4, NCORES=8):
        self.D, self.H, self.GF, self.GC, self.DFF, self.DEPTH, self.NCORES = D, H, GF, GC, DFF, DEPTH, NCORES
        self.SEQ, self.PB, self.NS, self.PAST, self.GW = 256, 4, 1024, 512, 64
        self.KC = D // 128
        self.NO = H + GF + GC
        assert self.NO == self.KC
        self.WA, self.WF, self.WC = H * 128, GF * 128, GC * 128
        self.INW = 3 * self.WA + self.WF + 2 * self.WC + 3 * D
        self.NIN = self.INW // 128
        self.NHC = DFF // 1024
        self.TB = 1024
        self.oq, self.ok, self.ov = 0, H, 2 * H
        self.of = 3 * H
        self.ou = 3 * H + GF
        self.ovs = 3 * H + GF + GC
        self.oga = 3 * H + GF + 2 * GC
        self.ogf = self.oga + self.KC
        self.ogc = self.ogf + self.KC
        self.NMOD = DEPTH * 6 * self.KC


class Tok:
    __slots__ = ("sem", "key", "val")

    def __init__(self, sem, key, val):
        self.sem, self.key, self.val = sem, key, val


class Buf:
    __slots__ = ("name", "w", "r", "excl")

    def __init__(self, name, excl=False):
        self.name, self.w, self.r, self.excl = name, None, {}, excl


class V:
    __slots__ = ("ap", "bufs")

    def __init__(self, ap, bufs):
        self.ap, self.bufs = ap, list(bufs)


class Eng:
    def __init__(self, nc, es, name, handle):
        self.name, self.h = name, handle
        self.sem = es.enter_context(nc.semaphore("pg_" + name))
        self.key = "pg_" + name
        self.cnt = 0
        self.seen = {}

    def wait(self, tok):
        if self.seen.get(tok.key, 0) >= tok.val:
            return
        self.h.wait_ge(tok.sem, tok.val)
        self.seen[tok.key] = tok.val


class DSem:
    def __init__(self, nc, es, name):
        self.sem = es.enter_context(nc.semaphore("d_" + name))
        self.key = "d_" + name
        self.cnt = 0


def _bufs(vs):
    out = []
    for v in vs:
        if isinstance(v, V):
            out.extend(v.bufs)
        elif isinstance(v, Buf):
            out.append(v)
        else:
            out.extend(_bufs(v))
    return out


class K:
    def __init__(self, nc, es):
        self.nc, self.es = nc, es
        self.pe = Eng(nc, es, "pe", nc.tensor)
        self.act = Eng(nc, es, "act", nc.scalar)
        self.dve = Eng(nc, es, "dve", nc.vector)
        self.pool = Eng(nc, es, "pool", nc.gpsimd)
        self.sp = Eng(nc, es, "sp", nc.sync)
        self.nds = 0

    def dsem(self, name):
        self.nds += 1
        return DSem(self.nc, self.es, "%s%d" % (name, self.nds))

    def _deps(self, eng, rb, wb):
        deps = []
        for b in rb:
            if b.w is not None:
                deps.append(b.w)
        for b in wb:
            deps.extend(b.r.values())
            if b.w is not None:
                deps.append(b.w)
        for t in deps:
            if eng is self.pe and t.key == self.pe.key:
                continue
            eng.wait(t)

    def _reg(self, tok, rb, wb):
        for b in rb:
            o = b.r.get(tok.key)
            if o is None or o.val < tok.val:
                b.r[tok.key] = tok
        for b in wb:
            b.w = tok
            b.r = {}

    @staticmethod
    def _split(reads, writes):
        rb, wb = _bufs(reads), _bufs(writes)
        ex = [b for b in rb if b.excl]
        if ex:
            rb = [b for b in rb if not b.excl]
            wb = wb + [b for b in ex if b not in wb]
        return rb, wb

    def op(self, eng, fn, reads=(), writes=()):
        rb, wb = self._split(reads, writes)
        self._deps(eng, rb, wb)
        ins = fn()
        eng.cnt += 1
        ins.then_inc(eng.sem, 1)
        self._reg(Tok(eng.sem, eng.key, eng.cnt), rb, wb)

    def group(self, fns, reads=(), writes=()):
        eng = self.pe
        rb, wb = self._split(reads, writes)
        self._deps(eng, rb, wb)
        ins = None
        for f in fns:
            ins = f()
        eng.cnt += 1
        ins.then_inc(eng.sem, 1)
        self._reg(Tok(eng.sem, eng.key, eng.cnt), rb, wb)

    def dma(self, q, out, in_, ds, reads=(), writes=()):
        rb, wb = _bufs(reads), _bufs(writes)
        self._deps(q, rb, wb)
        ins = q.h.dma_start(out=out, in_=in_)
        ins.then_inc(ds.sem, 16)
        ds.cnt += 16
        self._reg(Tok(ds.sem, ds.key, ds.cnt), rb, wb)


def build(cfg, plan_tags=None, rec_tags=None):
    c = cfg
    D, H, GF, GC, KC, DEPTH, TB = c.D, c.H, c.GF, c.GC, c.KC, c.DEPTH, c.TB
    nc = bass.Bass("TRN2", target_bir_lowering=False)

    def din(name, shape, dt=F32):
        return nc.dram_tensor(name, list(shape), dt, kind="ExternalInput").ap()

    def dout(name, shape, dt=F32):
        return nc.dram_tensor(name, list(shape), dt, kind="ExternalOutput").ap()

    xp_d = din("xp", [TB, D])
    xs_d = din("xs", [TB, D])
    ckT_d = din("ckT", [DEPTH * H * 128, c.PAST])
    cv_d = din("cv", [DEPTH * c.PAST, H * 128])
    cvec_d = din("cvec", [128, KC * 2])
    wada_d = din("wada", [c.NMOD * 128, KC * 128])
    bada_d = din("bada", [128, c.NMOD])
    gains_d = din("gains", [128, 4 * DEPTH * KC])
    win_d = din("win", [DEPTH * c.NIN * 128, KC * 128])
    wbr_d = din("wbr", [DEPTH * KC * 128, KC * 128])
    wout_d = din("wout", [DEPTH * KC * 128, KC * 128])
    w1_d = din("w1", [DEPTH * (c.DFF // 128) * 128, KC * 128])
    w2_d = din("w2", [DEPTH * c.NHC * (KC // 2) * 128, 8 * 256])
    wspT_d = din("wspT", [DEPTH * GC * 128, 128])
    bsp_d = din("bsp", [DEPTH * GC * 128, 512])
    gsgu_d = din("gsgu", [DEPTH * GC * 128, 128])
    bias_d = din("biast", [DEPTH * H * 8 * 128, 640])
    csc_d = din("csc", [128, 256], BF16)
    dftp_d = din("dftp", [2 * 128, 2 * 256], BF16)
    dfts_d = din("dfts", [4 * 2 * 128, 8 * 256], BF16)

    yp_d = dout("yp", [TB, D])
    ys_d = dout("ys", [TB, D])
    nk_d = dout("nk", [c.PB * DEPTH * c.SEQ, H * 128])
    nv_d = dout("nv", [c.PB * DEPTH * c.SEQ, H * 128])

    with contextlib.ExitStack() as es:
        k = K(nc, es)
        pe, act, dve, pool, sp = k.pe, k.act, k.dve, k.pool, k.sp

        def sb(name, shape, dt):
            return es.enter_context(nc.sbuf_tensor("s_" + name, list(shape), dt))

        RB = [sb("rb%d" % i, [128, KC * 1024], BF16) for i in range(3)]
        RBf = [r[:, :].bitcast(F32) for r in RB]
        RU = [[Buf("rb%d_%d" % (i, u)) for u in range(2 * KC)] for i in range(3)]
        xres = sb("xres", [128, KC * 1024], F32)
        XB = [[Buf("x%d_%d" % (j, t)) for t in range(2)] for j in range(KC)]
        NSLOT = 4
        SLOT = max(KC * 128, 8 * 256)
        wb_t = [sb("wb%d" % i, [128, SLOT], BF16) for i in range(NSLOT)]
        wb_b = [Buf("wb%d" % i) for i in range(NSLOT)]
        wb_s = [k.dsem("wb") for _ in range(NSLOT)]
        wb_s2 = [k.dsem("wbh") for _ in range(NSLOT)]
        SCRB = 24576
        scr_t = sb("scr", [128, SCRB // 2], BF16)
        scr_pages = [Buf("scr%d" % i) for i in range(SCRB // 512)]
        modsb = sb("modsb", [128, c.NMOD * 2], F32)
        modb = Buf("modsb")
        gains = sb("gains", [128, 4 * DEPTH * KC], F32)
        gainsb = Buf("gains")
        lvec = sb("lvec", [128, 6 * KC], F32)
        lvb = Buf("lvec")
        identb = sb("identb", [128, 128], BF16)
        identf = sb("identf", [128, 128], F32)
        onesb = sb("onesb", [128, 128], BF16)
        cbuf = Buf("consts")
        csc = sb("csc", [128, 256], BF16)
        bada_p = sb("badap", [128, c.NMOD], F32)
        sT_p = sb("sTp", [128, KC * 2], BF16)
        badab = Buf("badab")
        sTb = Buf("sTb")
        final_sems = []
        stage_s = [k.dsem("stg"), k.dsem("stg")]
        kout_s, vout_s, bias_s, ck_s, cvh_s = k.dsem("ko"), k.dsem("vo"), k.dsem("bi"), k.dsem("ck"), k.dsem("cvh")
        wsp_s, bsp_s, gsg_s = k.dsem("wsp"), k.dsem("bsp"), k.dsem("gsg")
        final_sems += stage_s + [kout_s, vout_s]

        def scr(off, nbytes, dt, shape=None):
            assert off % 4 == 0 and off + nbytes <= SCRB, (off, nbytes)
            ap = scr_t[:, off // 2:(off + nbytes) // 2]
            if dt == F32:
                ap = ap.bitcast(F32)
            if shape is not None:
                ap = ap.rearrange("p (a b) -> p a b", a=shape[0])
            return V(ap, scr_pages[off // 512:(off + nbytes + 511) // 512])

        def sub(v, ap):
            return V(ap, v.bufs)

        psS = es.enter_context(nc.psum_tensor("psS", [128, 1536], F32))
        singles = [es.enter_context(nc.psum_tensor("ps%d" % i, [128, 512], F32)) for i in range(5)]
        bank_ap = [psS[:, i * 512:(i + 1) * 512] for i in range(3)] + [t[:, :] for t in singles]
        bank_b = [Buf("ps%d" % i, excl=True) for i in range(8)]
        rot = [0, 0]

        def bank():
            i = rot[0] % 5
            rot[0] += 1
            return V(bank_ap[i], [bank_b[i]])

        def bank2():
            i = 3 + rot[1] % 2
            rot[1] += 1
            return V(bank_ap[i], [bank_b[i]])

        def hbank():
            return V(bank_ap[5], [bank_b[5]])

        def sbank(t):
            return V(bank_ap[6 + t], [bank_b[6 + t]])

        def hch(i, j, t0=0, t1=1024):
            return V(RB[i][:, j * 1024 + t0:j * 1024 + t1], RU[i][2 * j + t0 // 512:2 * j + (t1 - 1) // 512 + 1])

        def fch(pair, jj, tb):
            half = KC // 2
            b = pair[jj // half]
            loc = jj % half
            return V(RBf[b][:, loc * 1024 + tb * 512:loc * 1024 + tb * 512 + 512], RU[b][4 * loc + 2 * tb:4 * loc + 2 * tb + 2])

        def xch(j, tb):
            return V(xres[:, j * 1024 + tb * 512:j * 1024 + tb * 512 + 512], [XB[j][tb]])

        wq = []
        wstate = {"n": 0, "plan": iter(plan_tags) if plan_tags is not None else None}

        def src_of(tag):
            kind = tag[0]
            if kind == "ada":
                idx = tag[1]
                return wada_d[idx * 128:(idx + 1) * 128, :], KC * 128, True
            sbk, l = tag[1], tag[2]

            def win(ch):
                r0 = (l * c.NIN + ch) * 128
                return win_d[r0:r0 + 128, :], KC * 128, True
            if kind == "q":
                return win(c.oq + tag[3])
            if kind == "k":
                return win(c.ok + tag[3])
            if kind == "v":
                return win(c.ov + tag[3])
            if kind == "f":
                return win(c.of + tag[3])
            if kind == "u":
                return win(c.ou + tag[3])
            if kind == "vs":
                return win(c.ovs + tag[3])
            if kind == "ga":
                return win(c.oga + tag[3])
            if kind == "gf":
                return win(c.ogf + tag[3])
            if kind == "gc":
                return win(c.ogc + tag[3])
            if kind in ("dftc", "dfts"):
                cs_ = 0 if kind == "dftc" else 1
                if sbk == "P":
                    return dftp_d[cs_ * 128:(cs_ + 1) * 128, :], 512, False
                nb = tag[4]
                return dfts_d[(nb * 2 + cs_) * 128:(nb * 2 + cs_ + 1) * 128, :], 2048, False
            if kind == "br":
                r0 = (l * KC + tag[3]) * 128
                return wbr_d[r0:r0 + 128, :], KC * 128, True
            if kind == "wo":
                r0 = (l * KC + tag[3]) * 128
                return wout_d[r0:r0 + 128, :], KC * 128, True
            if kind == "w1":
                r0 = (l * (c.DFF // 128) + tag[3] * 8 + tag[4]) * 128
                return w1_d[r0:r0 + 128, :], KC * 128, True
            if kind == "w2":
                r0 = ((l * c.NHC + tag[3]) * (KC // 2) + tag[4]) * 128
                return w2_d[r0:r0 + 128, :], 2048, True
            raise KeyError(tag)

        def wissue():
            try:
                tag = next(wstate["plan"])
            except StopIteration:
                return False
            src, n, cast = src_of(tag)
            i = wstate["n"] % NSLOT
            wstate["n"] += 1
            q = pool if cast else sp
            k.dma(q, wb_t[i][:, 0:n], src, (wb_s if cast else wb_s2)[i], writes=[wb_b[i]])
            wq.append((tag, V(wb_t[i][:, 0:n], [wb_b[i]])))
            return True

        def wget(tag):
            if plan_tags is None:
                rec_tags.append(tag)
                n = src_of(tag)[1]
                return V(wb_t[0][:, 0:n], [wb_b[0]])
            while len(wq) < NSLOT - 1:
                if not wissue():
                    break
            t, v = wq.pop(0)
            assert t == tag, (t, tag)
            return v

        def wrefill():
            if plan_tags is None:
                return
            while len(wq) < NSLOT - 1:
                if not wissue():
                    break

        k.op(pool, lambda: nc.gpsimd.memset(identb[:, :], 1.0), writes=[cbuf])
        k.op(pool, lambda: nc.gpsimd.affine_select(out=identb[:, :], in_=identb[:, :], pattern=[[-1, 128]],
                                                   compare_op=ALU.is_equal, fill=0.0, base=0, channel_multiplier=1),
             reads=[cbuf], writes=[cbuf])
        k.op(pool, lambda: nc.gpsimd.memset(identf[:, :], 1.0), writes=[cbuf])
        k.op(pool, lambda: nc.gpsimd.affine_select(out=identf[:, :], in_=identf[:, :], pattern=[[-1, 128]],
                                                   compare_op=ALU.is_equal, fill=0.0, base=0, channel_multiplier=1),
             reads=[cbuf], writes=[cbuf])
        k.op(pool, lambda: nc.gpsimd.memset(onesb[:, :], 1.0), writes=[cbuf])
        k.dma(sp, csc[:, :], csc_d[:, :], k.dsem("m"), writes=[cbuf])
        k.dma(sp, gains[:, :], gains_d[:, :], k.dsem("m"), writes=[gainsb])
        CB = [cbuf]

        cv_f = scr(0, KC * 2 * 4, F32)
        k.dma(sp, cv_f.ap, cvec_d[:, :], k.dsem("m"), writes=[cv_f])
        k.dma(sp, bada_p[:, :], bada_d[:, :], k.dsem("m"), writes=[badab])
        k.op(act, lambda: nc.scalar.activation(out=sT_p[:, :], in_=cv_f.ap, func=AF.Silu), reads=[cv_f], writes=[sTb])
        mod3 = modsb[:, :].rearrange("p (i v) -> p i v", v=2)

        def ada_chunk(idx):
            w = wget(("ada", idx))
            bk = bank()
            fns = [lambda kc=kc: nc.tensor.matmul(bk.ap[:, 0:2], lhsT=w.ap[:, kc * 128:(kc + 1) * 128],
                                                  rhs=sT_p[:, kc * 2:kc * 2 + 2], start=(kc == 0), stop=(kc == KC - 1))
                   for kc in range(KC)]
            k.group(fns, reads=[w, sTb], writes=[bk])
            k.op(dve, lambda: nc.vector.tensor_scalar(out=mod3[:, idx, :], in0=bk.ap[:, 0:2], scalar1=bada_p[:, idx:idx + 1],
                                                      scalar2=None, op0=ALU.add), reads=[bk, badab], writes=[modb])
            wrefill()

        N0 = 6 * KC
        for idx in range(N0):
            ada_chunk(idx)
        pending_ada = list(range(N0, c.NMOD))
        ada_on = [False]

        def ada_tick():
            if not ada_on[0] or not pending_ada:
                return
            n = 2 if len(pending_ada) > 200 else 1
            for _ in range(n):
                if pending_ada:
                    ada_chunk(pending_ada.pop(0))

        def ada_drain():
            while pending_ada:
                ada_chunk(pending_ada.pop(0))

        def modv(l, kind, v):
            i0 = (l * 6 + kind) * KC
            return modsb[:, :].rearrange("p (i v) -> p i v", v=2)[:, i0:i0 + KC, v]

        def gainv(kind, l):
            i0 = (kind * DEPTH + l) * KC
            return gains[:, i0:i0 + KC]

        def lv(i):
            return lvec[:, i * KC:(i + 1) * KC]

        def layer_vectors(l, v):
            rd, wr = [modb, gainsb], [lvb]
            k.op(dve, lambda: nc.vector.scalar_tensor_tensor(out=lv(0), in0=modv(l, 1, v), scalar=1.0, in1=gainv(0, l),
                                                             op0=ALU.add, op1=ALU.mult), reads=rd, writes=wr)
            k.op(dve, lambda: nc.vector.tensor_copy(out=lv(1), in_=modv(l, 0, v)), reads=rd, writes=wr)
            k.op(dve, lambda: nc.vector.tensor_tensor(out=lv(2), in0=modv(l, 2, v), in1=gainv(1, l), op=ALU.mult),
                 reads=rd, writes=wr)
            k.op(dve, lambda: nc.vector.scalar_tensor_tensor(out=lv(3), in0=modv(l, 4, v), scalar=1.0, in1=gainv(2, l),
                                                             op0=ALU.add, op1=ALU.mult), reads=rd, writes=wr)
            k.op(dve, lambda: nc.vector.tensor_copy(out=lv(4), in_=modv(l, 3, v)), reads=rd, writes=wr)
            k.op(dve, lambda: nc.vector.tensor_tensor(out=lv(5), in0=modv(l, 5, v), in1=gainv(3, l), op=ALU.mult),
                 reads=rd, writes=wr)

        def rstd_from(ssb, dst):
            k.op(dve, lambda: nc.vector.tensor_scalar(out=dst.ap, in0=ssb.ap, scalar1=1.0 / D, scalar2=EPS,
                                                      op0=ALU.mult, op1=ALU.add), reads=[ssb], writes=[dst])
            k.op(act, lambda: nc.scalar.activation(out=dst.ap, in_=dst.ap, func=AF.Sqrt), reads=[dst], writes=[dst])
            k.op(dve, lambda: nc.vector.reciprocal(out=dst.ap, in_=dst.ap), reads=[dst], writes=[dst])

        def stat_acc(src_v, tb, first, last, sqi):
            sq = scr(16384 + (sqi % 2) * 1024, 1024, BF16)
            k.op(act, lambda: nc.scalar.activation(out=sq.ap, in_=src_v.ap, func=AF.Square), reads=[src_v], writes=[sq])
            ssb = sbank(tb)
            k.group([lambda: nc.tensor.matmul(ssb.ap, lhsT=onesb[:, :], rhs=sq.ap, start=first, stop=last)],
                    reads=[sq] + CB, writes=[ssb])

        def norm_to_h(ih, gi, si, tb):
            rs = scr(tb * 2048, 2048, F32)
            rstd_from(sbank(tb), rs)
            for j in range(KC):
                tmp = scr(4096 + (j % 2) * 2048, 2048, F32)
                xv = xch(j, tb)
                k.op(dve, lambda: nc.vector.tensor_tensor(out=tmp.ap, in0=xv.ap, in1=rs.ap, op=ALU.mult),
                     reads=[xv, rs], writes=[tmp])
                hv = hch(ih, j, tb * 512, tb * 512 + 512)
                k.op(act, lambda: nc.scalar.activation(out=hv.ap, in_=tmp.ap, func=AF.Identity,
                                                       scale=lv(gi)[:, j:j + 1], bias=lv(si)[:, j:j + 1]),
                     reads=[tmp, lvb], writes=[hv])

        def sublayer_finish(pair, gti, nxt):
            for tb in range(2):
                rs = scr(tb * 2048, 2048, F32)
                rstd_from(sbank(tb), rs)
                for j in range(KC):
                    tmp = scr(4096 + (j % 2) * 2048, 2048, F32)
                    rv = fch(pair, j, tb)
                    xv = xch(j, tb)
                    k.op(dve, lambda: nc.vector.tensor_tensor(out=tmp.ap, in0=rv.ap, in1=rs.ap, op=ALU.mult),
                         reads=[rv, rs], writes=[tmp])
                    k.op(dve, lambda: nc.vector.scalar_tensor_tensor(out=xv.ap, in0=tmp.ap, scalar=lv(gti)[:, j:j + 1],
                                                                     in1=xv.ap, op0=ALU.mult, op1=ALU.add),
                         reads=[tmp, xv, lvb], writes=[xv])
                    if nxt is not None:
                        stat_acc(xv, tb, j == 0, j == KC - 1, j)
                if nxt is not None:
                    norm_to_h(nxt[0], nxt[1], nxt[2], tb)

        def proj_groups(w, ih):
            res = []
            for tb in range(2):
                bk = bank()
                hs = [hch(ih, kc, tb * 512, tb * 512 + 512) for kc in range(KC)]
                fns = [lambda kc=kc, bk=bk, hs=hs: nc.tensor.matmul(
                    bk.ap, lhsT=w.ap[:, kc * 128:(kc + 1) * 128], rhs=hs[kc].ap,
                    start=(kc == 0), stop=(kc == KC - 1)) for kc in range(KC)]
                k.group(fns, reads=[w] + hs, writes=[bk])
                res.append((tb, bk))
            wrefill()
            return res

        def early_proj(tags, ih):
            results = [[] for _ in tags]
            steps = []
            state = {}
            for n, tag in enumerate(tags):
                for tb in range(2):
                    def step(n=n, tag=tag, tb=tb):
                        if tb == 0:
                            state[n] = wget(tag)
                        w = state[n]
                        bk = bank()
                        hs = [hch(ih, kc, tb * 512, tb * 512 + 512) for kc in range(KC)]
                        fns = [lambda kc=kc: nc.tensor.matmul(
                            bk.ap, lhsT=w.ap[:, kc * 128:(kc + 1) * 128], rhs=hs[kc].ap,
                            start=(kc == 0), stop=(kc == KC - 1)) for kc in range(KC)]
                        k.group(fns, reads=[w] + hs, writes=[bk])
                        results[n].append((tb, bk))
                        if tb == 1:
                            wrefill()
                    steps.append(step)
            return steps, results

        def proj_fm(w, ih, evac):
            for tb in range(2):
                bk = bank()
                hs = [hch(ih, kc, tb * 512, tb * 512 + 512) for kc in range(KC)]
                fns = [lambda kc=kc, bk=bk, hs=hs: nc.tensor.matmul(
                    bk.ap, lhsT=w.ap[:, kc * 128:(kc + 1) * 128], rhs=hs[kc].ap,
                    start=(kc == 0), stop=(kc == KC - 1)) for kc in range(KC)]
                k.group(fns, reads=[w] + hs, writes=[bk])
                evac(tb, bk)
            wrefill()
            ada_tick()

        small = scr(22016, 512, F32)

        def superblock(sbk):
            isP = sbk == "P"
            vsel = 0 if isP else 1
            x_d = xp_d if isP else xs_d
            y_d = yp_d if isP else ys_d
            for t in range(8):
                for fb in range(D // 512):
                    si = (t * (D // 512) + fb) % 2
                    st = scr(si * 2048, 2048, F32)
                    k.dma(sp, st.ap, x_d[t * 128:(t + 1) * 128, fb * 512:(fb + 1) * 512], stage_s[si], writes=[st])
                    bk = bank()
                    k.group([lambda i=i, bk=bk, st=st: nc.tensor.transpose(
                        bk.ap[:, i * 128:(i + 1) * 128], st.ap[:, i * 128:(i + 1) * 128], identf[:, :])
                        for i in range(4)], reads=[st] + CB, writes=[bk])
                    for i in range(4):
                        j = fb * 4 + i
                        tb, tt = t // 4, t % 4
                        xv = xch(j, tb)
                        k.op(dve if i % 2 == 0 else act,
                             (lambda i=i, bk=bk, xv=xv, tt=tt: nc.vector.tensor_copy(
                                 out=xv.ap[:, tt * 128:(tt + 1) * 128], in_=bk.ap[:, i * 128:(i + 1) * 128]))
                             if i % 2 == 0 else
                             (lambda i=i, bk=bk, xv=xv, tt=tt: nc.scalar.copy(
                                 out=xv.ap[:, tt * 128:(tt + 1) * 128], in_=bk.ap[:, i * 128:(i + 1) * 128])),
                             reads=[bk], writes=[xv])
            ih, io, im = 0, 1, 2
            layer_vectors(0, vsel)
            ada_on[0] = True
            for tb in range(2):
                for j in range(KC):
                    stat_acc(xch(j, tb), tb, j == 0, j == KC - 1, j)
                norm_to_h(ih, 0, 1, tb)

            for l in range(DEPTH):
                qT = scr(0, 2048, BF16)
                kT = scr(2048, 2048, BF16)
                carry = None
                for h in range(H):
                    oh = h
                    if isP:
                        carry = prompt_head(l, h, ih, io, oh, qT, kT, carry)
                    else:
                        carry = sample_head(l, h, ih, io, oh, qT, kT, carry)
                for g in range(GF):
                    fourier_group(sbk, l, g, ih, io, H + g)
                for g in range(GC):
                    sgu_group(sbk, l, g, ih, io, H + GF + g)
                for j in range(KC):
                    sg = [scr(i * 2048, 2048, BF16) for i in range(3)]

                    def ev_gate(gi):
                        def ev(tb, bk, gi=gi):
                            sv = sub(sg[gi], sg[gi].ap[:, tb * 512:(tb + 1) * 512])
                            k.op(act, lambda: nc.scalar.activation(out=sv.ap, in_=bk.ap, func=AF.Sigmoid),
                                 reads=[bk], writes=[sv])
                        return ev
                    proj_fm(wget(("ga", sbk, l, j)), ih, ev_gate(0))
                    proj_fm(wget(("gf", sbk, l, j)), ih, ev_gate(1))
                    proj_fm(wget(("gc", sbk, l, j)), ih, ev_gate(2))
                    wbr = wget(("br", sbk, l, j))
                    for tb in range(2):
                        acc = scr(6144, 2048, F32)
                        ranges = [(0, H), (H, H + GF), (H + GF, KC)]
                        for gi, (a0, a1) in enumerate(ranges):
                            bk = bank()
                            os_ = [hch(io, kc, tb * 512, tb * 512 + 512) for kc in range(a0, a1)]
                            fns = [lambda n=n, kc=kc, bk=bk, os_=os_, a0=a0, a1=a1: nc.tensor.matmul(
                                bk.ap, lhsT=wbr.ap[:, kc * 128:(kc + 1) * 128], rhs=os_[n].ap,
                                start=(kc == a0), stop=(kc == a1 - 1)) for n, kc in enumerate(range(a0, a1))]
                            k.group(fns, reads=[wbr] + os_, writes=[bk])
                            mv = hch(im, j, tb * 512, tb * 512 + 512)
                            sv = sub(sg[gi], sg[gi].ap[:, tb * 512:(tb + 1) * 512])
                            if gi == 0:
                                k.op(dve, lambda: nc.vector.tensor_tensor(
                                    out=acc.ap, in0=bk.ap, in1=sv.ap, op=ALU.mult), reads=[bk, sv], writes=[acc])
                            else:
                                t2 = scr(8192, 2048, F32)
                                k.op(dve, lambda: nc.vector.tensor_tensor(
                                    out=t2.ap, in0=bk.ap, in1=sv.ap, op=ALU.mult), reads=[bk, sv], writes=[t2])
                                dst = acc if gi == 1 else mv
                                k.op(dve, lambda: nc.vector.tensor_tensor(
                                    out=dst.ap, in0=t2.ap, in1=acc.ap, op=ALU.add), reads=[t2, acc], writes=[dst])
                    wrefill()
                pair = (ih, io)
                for j in range(KC):
                    w = wget(("wo", sbk, l, j))
                    for tb in range(2):
                        bk = bank()
                        ms = [hch(im, kc, tb * 512, tb * 512 + 512) for kc in range(KC)]
                        fns = [lambda kc=kc, bk=bk, ms=ms, w=w: nc.tensor.matmul(
                            bk.ap, lhsT=w.ap[:, kc * 128:(kc + 1) * 128], rhs=ms[kc].ap,
                            start=(kc == 0), stop=(kc == KC - 1)) for kc in range(KC)]
                        k.group(fns, reads=[w] + ms, writes=[bk])
                        rv = fch(pair, j, tb)
                        k.op(dve, lambda bk=bk, rv=rv: nc.vector.tensor_copy(out=rv.ap, in_=bk.ap), reads=[bk], writes=[rv])
                        stat_acc(bk, tb, j == 0, j == KC - 1, j)
                    wrefill()
                sublayer_finish(pair, 2, (im, 3, 4))
                for hc in range(c.NHC):
                    for fc in range(8):
                        w = wget(("w1", sbk, l, hc, fc))

                        def ev_a(tb, bk, fc=fc):
                            av = scr(fc * 2048 + tb * 1024, 1024, BF16)
                            k.op(act, lambda: nc.scalar.activation(out=av.ap, in_=bk.ap, func=AF.Relu), reads=[bk], writes=[av])
                            k.op(dve, lambda: nc.vector.tensor_tensor(out=av.ap, in0=av.ap, in1=av.ap, op=ALU.mult),
                                 reads=[av], writes=[av])
                        proj_fm(w, im, ev_a)
                    for jb in range(KC // 2):
                        w = wget(("w2", sbk, l, hc, jb))
                        for jj in range(2):
                            j = jb * 2 + jj
                            for tb in range(2):
                                bk = bank()
                                as_ = [scr(fc * 2048 + tb * 1024, 1024, BF16) for fc in range(8)]
                                fns = [lambda fc=fc, bk=bk, as_=as_, w=w, jj=jj: nc.tensor.matmul(
                                    bk.ap, lhsT=w.ap[:, fc * 256 + jj * 128:fc * 256 + jj * 128 + 128], rhs=as_[fc].ap,
                                    start=(fc == 0), stop=(fc == 7)) for fc in range(8)]
                                k.group(fns, reads=[w] + as_, writes=[bk])
                                fv = fch(pair, j, tb)
                                if hc == 0:
                                    k.op(dve, lambda bk=bk, fv=fv: nc.vector.tensor_copy(out=fv.ap, in_=bk.ap),
                                         reads=[bk], writes=[fv])
                                else:
                                    k.op(dve, lambda bk=bk, fv=fv: nc.vector.tensor_tensor(
                                        out=fv.ap, in0=bk.ap, in1=fv.ap, op=ALU.add), reads=[bk, fv], writes=[fv])
                                if hc == c.NHC - 1:
                                    stat_acc(fv, tb, j == 0, j == KC - 1, j)
                        wrefill()
                sublayer_finish(pair, 5, None)
                if l + 1 < DEPTH:
                    ada_drain()
                    layer_vectors(l + 1, vsel)
                    for tb in range(2):
                        for j in range(KC):
                            stat_acc(xch(j, tb), tb, j == 0, j == KC - 1, j)
                        norm_to_h(im, 0, 1, tb)
                    ih, io, im = im, ih, io
            outs = []
            for t in range(8):
                tb, tt = t // 4, t % 4
                for fb in range(D // 512):
                    bk = bank()
                    xs_ = [xch(fb * 4 + i, tb) for i in range(4)]
                    k.group([lambda i=i, bk=bk, xs_=xs_, tt=tt: nc.tensor.transpose(
                        bk.ap[:, i * 128:(i + 1) * 128], xs_[i].ap[:, tt * 128:(tt + 1) * 128], identf[:, :])
                        for i in range(4)], reads=xs_ + CB, writes=[bk])
                    si = (t * (D // 512) + fb) % 2
                    st = scr(si * 2048, 2048, F32)
                    k.op(act, lambda bk=bk, st=st: nc.scalar.copy(out=st.ap, in_=bk.ap), reads=[bk], writes=[st])
                    k.dma(sp, y_d[t * 128:(t + 1) * 128, fb * 512:(fb + 1) * 512], st.ap, stage_s[si], reads=[st])

        def prompt_head(l, h, ih, io, oh, qT, kT, carry):
            kf = scr(4096, 2048, F32)
            vf = scr(6144, 2048, F32)
            vtm = scr(8192, 2048, BF16, (8, 128))
            kout = scr(10240, 2048, F32, (4, 128))
            vout = scr(12288, 2048, F32, (4, 128))
            Pm = scr(14336, 1024, BF16, (2, 256))
            PT = scr(15360, 1024, BF16, (4, 128))
            scale = 1.0 / math.sqrt(128.0)

            def ev_q(tb, bk):
                qv = sub(qT, qT.ap[:, tb * 512:(tb + 1) * 512])
                k.op(act, lambda: nc.scalar.activation(out=qv.ap, in_=bk.ap, func=AF.Copy, scale=scale),
                     reads=[bk], writes=[qv])
            if carry is None:
                gq = proj_groups(wget(("q", "P", l, h)), ih)
                gk = proj_groups(wget(("k", "P", l, h)), ih)
            else:
                gq, gk = carry
            for tb, bk in gq:
                ev_q(tb, bk)

            def kv_out(tb, srcf, outst, dst_d, with_tm, kv_s):
                bk = bank()
                k.group([lambda i=i, bk=bk: nc.tensor.transpose(
                    bk.ap[:, i * 128:(i + 1) * 128], srcf.ap[:, i * 128:(i + 1) * 128], identf[:, :])
                    for i in range(4)], reads=[srcf] + CB, writes=[bk])
                k.op(act, lambda: nc.scalar.copy(out=outst.ap, in_=bk.ap.rearrange("p (a b) -> p a b", a=4)),
                     reads=[bk], writes=[outst])
                if with_tm:
                    tv = sub(vtm, vtm.ap[:, tb * 4:(tb + 1) * 4, :])
                    k.op(dve, lambda: nc.vector.tensor_copy(out=tv.ap, in_=bk.ap.rearrange("p (a b) -> p a b", a=4)),
                         reads=[bk], writes=[tv])
                for bb in range(2):
                    b = tb * 2 + bb
                    r0 = (b * DEPTH + l) * c.SEQ
                    dst = dst_d[r0:r0 + 256, h * 128:(h + 1) * 128].rearrange("(a p) d -> p a d", p=128)
                    k.dma(sp, dst, outst.ap[:, bb * 2:bb * 2 + 2, :], kv_s, reads=[outst])

            kst = [kf, vf]
            for tb, bk in gk:
                kv = sub(kT, kT.ap[:, tb * 512:(tb + 1) * 512])
                k.op(dve, lambda: nc.vector.tensor_copy(out=kv.ap, in_=bk.ap), reads=[bk], writes=[kv])
                k.op(act, lambda: nc.scalar.copy(out=kst[tb].ap, in_=bk.ap), reads=[bk], writes=[kst[tb]])
            for tb, bk in gk:
                kv_out(tb, kst[tb], kout, nk_d, False, kout_s)

            def ev_v(tb, bk):
                k.op(act, lambda: nc.scalar.copy(out=vf.ap, in_=bk.ap), reads=[bk], writes=[vf])
                kv_out(tb, vf, vout, nv_d, True, vout_s)
            proj_fm(wget(("v", "P", l, h)), ih, ev_v)

            Sb = {}

            def pbufs(b):
                Pm = scr([14336, 16384, 18432][b % 3], 1024, BF16, (2, 256))
                PT = scr(15360 + (b % 2) * 2048, 1024, BF16, (4, 128))
                sm = scr(22016 + (b % 2) * 512, 512, F32)
                return Pm, PT, sm

            def p_S(b):
                t0 = b * 256
                sbk_ = bank()
                Sb[b] = sbk_
                for qt in range(2):
                    k.group([lambda qt=qt: nc.tensor.matmul(
                        sbk_.ap[:, qt * 256:(qt + 1) * 256], lhsT=qT.ap[:, t0 + qt * 128:t0 + (qt + 1) * 128],
                        rhs=kT.ap[:, t0:t0 + 256], start=True, stop=True)], reads=[qT, kT], writes=[sbk_])

            def p_a(b):
                Pm, PT, sm = pbufs(b)
                sbk_ = Sb[b]
                mx, nmx, rsum = sm.ap[:, 0:2], sm.ap[:, 2:4], sm.ap[:, 4:6]
                k.op(dve, lambda: nc.vector.reduce_max(out=mx, in_=sbk_.ap.rearrange("p (a b) -> p a b", a=2), axis=AX.X),
                     reads=[sbk_], writes=[sm])
                k.op(dve, lambda: nc.vector.tensor_scalar(out=nmx, in0=mx, scalar1=-1.0, scalar2=None, op0=ALU.mult),
                     reads=[sm], writes=[sm])
                k.op(dve, lambda: nc.vector.memset(rsum, 0.0), writes=[sm])
                for qt in range(2):
                    k.op(act, lambda qt=qt: nc.scalar.activation(
                        out=Pm.ap[:, qt, :], in_=sbk_.ap[:, qt * 256:(qt + 1) * 256], func=AF.Exp,
                        bias=nmx[:, qt:qt + 1], scale=1.0, accum_out=rsum[:, qt:qt + 1]),
                        reads=[sbk_, sm], writes=[Pm, sm])

            def p_b(b):
                Pm, PT, sm = pbufs(b)
                rsum, rrec = sm.ap[:, 4:6], sm.ap[:, 6:8]
                k.op(dve, lambda: nc.vector.reciprocal(out=rrec, in_=rsum), reads=[sm], writes=[sm])
                for qt in range(2):
                    k.op(dve, lambda qt=qt: nc.vector.tensor_scalar(
                        out=Pm.ap[:, qt, :], in0=Pm.ap[:, qt, :], scalar1=rrec[:, qt:qt + 1], scalar2=None,
                        op0=ALU.mult), reads=[Pm, sm], writes=[Pm])

            def p_T(b):
                Pm, PT, sm = pbufs(b)
                tbk = V(bank_ap[7], [bank_b[7]])
                tb16 = tbk.ap.bitcast(BF16)
                fns = []
                for kt in range(2):
                    for qt in range(2):
                        fns.append(lambda kt=kt, qt=qt: nc.tensor.transpose(
                            tb16[:, (kt * 2 + qt) * 128:(kt * 2 + qt + 1) * 128],
                            Pm.ap[:, qt, kt * 128:(kt + 1) * 128], identb[:, :]))
                k.group(fns, reads=[Pm] + CB, writes=[tbk])
                k.op(act, lambda: nc.scalar.copy(out=PT.ap, in_=tb16[:, 0:512].rearrange("p (a b) -> p a b", a=4)),
                     reads=[tbk], writes=[PT])

            def p_O(b):
                Pm, PT, sm = pbufs(b)
                t0 = b * 256
                obk = V(bank_ap[6], [bank_b[6]])
                k.group([lambda kt=kt: nc.tensor.matmul(
                    obk.ap[:, 0:256], lhsT=vtm.ap[:, b * 2 + kt, :],
                    rhs=PT.ap[:, kt * 2:kt * 2 + 2, :].rearrange("p a b -> p (a b)"),
                    start=(kt == 0), stop=(kt == 1)) for kt in range(2)], reads=[vtm, PT], writes=[obk])
                ov = hch(io, oh, t0, t0 + 256)
                k.op(act, lambda: nc.scalar.copy(out=ov.ap, in_=obk.ap[:, 0:256]), reads=[obk], writes=[ov])

            nxt = None
            steps = []
            p_S(0)
            p_S(1)
            p_a(0)
            for i in range(c.PB + 3):
                if 0 <= i - 2 < c.PB:
                    p_T(i - 2)
                if i + 2 < c.PB:
                    p_S(i + 2)
                if i + 1 < c.PB:
                    p_a(i + 1)
                    if i + 1 == c.PB - 1 and h + 1 < H:
                        steps, res = early_proj([("q", "P", l, h + 1), ("k", "P", l, h + 1)], ih)
                        nxt = (res[0], res[1])
                if steps:
                    steps.pop(0)()
                if i < c.PB:
                    p_b(i)
                if 0 <= i - 3 < c.PB:
                    p_O(i - 3)
            while steps:
                steps.pop(0)()
            return nxt

        def sample_head(l, h, ih, io, oh, qT, kT, carry):
            vT = scr(4096, 2048, BF16)
            vtm = scr(6656, 2048, BF16, (8, 128))
            ckT = scr(8704, 1024, BF16)
            cvh = scr(9728, 1024, BF16, (4, 128))
            bias = scr(10752, 2560, F32)
            scale = 1.0 / math.sqrt(128.0)
            def ck_load(hh):
                r0 = (l * H + hh) * 128
                k.dma(pool, ckT.ap, ckT_d[r0:r0 + 128, :], ck_s, writes=[ckT])
            if carry is None:
                ck_load(h)
                gq = proj_groups(wget(("q", "S", l, h)), ih)
                gk = proj_groups(wget(("k", "S", l, h)), ih)
            else:
                gq, gk = carry
            k.dma(pool, cvh.ap, cv_d[l * c.PAST:(l + 1) * c.PAST, h * 128:(h + 1) * 128].rearrange(
                "(a p) d -> p a d", p=128), cvh_s, writes=[cvh])

            def ev_q(tb, bk):
                qv = sub(qT, qT.ap[:, tb * 512:(tb + 1) * 512])
                k.op(act, lambda: nc.scalar.activation(out=qv.ap, in_=bk.ap, func=AF.Copy, scale=scale),
                     reads=[bk], writes=[qv])
            for tb, bk in gq:
                ev_q(tb, bk)

            def ev_k(tb, bk):
                kv = sub(kT, kT.ap[:, tb * 512:(tb + 1) * 512])
                k.op(dve, lambda: nc.vector.tensor_copy(out=kv.ap, in_=bk.ap), reads=[bk], writes=[kv])
            for tb, bk in gk:
                ev_k(tb, bk)

            def ev_v(tb, bk):
                vv = sub(vT, vT.ap[:, tb * 512:(tb + 1) * 512])
                k.op(act, lambda: nc.scalar.copy(out=vv.ap, in_=bk.ap), reads=[bk], writes=[vv])
            proj_fm(wget(("v", "S", l, h)), ih, ev_v)
            tbk = bank()
            tb16 = tbk.ap.bitcast(BF16)
            k.group([lambda i=i: nc.tensor.transpose(tb16[:, i * 128:(i + 1) * 128], vT.ap[:, i * 128:(i + 1) * 128],
                                                     identb[:, :]) for i in range(8)], reads=[vT] + CB, writes=[tbk])
            k.op(dve, lambda: nc.vector.tensor_copy(out=vtm.ap, in_=tb16.rearrange("p (a b) -> p a b", a=8)),
                 reads=[tbk], writes=[vtm])
            def geom(qt):
                rs0 = min(max(2 * qt - 4, 0), 8)
                rs1 = min(max(2 * qt - 3, 0), 8)
                kt0, kt1 = rs0 // 2, (rs1 + 7) // 2
                nkt = kt1 - kt0 + 1
                return kt0, nkt

            def bufs(qt):
                par = qt % 2
                RBK = [3 * par, 3 * par + 1, 3 * par + 2]

                def sap(c0, n):
                    bi = c0 // 512
                    return bank_ap[RBK[bi]][:, c0 - 512 * bi:c0 - 512 * bi + n]
                Pm = scr([13312, 15872, 4096][qt % 3], 2304, BF16)
                PT = scr(18432 + par * 2560, 2304, BF16, (9, 128))
                smv = scr(23552 + par * 512, 512, F32)
                return RBK, sap, Pm, PT, smv

            def bias_load(qt):
                br0 = ((l * H + h) * 8 + qt) * 128
                k.dma(sp, bias.ap, bias_d[br0:br0 + 128, :], bias_s, writes=[bias])

            def stage_S(qt):
                kt0, nkt = geom(qt)
                RBK, sap, Pm, PT, smv = bufs(qt)
                Sreg = [bank_b[i] for i in RBK]
                qv = qT.ap[:, qt * 128:(qt + 1) * 128]
                segs = [(0, 512, kT.ap[:, kt0 * 128:kt0 * 128 + 512])]
                if nkt == 5:
                    segs.append((512, 128, kT.ap[:, kt0 * 128 + 512:kt0 * 128 + 640]))
                    segs.append((640, 384, ckT.ap[:, 0:384]))
                    segs.append((1024, 128, ckT.ap[:, 384:512]))
                else:
                    segs.append((512, 512, ckT.ap[:, 0:512]))
                k.group([lambda c0=c0, n=n, rhs=rhs: nc.tensor.matmul(sap(c0, n), lhsT=qv, rhs=rhs, start=True, stop=True)
                         for (c0, n, rhs) in segs], reads=[qT, kT, ckT], writes=Sreg)

            def stage_a(qt):
                kt0, nkt = geom(qt)
                RBK, sap, Pm, PT, smv = bufs(qt)
                m = smv.ap
                bsegs = [(0, 0, 512), (1, 512, 512)] + ([(2, 1024, 128)] if nkt == 5 else [])
                k.op(dve, lambda: nc.vector.tensor_tensor(out=sap(0, 512), in0=sap(0, 512), in1=bias.ap[:, 0:512], op=ALU.add),
                     reads=[bank_b[RBK[0]], bias], writes=[bank_b[RBK[0]]])
                if nkt == 5:
                    k.op(dve, lambda: nc.vector.tensor_tensor(out=sap(512, 128), in0=sap(512, 128), in1=bias.ap[:, 512:640],
                                                              op=ALU.add), reads=[bank_b[RBK[1]], bias], writes=[bank_b[RBK[1]]])
                if qt + 1 < 8:
                    bias_load(qt + 1)
                for i, (bi, c0, n) in enumerate(bsegs):
                    k.op(dve, lambda i=i, c0=c0, n=n: nc.vector.reduce_max(out=m[:, 8 + i:9 + i], in_=sap(c0, n), axis=AX.X),
                         reads=[bank_b[RBK[bi]]], writes=[smv])
                k.op(dve, lambda: nc.vector.reduce_max(out=m[:, 0:1], in_=m[:, 8:8 + len(bsegs)], axis=AX.X),
                     reads=[smv], writes=[smv])
                k.op(dve, lambda: nc.vector.tensor_scalar(out=m[:, 1:2], in0=m[:, 0:1], scalar1=-1.0, scalar2=None, op0=ALU.mult),
                     reads=[smv], writes=[smv])
                k.op(dve, lambda: nc.vector.memset(m[:, 12:16], 0.0), writes=[smv])
                for i, (bi, c0, n) in enumerate(bsegs):
                    k.op(act, lambda i=i, c0=c0, n=n: nc.scalar.activation(
                        out=Pm.ap[:, c0:c0 + n], in_=sap(c0, n), func=AF.Exp, bias=m[:, 1:2], scale=1.0,
                        accum_out=m[:, 12 + i:13 + i]), reads=[bank_b[RBK[bi]], smv], writes=[Pm, smv])

            def stage_b(qt):
                kt0, nkt = geom(qt)
                RBK, sap, Pm, PT, smv = bufs(qt)
                m = smv.ap
                ntot = nkt * 128 + 512
                nb_ = 3 if nkt == 5 else 2
                k.op(dve, lambda: nc.vector.reduce_sum(out=m[:, 2:3], in_=m[:, 12:12 + nb_], axis=AX.X),
                     reads=[smv], writes=[smv])
                k.op(dve, lambda: nc.vector.reciprocal(out=m[:, 3:4], in_=m[:, 2:3]), reads=[smv], writes=[smv])
                k.op(act, lambda: nc.scalar.activation(out=Pm.ap[:, 0:ntot], in_=Pm.ap[:, 0:ntot], func=AF.Copy, scale=m[:, 3:4]),
                     reads=[Pm, smv], writes=[Pm])

            def stage_T(qt):
                kt0, nkt = geom(qt)
                RBK, sap, Pm, PT, smv = bufs(qt)
                nt = nkt + 4
                t1 = V(bank_ap[7], [bank_b[7]])
                t1b = t1.ap.bitcast(BF16)
                k.group([lambda i=i: nc.tensor.transpose(t1b[:, i * 128:(i + 1) * 128], Pm.ap[:, i * 128:(i + 1) * 128],
                                                         identb[:, :]) for i in range(8)], reads=[Pm] + CB, writes=[t1])
                k.op(act, lambda: nc.scalar.copy(out=PT.ap[:, 0:4, :], in_=t1b[:, 0:512].rearrange("p (a b) -> p a b", a=4)),
                     reads=[t1], writes=[PT])
                k.op(dve, lambda: nc.vector.tensor_copy(out=PT.ap[:, 4:8, :], in_=t1b[:, 512:1024].rearrange("p (a b) -> p a b", a=4)),
                     reads=[t1], writes=[PT])
                if nt == 9:
                    k.group([lambda: nc.tensor.transpose(t1b[:, 0:128], Pm.ap[:, 1024:1152], identb[:, :])],
                            reads=[Pm] + CB, writes=[t1])
                    k.op(dve, lambda: nc.vector.tensor_copy(out=PT.ap[:, 8, :], in_=t1b[:, 0:128]), reads=[t1], writes=[PT])

            def stage_O(qt):
                kt0, nkt = geom(qt)
                RBK, sap, Pm, PT, smv = bufs(qt)
                nt = nkt + 4
                obk = V(bank_ap[6], [bank_b[6]])
                fns = []
                for i in range(nt):
                    lhs = vtm.ap[:, kt0 + i, :] if i < nkt else cvh.ap[:, i - nkt, :]
                    fns.append(lambda i=i, lhs=lhs: nc.tensor.matmul(
                        obk.ap[:, (qt % 4) * 128:(qt % 4 + 1) * 128], lhsT=lhs, rhs=PT.ap[:, i, :],
                        start=(i == 0), stop=(i == nt - 1)))
                k.group(fns, reads=[vtm, cvh, PT], writes=[obk])
                if qt % 4 == 3:
                    tb = qt // 4
                    ov = hch(io, oh, tb * 512, tb * 512 + 512)
                    k.op(act, lambda: nc.scalar.copy(out=ov.ap, in_=obk.ap), reads=[obk], writes=[ov])

            bias_load(0)
            stage_S(0)
            stage_S(1)
            stage_a(0)
            nxt = None
            steps = []
            for i in range(8 + 3):
                if 0 <= i - 2 < 8:
                    stage_T(i - 2)
                if i + 2 < 8:
                    stage_S(i + 2)
                if i + 1 < 8:
                    stage_a(i + 1)
                    if i + 1 == 7 and h + 1 < H:
                        ck_load(h + 1)
                        steps, res = early_proj([("q", "S", l, h + 1), ("k", "S", l, h + 1)], ih)
                        nxt = (res[0], res[1])
                if steps:
                    steps.pop(0)()
                if i < 8:
                    stage_b(i)
                if 0 <= i - 3 < 8:
                    stage_O(i - 3)
            while steps:
                steps.pop(0)()
            return nxt

        def fourier_group(sbk, l, g, ih, io, oc):
            isP = sbk == "P"
            fT = scr(0, 2048, BF16)
            AB = scr(2048, 4096, BF16, (8, 256))

            def ev_f(tb, bk):
                fv = sub(fT, fT.ap[:, tb * 512:(tb + 1) * 512])
                k.op(act, lambda: nc.scalar.copy(out=fv.ap, in_=bk.ap), reads=[bk], writes=[fv])
            proj_fm(wget(("f", sbk, l, g)), ih, ev_f)
            for t2 in range(4):
                bk = bank()
                for i in range(2):
                    t = t2 * 2 + i
                    k.group([lambda t=t, i=i, bk=bk: nc.tensor.matmul(
                        bk.ap[:, i * 256:(i + 1) * 256], lhsT=fT.ap[:, t * 128:(t + 1) * 128], rhs=csc[:, :],
                        start=True, stop=True)], reads=[fT] + CB, writes=[bk])
                av = sub(AB, AB.ap[:, t2 * 2:t2 * 2 + 2, :])
                k.op(dve if t2 % 2 == 0 else act,
                     (lambda bk=bk, av=av: nc.vector.tensor_copy(out=av.ap, in_=bk.ap.rearrange("p (a b) -> p a b", a=2)))
                     if t2 % 2 == 0 else
                     (lambda bk=bk, av=av: nc.scalar.copy(out=av.ap, in_=bk.ap.rearrange("p (a b) -> p a b", a=2))),
                     reads=[bk], writes=[av])
            if isP:
                wc = wget(("dftc", sbk, l, g, 0))
                ws_ = wget(("dfts", sbk, l, g, 0))
                for b in range(c.PB):
                    bk = bank()
                    fns = []
                    for m in range(2):
                        t = b * 2 + m
                        fns.append(lambda t=t, m=m, bk=bk: nc.tensor.matmul(
                            bk.ap[:, 0:256], lhsT=AB.ap[:, t, 0:128], rhs=wc.ap[:, m * 256:(m + 1) * 256],
                            start=(m == 0), stop=False))
                        fns.append(lambda t=t, m=m, bk=bk: nc.tensor.matmul(
                            bk.ap[:, 0:256], lhsT=AB.ap[:, t, 128:256], rhs=ws_.ap[:, m * 256:(m + 1) * 256],
                            start=False, stop=(m == 1)))
                    k.group(fns, reads=[AB, wc, ws_], writes=[bk])
                    ov = hch(io, oc, b * 256, b * 256 + 256)
                    k.op(act, lambda bk=bk, ov=ov: nc.scalar.copy(out=ov.ap, in_=bk.ap[:, 0:256]), reads=[bk], writes=[ov])
                wrefill()
            else:
                for nb in range(4):
                    wc = wget(("dftc", sbk, l, g, nb))
                    ws_ = wget(("dfts", sbk, l, g, nb))
                    bk = bank()
                    fns = []
                    for m in range(8):
                        fns.append(lambda m=m, bk=bk, wc=wc: nc.tensor.matmul(
                            bk.ap[:, 0:256], lhsT=AB.ap[:, m, 0:128], rhs=wc.ap[:, m * 256:(m + 1) * 256],
                            start=(m == 0), stop=False))
                        fns.append(lambda m=m, bk=bk, ws_=ws_: nc.tensor.matmul(
                            bk.ap[:, 0:256], lhsT=AB.ap[:, m, 128:256], rhs=ws_.ap[:, m * 256:(m + 1) * 256],
                            start=False, stop=(m == 7)))
                    k.group(fns, reads=[AB, wc, ws_], writes=[bk])
                    ov = hch(io, oc, nb * 256, nb * 256 + 256)
                    k.op(act, lambda bk=bk, ov=ov: nc.scalar.copy(out=ov.ap, in_=bk.ap[:, 0:256]), reads=[bk], writes=[ov])
                    wrefill()

        def sgu_group(sbk, l, g, ih, io, oc):
            uT = scr(0, 2048, BF16)
            vsT = scr(2048, 2048, BF16)
            vn = scr(4096, 2048, BF16, (8, 128))
            wsp = scr(6144, 256, BF16)
            gsg = scr(6656, 512, F32)
            bsp = scr(8192, 2048, F32)
            tmp = scr(10240, 2048, F32)
            r0 = (l * GC + g) * 128
            k.dma(pool, wsp.ap, wspT_d[r0:r0 + 128, :], wsp_s, writes=[wsp])
            k.dma(sp, bsp.ap, bsp_d[r0:r0 + 128, :], bsp_s, writes=[bsp])
            k.dma(sp, gsg.ap, gsgu_d[r0:r0 + 128, :], gsg_s, writes=[gsg])

            def ev_u(tb, bk):
                uv = sub(uT, uT.ap[:, tb * 512:(tb + 1) * 512])
                k.op(act, lambda: nc.scalar.activation(out=uv.ap, in_=bk.ap, func=AF.Gelu), reads=[bk], writes=[uv])
            proj_fm(wget(("u", sbk, l, g)), ih, ev_u)

            def ev_v(tb, bk):
                vv = sub(vsT, vsT.ap[:, tb * 512:(tb + 1) * 512])
                k.op(act, lambda: nc.scalar.activation(out=vv.ap, in_=bk.ap, func=AF.Gelu), reads=[bk], writes=[vv])
            proj_fm(wget(("vs", sbk, l, g)), ih, ev_v)
            m = small.ap
            tbk = bank()
            tb16 = tbk.ap.bitcast(BF16)
            k.group([lambda i=i: nc.tensor.transpose(tb16[:, i * 128:(i + 1) * 128], vsT.ap[:, i * 128:(i + 1) * 128],
                                                     identb[:, :]) for i in range(8)], reads=[vsT] + CB, writes=[tbk])
            k.op(dve, lambda: nc.vector.memset(m[:, 16:24], 0.0), writes=[small])
            for i in range(8):
                k.op(act, lambda i=i: nc.scalar.activation(out=tmp.ap.bitcast(BF16)[:, 0:128], in_=tb16[:, i * 128:(i + 1) * 128],
                                                           func=AF.Square, accum_out=m[:, 16 + i:17 + i]),
                     reads=[tbk, small], writes=[tmp, small])
            k.op(dve, lambda: nc.vector.tensor_scalar(out=m[:, 24:32], in0=m[:, 16:24], scalar1=1.0 / 128, scalar2=EPS,
                                                      op0=ALU.mult, op1=ALU.add), reads=[small], writes=[small])
            k.op(act, lambda: nc.scalar.activation(out=m[:, 24:32], in_=m[:, 24:32], func=AF.Sqrt), reads=[small], writes=[small])
            k.op(dve, lambda: nc.vector.reciprocal(out=m[:, 32:40], in_=m[:, 24:32]), reads=[small], writes=[small])
            for i in range(8):
                k.op(dve, lambda i=i: nc.vector.scalar_tensor_tensor(
                    out=vn.ap[:, i, :], in0=tb16[:, i * 128:(i + 1) * 128], scalar=m[:, 32 + i:33 + i], in1=gsg.ap,
                    op0=ALU.mult, op1=ALU.mult), reads=[tbk, small, gsg], writes=[vn])
            for tb in range(2):
                bk = bank()
                for i in range(4):
                    ch = tb * 4 + i
                    k.group([lambda i=i, ch=ch, bk=bk: nc.tensor.matmul(
                        bk.ap[:, i * 128:(i + 1) * 128], lhsT=vn.ap[:, ch, :], rhs=wsp.ap, start=True, stop=True)],
                        reads=[vn, wsp], writes=[bk])
                k.op(dve, lambda bk=bk: nc.vector.tensor_tensor(out=tmp.ap, in0=bk.ap, in1=bsp.ap, op=ALU.add),
                     reads=[bk, bsp], writes=[tmp])
                ov = hch(io, oc, tb * 512, tb * 512 + 512)
                k.op(dve, lambda ov=ov, tb=tb: nc.vector.tensor_tensor(
                    out=ov.ap, in0=tmp.ap, in1=uT.ap[:, tb * 512:(tb + 1) * 512], op=ALU.mult),
                    reads=[tmp, uT], writes=[ov])

        superblock("P")
        superblock("S")
        for ds in final_sems:
            sp.wait(Tok(ds.sem, ds.key, ds.cnt))
        for e in (pe, act, dve):
            sp.wait(Tok(e.sem, e.key, e.cnt))
    return nc


def build_two_pass(cfg):
    rec = []
    build(cfg, plan_tags=None, rec_tags=rec)
    return build(cfg, plan_tags=rec)


def _chunk_layout(w, KC):
    Kd, N = w.shape
    a = w.reshape(KC, 128, N // 128, 128).transpose(2, 1, 0, 3)
    return np.ascontiguousarray(a).reshape((N // 128) * 128, KC * 128)


def _bias_index(cfg):
    GW, rows, WR, WC_ = cfg.GW, cfg.NS // cfg.GW, 8, 16
    idx = np.full((8, 128, 640), 15 * 31, dtype=np.int64)
    for qt in range(8):
        rs0 = min(max(2 * qt - 4, 0), 8)
        rs1 = min(max(2 * qt - 3, 0), 8)
        kt0, kt1 = rs0 // 2, (rs1 + 7) // 2
        for q in range(128):
            r, cq = 2 * qt + q // GW, q % GW
            rstart = min(max(r - WR // 2, 0), rows - WR)
            cstart = min(max(cq - WC_ // 2, 0), GW - WC_)
            for kl in range((kt1 - kt0 + 1) * 128):
                key = kt0 * 128 + kl
                kr, kc_ = key // GW, key % GW
                if rstart <= kr < rstart + WR and cstart <= kc_ < cstart + WC_:
                    dr = kr - r + 7
                    dc = min(max(kc_ - cq, -15), 15) + 15
                    idx[qt, q, kl] = dr * 31 + dc
    return idx


_CACHE = {}


def _consts(cfg):
    if "c" in _CACHE:
        return _CACHE["c"]
    bf = ml_dtypes.bfloat16
    n = np.arange(128)
    ang = 2 * np.pi * np.outer(n, n) / 128.0
    csc = np.concatenate([np.cos(ang), np.sin(ang)], axis=1) / math.sqrt(128.0)

    def dft(N):
        m = np.arange(N)
        a = 2 * np.pi * (np.outer(m, m) % N) / N
        return np.cos(a) / math.sqrt(N), -np.sin(a) / math.sqrt(N)
    c256, s256 = dft(256)
    dftp = np.stack([c256, s256]).reshape(2, 2, 128, 256).transpose(0, 2, 1, 3).reshape(256, 512)
    c1k, s1k = dft(1024)
    d = np.stack([c1k, s1k]).reshape(2, 8, 128, 4, 256).transpose(3, 0, 2, 1, 4).reshape(4 * 2 * 128, 8 * 256)
    out = dict(csc=np.ascontiguousarray(csc).astype(bf), dftp=np.ascontiguousarray(dftp).astype(bf),
               dfts=np.ascontiguousarray(d).astype(bf), bidx=_bias_index(cfg))
    _CACHE["c"] = out
    return out


def prepare(cfg, inp):
    c = cfg
    KC, DEPTH, H, GC = c.KC, c.DEPTH, c.H, c.GC
    f = lambda a: np.ascontiguousarray(np.asarray(a, dtype=np.float32))
    cs = _consts(c)
    shared = {}
    shared["wada"] = np.concatenate([_chunk_layout(f(inp["w_ada"][l]), KC) for l in range(DEPTH)], axis=0)
    shared["bada"] = np.ascontiguousarray(f(inp["b_ada"]).reshape(DEPTH * 6 * KC, 128).T)
    g = np.stack([f(inp["g_pre_mix"]), f(inp["g_post_mix"]), f(inp["g_pre_mlp"]), f(inp["g_post_mlp"])])
    shared["gains"] = np.ascontiguousarray(g.reshape(4 * DEPTH * KC, 128).T)
    shared["win"] = np.concatenate([_chunk_layout(f(inp["w_in"][l]), KC) for l in range(DEPTH)], axis=0)
    wbr = [np.concatenate([f(inp["w_br_a"][l]), f(inp["w_br_f"][l]), f(inp["w_br_c"][l])], axis=0) for l in range(DEPTH)]
    shared["wbr"] = np.concatenate([_chunk_layout(w, KC) for w in wbr], axis=0)
    shared["wout"] = np.concatenate([_chunk_layout(f(inp["w_out"][l]), KC) for l in range(DEPTH)], axis=0)
    shared["w1"] = np.concatenate([_chunk_layout(f(inp["w_mlp1"][l]), KC) for l in range(DEPTH)], axis=0)
    w2 = f(inp["w_mlp2"]).reshape(DEPTH, c.NHC, 8, 128, KC // 2, 256).transpose(0, 1, 4, 3, 2, 5)
    shared["w2"] = np.ascontiguousarray(w2).reshape(DEPTH * c.NHC * (KC // 2) * 128, 8 * 256)
    shared["wspT"] = np.ascontiguousarray(f(inp["w_spatial"]).transpose(0, 1, 3, 2)).reshape(DEPTH * GC * 128, 128)
    shared["bsp"] = np.ascontiguousarray(np.broadcast_to(f(inp["b_spatial"])[:, :, None, None, :], (DEPTH, GC, 128, 4, 128))).reshape(-1, 512)
    shared["gsgu"] = np.ascontiguousarray(np.broadcast_to(f(inp["g_sgu"])[:, :, None, :], (DEPTH, GC, 128, 128))).reshape(-1, 128)
    rpb = f(inp["rpb"]).reshape(DEPTH, H, 15 * 31)
    rpb_ext = np.concatenate([rpb, np.full((DEPTH, H, 1), NEG, np.float32)], axis=2)
    shared["biast"] = np.ascontiguousarray(rpb_ext[:, :, cs["bidx"]]).reshape(DEPTH * H * 8 * 128, 640)
    shared["csc"], shared["dftp"], shared["dfts"] = cs["csc"], cs["dftp"], cs["dfts"]
    xp, xs = f(inp["x_prompt"]), f(inp["x_sample"])
    ck, cv, cc, cctx = f(inp["cache_k"]), f(inp["cache_v"]), f(inp["c"]), f(inp["c_ctx"])
    in_maps = []
    for core in range(c.NCORES):
        b = core // 4
        m = dict(shared)
        m["xp"] = xp[core * c.PB:(core + 1) * c.PB].reshape(c.TB, c.D)
        m["xs"] = xs[b]
        m["ckT"] = np.ascontiguousarray(ck[b].transpose(0, 2, 3, 1)).reshape(DEPTH * H * 128, c.PAST)
        m["cv"] = cv[b].reshape(DEPTH * c.PAST, H * 128)
        m["cvec"] = np.ascontiguousarray(np.stack([cctx.reshape(KC, 128), cc[b].reshape(KC, 128)], axis=2).transpose(1, 0, 2)).reshape(128, KC * 2)
        in_maps.append(m)
    return in_maps


def run(cfg, inp, trace=False):
    in_maps = prepare(cfg, inp)
    nc = build_two_pass(cfg)
    res = run_bass_kernel_spmd(nc, in_maps, core_ids=list(range(cfg.NCORES)), **({"trace": True} if trace else {}))
    c = cfg
    R = res.results
    yp = np.concatenate([R[i]["yp"].reshape(c.PB, c.SEQ, c.D) for i in range(c.NCORES)], axis=0)
    nb = (c.NCORES + 3) // 4
    ys = np.zeros((nb, c.NS, c.D), np.float32)
    for i in range(c.NCORES):
        q = i % 4
        ys[i // 4, q * 256:(q + 1) * 256] = R[i]["ys"][q * 256:(q + 1) * 256]
    nk = np.concatenate([R[i]["nk"].reshape(c.PB, c.DEPTH, c.SEQ, c.H, 128) for i in range(c.NCORES)], axis=0)
    nv = np.concatenate([R[i]["nv"].reshape(c.PB, c.DEPTH, c.SEQ, c.H, 128) for i in range(c.NCORES)], axis=0)
    return (yp.astype(np.float32), ys, nk.astype(np.float32), nv.astype(np.float32)), res


def kernel(**inputs):
    cfg = Cfg()
    outs, _ = run(cfg, inputs)
    return outs
```
